# Optimizing a Trainium2 kernel written in Bass

```python
import math
import jax, jax.numpy as jnp
from jax import lax
import numpy as np

D_MODEL = 1024
BATCH = 8
SEQ = 4096
DEPTH = 4

N_MIXERS = 3
Q_BLOCK = 128
EPS = 1e-6
D_FF = 4 * D_MODEL
MLA_HEADS = 16
MLA_NOPE = 64
MLA_ROPE = 32
MLA_V = 64
MLA_Q_RANK = 384
MLA_KV_RANK = 256
ROPE_THETA = 10000.0
CONV_WIDTH = 3
SB_HEADS = 16
SB_HEAD_DIM = D_MODEL // SB_HEADS

kernel_name = "hybrid_mla_shortconv_stickbreaking_trunk"


def rmsnorm(x, g):
    xf = x.astype(jnp.float32)
    y = xf * lax.rsqrt(jnp.mean(xf * xf, axis=-1, keepdims=True) + EPS)
    return (y * g.astype(jnp.float32)).astype(x.dtype)


def rope(x, ang):
    half = x.shape[-1] // 2
    x1, x2 = x[..., :half], x[..., half:]
    cos = jnp.cos(ang).astype(x.dtype)
    sin = jnp.sin(ang).astype(x.dtype)
    return jnp.concatenate([x1 * cos - x2 * sin, x2 * cos + x1 * sin], axis=-1)


def to_blocks(q):
    b, s, h, d = q.shape
    return q.reshape(b, s // Q_BLOCK, Q_BLOCK, h, d).transpose(1, 0, 3, 2, 4)


def from_blocks(o):
    nb, b, h, qb, d = o.shape
    return o.transpose(1, 0, 3, 2, 4).reshape(b, nb * qb, h * d)


def causal_softmax_attention(q, k, v, scale):
    s_len = q.shape[1]
    kt = k.transpose(0, 2, 1, 3)
    vt = v.transpose(0, 2, 1, 3)
    kpos = jnp.arange(s_len)

    def body(args):
        qb, bi = args
        qpos = bi * Q_BLOCK + jnp.arange(Q_BLOCK)
        sc = jnp.einsum('bhqd,bhkd->bhqk', qb, kt).astype(jnp.float32) * scale
        mask = kpos[None, :] <= qpos[:, None]
        p = jax.nn.softmax(jnp.where(mask, sc, -jnp.inf), axis=-1)
        return jnp.einsum('bhqk,bhkd->bhqd', p.astype(vt.dtype), vt)

    out = lax.map(body, (to_blocks(q), jnp.arange(s_len // Q_BLOCK)))
    return from_blocks(out)


def stick_breaking_attention(q, k, v, scale):
    s_len = q.shape[1]
    kt = k.transpose(0, 2, 1, 3)
    vt = v.transpose(0, 2, 1, 3)
    kpos = jnp.arange(s_len)

    def body(args):
        qb, bi = args
        qpos = bi * Q_BLOCK + jnp.arange(Q_BLOCK)
        z = jnp.einsum('bhqd,bhkd->bhqk', qb, kt).astype(jnp.float32) * scale
        strict = kpos[None, :] < qpos[:, None]
        log_1m_beta = jnp.where(strict, jax.nn.log_sigmoid(-z), 0.0)
        key_axis = log_1m_beta.ndim - 1
        suffix = lax.cumsum(log_1m_beta, axis=key_axis, reverse=True) - log_1m_beta
        log_a = jax.nn.log_sigmoid(z) + suffix
        a = jnp.where(strict, jnp.exp(log_a), 0.0)
        return jnp.einsum('bhqk,bhkd->bhqd', a.astype(vt.dtype), vt)

    out = lax.map(body, (to_blocks(q), jnp.arange(s_len // Q_BLOCK)))
    return from_blocks(out)


def mla_mixer(h, positions, w_dq, norm_q, w_uq, w_dkv, norm_kv, w_uk, w_uv, w_o):
    b, s, _ = h.shape
    inv_freq = ROPE_THETA ** (-jnp.arange(0, MLA_ROPE, 2, dtype=jnp.float32) / MLA_ROPE)
    ang = positions.astype(jnp.float32)[..., None] * inv_freq
    cq = rmsnorm(h @ w_dq, norm_q)
    q = (cq @ w_uq).reshape(b, s, MLA_HEADS, MLA_NOPE + MLA_ROPE)
    q_nope, q_rope = q[..., :MLA_NOPE], q[..., MLA_NOPE:]
    q_rope = rope(q_rope, ang[:, :, None, :])
    ckr = h @ w_dkv
    ckv = rmsnorm(ckr[..., :MLA_KV_RANK], norm_kv)
    k_rope = rope(ckr[..., MLA_KV_RANK:], ang)
    k_nope = (ckv @ w_uk).reshape(b, s, MLA_HEADS, MLA_NOPE)
    v = (ckv @ w_uv).reshape(b, s, MLA_HEADS, MLA_V)
    qf = jnp.concatenate([q_nope, q_rope], axis=-1)
    kf = jnp.concatenate([k_nope, jnp.broadcast_to(k_rope[:, :, None, :], (b, s, MLA_HEADS, MLA_ROPE))], axis=-1)
    o = causal_softmax_attention(qf, kf, v, 1.0 / math.sqrt(MLA_NOPE + MLA_ROPE))
    return o @ w_o


def conv_mixer(h, w_in, conv_w, conv_b, w_out):
    bcu = h @ w_in
    gb, gc, u = bcu[..., :D_MODEL], bcu[..., D_MODEL:2 * D_MODEL], bcu[..., 2 * D_MODEL:]
    u = gc * u
    up = jnp.pad(u, ((0, 0), (CONV_WIDTH - 1, 0), (0, 0)))
    s = u.shape[1]
    y = conv_w[0] * up[:, 0:s] + conv_w[1] * up[:, 1:s + 1] + conv_w[2] * up[:, 2:s + 2] + conv_b
    return (gb * y) @ w_out


def sb_mixer(h, w_qkv, w_o):
    b, s, _ = h.shape
    qkv = (h @ w_qkv).reshape(b, s, 3, SB_HEADS, SB_HEAD_DIM)
    q, k, v = qkv[:, :, 0], qkv[:, :, 1], qkv[:, :, 2]
    o = stick_breaking_attention(q, k, v, 1.0 / math.sqrt(SB_HEAD_DIM))
    return o @ w_o


def sqrelu_mlp(h, w_up, w_down):
    return jnp.square(jax.nn.relu(h @ w_up)) @ w_down


def setup_inputs(seed: int = 0) -> dict:
    key = jax.random.key(seed)
    keys = iter(jax.random.split(key, 64))

    def dense(shape, extra=1.0):
        return jax.random.normal(next(keys), shape, jnp.float32) * (shape[0] ** -0.5) * extra

    def gain(n):
        return 1.0 + 0.01 * jax.random.normal(next(keys), (n,), jnp.float32)

    p = {}
    p["x"] = jax.random.normal(next(keys), (BATCH, SEQ, D_MODEL), jnp.float32)
    p["positions"] = jnp.broadcast_to(jnp.arange(SEQ, dtype=jnp.int32)[None, :], (BATCH, SEQ))
    out_scale = (2.0 * DEPTH) ** -0.5
    for i in range(DEPTH):
        kind = i % N_MIXERS
        pre = f"l{i}_"
        p[pre + "norm_mix"] = gain(D_MODEL)
        if kind == 0:
            p[pre + "w_dq"] = dense((D_MODEL, MLA_Q_RANK))
            p[pre + "norm_q"] = gain(MLA_Q_RANK)
            p[pre + "w_uq"] = dense((MLA_Q_RANK, MLA_HEADS * (MLA_NOPE + MLA_ROPE)))
            p[pre + "w_dkv"] = dense((D_MODEL, MLA_KV_RANK + MLA_ROPE))
            p[pre + "norm_kv"] = gain(MLA_KV_RANK)
            p[pre + "w_uk"] = dense((MLA_KV_RANK, MLA_HEADS * MLA_NOPE))
            p[pre + "w_uv"] = dense((MLA_KV_RANK, MLA_HEADS * MLA_V))
            p[pre + "w_o"] = dense((MLA_HEADS * MLA_V, D_MODEL), out_scale)
        elif kind == 1:
            p[pre + "w_in"] = dense((D_MODEL, 3 * D_MODEL))
            p[pre + "conv_w"] = jax.random.normal(next(keys), (CONV_WIDTH, D_MODEL), jnp.float32) * CONV_WIDTH ** -0.5
            p[pre + "conv_b"] = 0.01 * jax.random.normal(next(keys), (D_MODEL,), jnp.float32)
            p[pre + "w_out"] = dense((D_MODEL, D_MODEL), out_scale)
        else:
            p[pre + "w_qkv"] = dense((D_MODEL, 3 * SB_HEADS * SB_HEAD_DIM))
            p[pre + "w_o"] = dense((SB_HEADS * SB_HEAD_DIM, D_MODEL), out_scale)
        p[pre + "norm_mlp"] = gain(D_MODEL)
        p[pre + "w_up"] = dense((D_MODEL, D_FF))
        p[pre + "w_down"] = dense((D_FF, D_MODEL), out_scale)
    p["final_norm"] = gain(D_MODEL)
    return p


def reference(x, positions,
              l0_norm_mix, l0_w_dq, l0_norm_q, l0_w_uq, l0_w_dkv, l0_norm_kv, l0_w_uk, l0_w_uv, l0_w_o,
              l0_norm_mlp, l0_w_up, l0_w_down,
              l1_norm_mix, l1_w_in, l1_conv_w, l1_conv_b, l1_w_out,
              l1_norm_mlp, l1_w_up, l1_w_down,
              l2_norm_mix, l2_w_qkv, l2_w_o,
              l2_norm_mlp, l2_w_up, l2_w_down,
              l3_norm_mix, l3_w_dq, l3_norm_q, l3_w_uq, l3_w_dkv, l3_norm_kv, l3_w_uk, l3_w_uv, l3_w_o,
              l3_norm_mlp, l3_w_up, l3_w_down,
              final_norm):
    mixers = [
        lambda h: mla_mixer(h, positions, l0_w_dq, l0_norm_q, l0_w_uq, l0_w_dkv, l0_norm_kv, l0_w_uk, l0_w_uv, l0_w_o),
        lambda h: conv_mixer(h, l1_w_in, l1_conv_w, l1_conv_b, l1_w_out),
        lambda h: sb_mixer(h, l2_w_qkv, l2_w_o),
        lambda h: mla_mixer(h, positions, l3_w_dq, l3_norm_q, l3_w_uq, l3_w_dkv, l3_norm_kv, l3_w_uk, l3_w_uv, l3_w_o),
    ]
    mix_norms = [l0_norm_mix, l1_norm_mix, l2_norm_mix, l3_norm_mix]
    mlps = [(l0_norm_mlp, l0_w_up, l0_w_down), (l1_norm_mlp, l1_w_up, l1_w_down),
            (l2_norm_mlp, l2_w_up, l2_w_down), (l3_norm_mlp, l3_w_up, l3_w_down)]
    for i in range(DEPTH):
        x = x + mixers[i](rmsnorm(x, mix_norms[i]))
        g, w_up, w_down = mlps[i]
        x = x + sqrelu_mlp(rmsnorm(x, g), w_up, w_down)
    return rmsnorm(x, final_norm)
```

```python
import math
from contextlib import ExitStack
import numpy as np
import concourse.bass as bass
import concourse.mybir as mybir
from concourse.bass_utils import run_bass_kernel_spmd

F32 = mybir.dt.float32
BF16 = mybir.dt.bfloat16
I32 = mybir.dt.int32
AF = mybir.ActivationFunctionType
ALU = mybir.AluOpType

T = 4096
D = 1024
BT = 512
NBLK = T // BT
NCH = D // 128
DFF = 4096
EPS = 1e-6
N_CORES = 8

MLA_LAYERS = (0, 3)
WNAMES = {
    0: ["w_dq", "norm_q", "w_uq", "w_dkv", "norm_kv", "w_uk", "w_uv", "w_o"],
    1: ["w_in", "conv_w", "conv_b", "w_out"],
    2: ["w_qkv", "w_o"],
    3: ["w_dq", "norm_q", "w_uq", "w_dkv", "norm_kv", "w_uk", "w_uv", "w_o"],
}
WSHAPES = {
    "w_dq": [1024, 384], "norm_q": [384], "w_uq": [384, 1536], "w_dkv": [1024, 288], "norm_kv": [256],
    "w_uk": [256, 1024], "w_uv": [256, 1024], "w_o": [1024, 1024], "w_in": [1024, 3072],
    "conv_w": [3, 1024], "conv_b": [1024], "w_out": [1024, 1024], "w_qkv": [1024, 3072],
    "norm_mix": [1024], "norm_mlp": [1024], "w_up": [1024, 4096], "w_down": [4096, 1024],
}


class Buf:
    __slots__ = ("name", "w", "r")

    def __init__(self, name):
        self.name = name
        self.w = None
        self.r = {}


class Chan:
    __slots__ = ("sem", "count")

    def __init__(self):
        self.sem = None
        self.count = 0


class Stream:
    def __init__(self, name):
        self.name = name
        self.items = []
        self.nops = 0
        self.seen = {}
        self.sem = None
        self.referenced = set()
        self.rank = {}


class FW:
    def __init__(self):
        self.streams = {n: Stream(n) for n in ("pe", "act", "dve", "pool", "sp")}
        self.chans = []

    def _need(self, st, tok):
        if tok is None:
            return
        if tok[0] == 'E':
            src = tok[1]
            if src is st and st.name in ("pe", "sp"):
                return
            key = src.name
        else:
            key = id(tok[1])
        val = tok[2]
        if st.seen.get(key, -1) >= val:
            return
        st.seen[key] = val
        st.items.append(('wait', tok))
        if tok[0] == 'E':
            src.referenced.add(val)

    def _deps(self, st, reads, writes):
        for b in reads:
            self._need(st, b.w)
        for b in writes:
            self._need(st, b.w)
            for t in b.r.values():
                self._need(st, t)

    def _commit(self, tok, key, reads, writes):
        for b in reads:
            b.r[key] = tok
        for b in writes:
            b.w = tok
            b.r = {}

    def op(self, eng, fn, reads=(), writes=(), chan=None):
        st = self.streams[eng]
        self._deps(st, reads, writes)
        if chan is not None:
            chan.count += 16
            tok = ('C', chan, chan.count)
            st.items.append(('dma', fn, chan))
            key = id(chan)
        else:
            st.nops += 1
            tok = ('E', st, st.nops)
            st.items.append(('op', fn, st.nops))
            key = st.name
        self._commit(tok, key, reads, writes)
        return tok

    def pe_group(self, fns, reads=(), writes=()):
        st = self.streams["pe"]
        self._deps(st, reads, writes)
        for fn in fns[:-1]:
            st.items.append(('op', fn, None))
        st.nops += 1
        tok = ('E', st, st.nops)
        st.items.append(('op', fns[-1], st.nops))
        self._commit(tok, "pe", reads, writes)
        return tok

    def new_chan(self):
        c = Chan()
        self.chans.append(c)
        return c

    def wait_all(self, eng, bufs):
        st = self.streams[eng]
        for b in bufs:
            self._need(st, b.w)
            for t in b.r.values():
                self._need(st, t)

    def barrier(self):
        toks = []
        for st in self.streams.values():
            if st.nops > 0:
                toks.append(('E', st, st.nops))
        for c in self.chans:
            if c.count > 0:
                toks.append(('C', c, c.count))
        for st in self.streams.values():
            for t in toks:
                self._need(st, t)

    def n_sems(self):
        return len(self.streams) + len(self.chans)

    def replay(self, block, sems):
        it = iter(sems)
        for st in self.streams.values():
            st.sem = next(it)
        for c in self.chans:
            c.sem = next(it)
        for st in self.streams.values():
            st.rank = {idx: i + 1 for i, idx in enumerate(sorted(st.referenced))}

        def run(st, h):
            for item in st.items:
                if item[0] == 'wait':
                    tok = item[1]
                    if tok[0] == 'E':
                        h.wait_ge(tok[1].sem, tok[1].rank[tok[2]])
                    else:
                        h.wait_ge(tok[1].sem, tok[2])
                elif item[0] == 'dma':
                    item[1](h).then_inc(item[2].sem, 16)
                else:
                    ins = item[1](h)
                    if item[2] is not None and item[2] in st.rank:
                        ins.then_inc(st.sem, 1)

        S = self.streams
        block.tensor(lambda h: run(S["pe"], h))
        block.scalar(lambda h: run(S["act"], h))
        block.vector(lambda h: run(S["dve"], h))
        block.gpsimd(lambda h: run(S["pool"], h))
        block.sync(lambda h: run(S["sp"], h))


class Arena:
    def __init__(self, ap):
        self.ap = ap
        self.n = ap.shape[1]
        self.off = 0
        self.peak = 0
        self.top = self.n

    def alloc(self, nelem, dt=BF16):
        nb = nelem * (4 if dt in (F32, I32) else 2)
        n16 = (nb + 31) // 32 * 16
        s = self.off
        self.off += n16
        self.peak = max(self.peak, self.off)
        assert self.off <= self.top, f"arena overflow {self.off}>{self.top}"
        v = self.ap[:, s:s + nb // 2]
        if dt != BF16:
            v = v.bitcast(dt)
        return v

    def alloc_top(self, nelem):
        self.top -= (nelem + 15) // 16 * 16
        assert self.off <= self.top, f"arena overflow(top) {self.off}>{self.top}"
        return self.ap[:, self.top:self.top + nelem]

    def release_top(self):
        self.top = self.n

    def mark(self):
        return self.off

    def release(self, m):
        self.off = m


class Prog:
    def __init__(self, cfg=None):
        self.cfg = cfg or {}
        self.nc = bass.Bass("TRN2", target_bir_lowering=False)
        self.fw = FW()

    def next_ps(self):
        i = self.ps_i
        self.ps_i = (self.ps_i + 1) % len(self.ps_gen)
        return self.ps_gen[i]

    def next_acc(self):
        i = self.acc_i
        self.acc_i = (self.acc_i + 1) % len(self.ps_acc)
        return self.ps_acc[i]

    def mm_group(self, out_ap, pairs, reads, ps_buf):
        n = len(pairs)
        fns = []
        for i, (l, r) in enumerate(pairs):
            fns.append(lambda e, l=l, r=r, i=i: e.matmul(out_ap, lhsT=l, rhs=r, start=(i == 0), stop=(i == n - 1)))
        return self.fw.pe_group(fns, reads=reads, writes=[ps_buf])

    def load_weight(self, dst3, src2, buf, chan, ncols_piece=None):
        self.fw.op("pool", lambda e: e.dma_start(out=dst3, in_=src2.rearrange("(c p) n -> p c n", p=128)),
                   writes=[buf], chan=chan)

    def rms_feature_major(self, xap, nch, width, gcol, out_fn, sq, b_sq, rstd, b_rstd, b_x, b_out, n_feat):
        fw = self.fw
        fw.op("act", lambda e: e.activation(sq[:, 0:nch, 0:width], xap, AF.Square), reads=[b_x], writes=[b_sq])
        ps, b_ps = self.next_ps()
        self.mm_group(ps[:, 0:width], [(self.ones[:], sq[:, c, 0:width]) for c in range(nch)], [b_sq, self.b_const], b_ps)
        fw.op("act", lambda e: e.activation(rstd[:, 0:width], ps[:, 0:width], AF.Sqrt, scale=1.0 / n_feat, bias=self.eps_col[:, 0:1]),
              reads=[], writes=[b_rstd, b_ps])
        fw.op("dve", lambda e: e.reciprocal(rstd[:, 0:width], rstd[:, 0:width]), reads=[], writes=[b_rstd])
        for c in range(nch):
            dst = out_fn(c)
            gsc = gcol(c)
            fw.op("dve", lambda e, c=c, dst=dst, gsc=gsc: e.scalar_tensor_tensor(out=dst, in0=xap[:, c, :], scalar=gsc, in1=rstd[:, 0:width],
                                                               op0=ALU.mult, op1=ALU.mult),
                  reads=[b_x, b_rstd, self.b_const], writes=[b_out])

    def build(self):
        nc, fw = self.nc, self.fw
        cfg = self.cfg
        self.es = es = ExitStack()
        with es:
            self.x_in = nc.dram_tensor("x", [T, D], F32, kind="ExternalInput").ap()
            self.pos_in = nc.dram_tensor("positions", [1, T], I32, kind="ExternalInput").ap()
            self.W = {}
            for l in range(4):
                for nm in ["norm_mix"] + WNAMES[l] + ["norm_mlp", "w_up", "w_down"]:
                    full = f"l{l}_{nm}"
                    self.W[full] = nc.dram_tensor(full, WSHAPES[nm], F32, kind="ExternalInput").ap()
            self.W["final_norm"] = nc.dram_tensor("final_norm", [D], F32, kind="ExternalInput").ap()
            self.out = nc.dram_tensor("out", [T, D], F32, kind="ExternalOutput").ap()
            self.xT_d = nc.dram_tensor("xT_scratch", [NCH, 128, T], F32).ap()
            self.b_xd = [Buf(f"xd{b}") for b in range(NBLK)]
            self.c_xd = [fw.new_chan() for _ in range(NBLK)]
            self.b_outd = Buf("outd")
            self.c_outd = fw.new_chan()
            self.dbg = []
            if cfg.get("dbg"):
                for k in range(7):
                    self.dbg.append(nc.dram_tensor(f"dbg{k}", [NCH, 128, T], F32, kind="ExternalOutput").ap())
            self.dbg_i = 0

            self.ps_all = []
            for i in range(8):
                t = es.enter_context(nc.psum_tensor(f"ps{i}", [128, 512], F32))
                self.ps_all.append((t, Buf(f"ps{i}")))
            self.ps_acc = self.ps_all[0:2]
            self.ps_gen = self.ps_all[2:8]
            self.ps_i = 0
            self.acc_i = 0

            def sb(name, shape, dt):
                return es.enter_context(nc.sbuf_tensor(name, shape, dt))
            self.ones = sb("ones", [128, 128], BF16)
            self.ident = sb("ident", [128, 128], F32)
            self.triI = sb("triI", [128, 128], BF16)
            self.triS = sb("triS", [128, 128], BF16)
            self.Uinc = sb("Uinc", [128, 128], BF16)
            self.neg8 = sb("neg8", [128, 128], BF16)
            self.gains = sb("gains", [128, NCH, 32], F32)
            self.eps_col = sb("eps_col", [128, 1], F32)
            self.negpi = sb("negpi", [128, 1], F32)
            self.b_const = Buf("const")
            self.b_TAB = Buf("TAB")
            arena_elems = (int(nc.sbuf_bytes_remaining) - 1024) // 64 * 32
            print("arena KiB", arena_elems * 2 / 1024)
            self.arena = Arena(sb("arena", [128, arena_elems], BF16))
            self.b_arena_guard = Buf("arena")

            self.setup_consts()
            fw.barrier()
            self.prologue()
            fw.barrier()
            layers = cfg.get("layers", [0, 1, 2, 3])
            for l in layers:
                if cfg.get("mix", True):
                    if l in MLA_LAYERS:
                        self.mla_phase(l)
                    elif l == 1:
                        self.conv_phase(l)
                    else:
                        self.sb_phase(l)
                    fw.barrier()
                    self.dump_dbg()
                if cfg.get("ffn", True):
                    self.ffn_phase(l, final=(l == layers[-1]) and cfg.get("final", True))
                    fw.barrier()
                    if not getattr(self, "_emitted_final", False):
                        self.dump_dbg()
            if not getattr(self, "_emitted_final", False):
                self.dump_xT()
            fw.wait_all("sp", [self.b_outd])
            sems = [es.enter_context(nc.semaphore(f"s{i}")) for i in range(fw.n_sems())]
            block = es.enter_context(nc.Block())
            fw.replay(block, sems)
        return nc

    def setup_consts(self):
        nc, fw, A = self.nc, self.fw, self.arena
        m = A.mark()
        bc = self.b_const
        iot = A.alloc(128, I32)
        b_t = Buf("ctmp")
        fw.op("pool", lambda e: e.iota(iot, pattern=[[1, 128]], base=0, channel_multiplier=-1), writes=[b_t])
        fw.op("dve", lambda e: e.tensor_single_scalar(self.ident[:], iot, 0, ALU.is_equal), reads=[b_t], writes=[bc])
        fw.op("dve", lambda e: e.tensor_single_scalar(self.triI[:], iot, 0, ALU.is_ge), reads=[b_t], writes=[bc])
        fw.op("dve", lambda e: e.tensor_single_scalar(self.triS[:], iot, 0, ALU.is_gt), reads=[b_t], writes=[bc])
        fw.op("dve", lambda e: e.tensor_scalar(self.Uinc[:], iot, 0, -8.0, op0=ALU.is_le, op1=ALU.mult), reads=[b_t], writes=[bc])
        fw.op("dve", lambda e: e.memset(self.ones[:], 1.0), writes=[bc])
        fw.op("dve", lambda e: e.memset(self.neg8[:], -8.0), writes=[bc])
        fw.op("dve", lambda e: e.memset(self.eps_col[:], EPS), writes=[bc])
        fw.op("dve", lambda e: e.memset(self.negpi[:], -math.pi), writes=[bc])
        gv = A.alloc(1024, F32)
        b_gv = Buf("gv")
        c_gv = fw.new_chan()
        fw.op("dve", lambda e: e.memset(gv[0:32, :], 0.0), writes=[b_gv])
        rows = []
        for l in range(4):
            rows.append((l, f"l{l}_norm_mix", 1024))
            rows.append((4 + l, f"l{l}_norm_mlp", 1024))
        rows.append((8, "final_norm", 1024))
        rows.append((12, "l1_conv_b", 1024))
        rows.append((13, "l0_norm_q", 384))
        rows.append((14, "l3_norm_q", 384))
        rows.append((15, "l0_norm_kv", 256))
        rows.append((16, "l3_norm_kv", 256))
        for r, nm, n in rows:
            fw.op("sp", lambda e, r=r, nm=nm, n=n: e.dma_start(out=gv[r:r + 1, 0:n], in_=self.W[nm].rearrange("(o n) -> o n", o=1)),
                  writes=[b_gv], chan=c_gv)
        fw.op("sp", lambda e: e.dma_start(out=gv[9:12, :], in_=self.W["l1_conv_w"]), writes=[b_gv], chan=c_gv)
        ps, b_ps = self.next_ps()
        fns = [lambda e, c=c: e.transpose(ps[:, c * 32:(c + 1) * 32], gv[0:32, c * 128:(c + 1) * 128], self.ident[0:32, 0:32]) for c in range(NCH)]
        fw.pe_group(fns, reads=[b_gv, bc], writes=[b_ps])
        fw.op("dve", lambda e: e.tensor_copy(self.gains[:].rearrange("p c r -> p (c r)"), ps[:, 0:256]), reads=[], writes=[bc, b_ps])
        self._const_mark = m

    def build_tables(self):
        fw, A = self.fw, self.arena
        bc = self.b_const
        b_t = Buf("ttmp")
        TAB = self.TAB = A.alloc(T, F32)
        m = A.mark()
        posi = A.alloc(T, I32)
        ang = A.alloc(T, F32)
        tq = A.alloc(T, F32)
        idx = A.alloc(1, I32)
        cf = A.alloc(1, F32)
        s1 = A.alloc(1, F32)
        invf = A.alloc(1, F32)
        off = A.alloc(1, F32)
        b_pos = Buf("pos")
        if not hasattr(self, "c_pos"):
            self.c_pos = fw.new_chan()
        c_pos = self.c_pos
        R = slice(64, 128)
        TWO_PI = 2.0 * math.pi
        fw.op("sp", lambda e: e.dma_start(out=posi[R, :], in_=self.pos_in[0, :].partition_broadcast(64)), writes=[b_pos], chan=c_pos)
        fw.op("pool", lambda e: e.iota(idx[R, :], pattern=[[0, 1]], base=0, channel_multiplier=1), writes=[b_t])
        fw.op("dve", lambda e: e.tensor_copy(cf[R, :], idx[R, :]), reads=[b_t], writes=[b_t])
        fw.op("dve", lambda e: e.tensor_copy(invf[R, :], cf[R, :]), writes=[b_t])
        for thr in (16.0, 32.0, 48.0):
            fw.op("dve", lambda e, thr=thr: e.tensor_scalar(s1[R, :], cf[R, :], thr, 16.0, op0=ALU.is_ge, op1=ALU.mult), writes=[b_t])
            fw.op("dve", lambda e: e.tensor_tensor(out=invf[R, :], in0=invf[R, :], in1=s1[R, :], op=ALU.subtract), writes=[b_t])
        fw.op("act", lambda e: e.activation(invf[R, :], invf[R, :], AF.Exp, scale=-math.log(10000.0) / 16.0), writes=[b_t])
        fw.op("dve", lambda e: e.memset(off[64:96, :], 0.5 * math.pi), writes=[b_t])
        fw.op("dve", lambda e: e.memset(off[96:128, :], 0.0), writes=[b_t])
        fw.op("dve", lambda e: e.memset(off[96:112, :], math.pi), writes=[b_t])
        fw.op("dve", lambda e: e.tensor_copy(ang[R, :], posi[R, :]), reads=[b_pos], writes=[b_t])
        fw.op("dve", lambda e: e.tensor_scalar(ang[R, :], ang[R, :], invf[R, 0:1], off[R, 0:1], op0=ALU.mult, op1=ALU.add), writes=[b_t])
        fw.op("dve", lambda e: e.tensor_scalar(tq[R, :], ang[R, :], 1.0 / TWO_PI, None, op0=ALU.mult), writes=[b_t])
        fw.op("dve", lambda e: e.tensor_copy(posi[R, :], tq[R, :]), writes=[b_t])
        fw.op("dve", lambda e: e.tensor_copy(tq[R, :], posi[R, :]), writes=[b_t])
        fw.op("dve", lambda e: e.scalar_tensor_tensor(out=ang[R, :], in0=tq[R, :], scalar=-TWO_PI, in1=ang[R, :], op0=ALU.mult, op1=ALU.add), writes=[b_t])
        fw.op("dve", lambda e: e.tensor_scalar(tq[R, :], ang[R, :], math.pi, TWO_PI, op0=ALU.is_ge, op1=ALU.mult), writes=[b_t])
        fw.op("dve", lambda e: e.tensor_tensor(out=ang[R, :], in0=ang[R, :], in1=tq[R, :], op=ALU.subtract), writes=[b_t])
        fw.op("dve", lambda e: e.tensor_scalar(tq[R, :], ang[R, :], -math.pi, TWO_PI, op0=ALU.is_lt, op1=ALU.mult), writes=[b_t])
        fw.op("dve", lambda e: e.tensor_tensor(out=ang[R, :], in0=ang[R, :], in1=tq[R, :], op=ALU.add), writes=[b_t])
        fw.op("act", lambda e: e.activation(TAB[R, :], ang[R, :], AF.Sin), reads=[b_t, bc], writes=[self.b_TAB])
        fw.barrier()
        A.release(m)

    def dump_dbg(self):
        if not self.dbg:
            return
        d = self.dbg[self.dbg_i]
        self.dbg_i += 1
        self.fw.op("sp", lambda e: e.dma_start(out=d[:, :, :], in_=self.xT_d[:, :, :]), reads=self.b_xd, writes=[self.b_outd], chan=self.c_outd)
        self.fw.barrier()

    def sdump(self, name, ap, buf):
        if not self.cfg.get("sdump"):
            return
        shp = list(ap.shape)
        d = self.nc.dram_tensor("sd_" + name, shp, ap.dtype, kind="ExternalOutput").ap()
        self.fw.barrier()
        self.fw.op("sp", lambda e: e.dma_start(out=d, in_=ap), reads=[buf], writes=[self.b_outd], chan=self.c_outd)
        self.fw.barrier()

    def gcol(self, row):
        return lambda c: self.gains[:, c, row:row + 1]

    def prologue(self):
        fw, A = self.fw, self.arena
        A.release(self._const_mark)
        m = A.mark()
        xin = [A.alloc(4 * D, F32).rearrange("p (t d) -> p t d", t=4) for _ in range(2)]
        xb = [A.alloc(NCH * BT, F32).rearrange("p (c t) -> p c t", c=NCH) for _ in range(2)]
        b_xin = [Buf("xin0"), Buf("xin1")]
        c_xin = [fw.new_chan(), fw.new_chan()]
        b_xb = [Buf("pxb0"), Buf("pxb1")]
        for blk in range(NBLK):
            s = blk % 2
            fw.op("sp", lambda e, s=s, blk=blk: e.dma_start(out=xin[s], in_=self.x_in[blk * BT:(blk + 1) * BT, :].rearrange("(t p) d -> p t d", p=128)),
                  writes=[b_xin[s]], chan=c_xin[s])
            for c in range(NCH):
                ps, b_ps = self.next_ps()
                fns = [lambda e, tt=tt, c=c, s=s, ps=ps: e.transpose(ps[:, tt * 128:(tt + 1) * 128], xin[s][:, tt, c * 128:(c + 1) * 128], self.ident[:])
                       for tt in range(4)]
                fw.pe_group(fns, reads=[b_xin[s], self.b_const], writes=[b_ps])
                eng = "dve" if c % 2 == 0 else "act"
                if eng == "dve":
                    fw.op("dve", lambda e, c=c, s=s, ps=ps: e.tensor_copy(xb[s][:, c, :], ps[:]), writes=[b_xb[s], b_ps])
                else:
                    fw.op("act", lambda e, c=c, s=s, ps=ps: e.copy(xb[s][:, c, :], ps[:]), writes=[b_xb[s], b_ps])
            self.store_xblk(xb[s], b_xb[s], blk)
        A.release(m)

    def xd_view(self, blk):
        return self.xT_d[:, :, blk * BT:(blk + 1) * BT].rearrange("c p t -> p c t")

    def store_xblk(self, xb, b_xb, blk):
        self.fw.op("sp", lambda e: e.dma_start(out=self.xd_view(blk), in_=xb), reads=[b_xb], writes=[self.b_xd[blk]], chan=self.c_xd[blk])

    def load_xblk(self, xb, b_xb, c_xb, blk):
        self.fw.op("sp", lambda e: e.dma_start(out=xb, in_=self.xd_view(blk)), reads=[self.b_xd[blk]], writes=[b_xb], chan=c_xb)

    def dump_xT(self):
        fw, A = self.fw, self.arena
        m = A.mark()
        xb = A.alloc(NCH * BT, F32).rearrange("p (c t) -> p c t", c=NCH)
        b_xb, c_xb = Buf("dxb"), fw.new_chan()
        for blk in range(NBLK):
            self.load_xblk(xb, b_xb, c_xb, blk)
            self.emit_output(xb, b_xb, blk)
        A.release(m)

    def emit_output(self, yb, b_yb, blk, ost=None, b_ost=None):
        fw, A = self.fw, self.arena
        m = A.mark()
        if ost is None:
            ost = A.alloc(4 * D, F32).rearrange("p (t d) -> p t d", t=4)
            b_ost = self.b_ost if hasattr(self, "b_ost") else Buf("ost")
            self.b_ost = b_ost
        for tt in range(4):
            for half in range(2):
                ps, b_ps = self.next_ps()
                fns = [lambda e, tt=tt, c=c, ps=ps: e.transpose(ps[:, (c % 4) * 128:(c % 4 + 1) * 128], yb[:, c, tt * 128:(tt + 1) * 128], self.ident[:])
                       for c in range(half * 4, half * 4 + 4)]
                fw.pe_group(fns, reads=[b_yb, self.b_const], writes=[b_ps])
                if (tt + half) % 2 == 0:
                    fw.op("dve", lambda e, tt=tt, half=half, ps=ps: e.tensor_copy(ost[:, tt, half * 512:(half + 1) * 512], ps[:]), writes=[b_ost, b_ps])
                else:
                    fw.op("act", lambda e, tt=tt, half=half, ps=ps: e.copy(ost[:, tt, half * 512:(half + 1) * 512], ps[:]), writes=[b_ost, b_ps])
        fw.op("sp", lambda e: e.dma_start(out=self.out[blk * BT:(blk + 1) * BT, :].rearrange("(t p) d -> p t d", p=128), in_=ost),
              reads=[b_ost], writes=[self.b_outd], chan=self.c_outd)
        A.release(m)

    def ffn_phase(self, l, final=False):
        fw, A = self.fw, self.arena
        if final:
            self._emitted_final = True
        m = A.mark()
        wup = A.alloc(NCH * DFF).rearrange("p (c n) -> p c n", c=NCH)
        wdn = A.alloc(32 * D).rearrange("p (f n) -> p f n", f=32)
        NP = 4
        b_wup = [Buf(f"wup{i}") for i in range(NP)]
        b_wdn = [Buf(f"wdn{i}") for i in range(NP)]
        if not hasattr(self, "c_wup"):
            self.c_wup = [fw.new_chan() for _ in range(NP)]
            self.c_wdn = [fw.new_chan() for _ in range(NP)]
        wu_d = self.W[f"l{l}_w_up"].rearrange("(c p) n -> p c n", p=128)
        wd_d = self.W[f"l{l}_w_down"].rearrange("(f p) n -> p f n", p=128)
        for i in range(NP):
            fw.op("pool", lambda e, i=i: e.dma_start(out=wup[:, :, i * 1024:(i + 1) * 1024], in_=wu_d[:, :, i * 1024:(i + 1) * 1024]),
                  writes=[b_wup[i]], chan=self.c_wup[i])
        for i in range(NP):
            fw.op("pool", lambda e, i=i: e.dma_start(out=wdn[:, i * 8:(i + 1) * 8, :], in_=wd_d[:, i * 8:(i + 1) * 8, :]),
                  writes=[b_wdn[i]], chan=self.c_wdn[i])
        xb = A.alloc(NCH * BT, F32).rearrange("p (c t) -> p c t", c=NCH)
        b_xb = Buf("fxb")
        if not hasattr(self, "c_fxb"):
            self.c_fxb = fw.new_chan()
        hT = A.alloc(NCH * BT).rearrange("p (c t) -> p c t", c=NCH)
        b_hT = Buf("hT")
        aT_raw = A.alloc(32 * BT)
        aT = aT_raw.rearrange("p (f t) -> p f t", f=32)
        b_aT = Buf("aT")
        rstd = A.alloc(BT, F32)
        b_rstd = Buf("rstd")
        NR = 2
        rbuf = [A.alloc(BT, F32) for _ in range(NR)]
        b_rbuf = [Buf(f"r{i}") for i in range(NR)]
        sq = aT_raw[:, 0:NCH * BT].rearrange("p (c t) -> p c t", c=NCH)
        yb = aT_raw[:, 0:2 * NCH * BT].bitcast(F32).rearrange("p (c t) -> p c t", c=NCH)
        ost = aT_raw[:, 2 * NCH * BT:4 * NCH * BT].bitcast(F32).rearrange("p (t d) -> p t d", t=4)
        print(f"[ffn {l}] arena peak {A.peak * 2 / 1024:.1f} KiB")
        ri = 0
        for blk in range(NBLK):
            self.load_xblk(xb, b_xb, self.c_fxb, blk)
            self.rms_feature_major(xb, NCH, BT, self.gcol(4 + l), lambda c: hT[:, c, :], sq, b_aT, rstd, b_rstd, b_xb, b_hT, D)
            for f in range(32):
                ps, b_ps = self.next_ps()
                self.mm_group(ps[:], [(wup[:, c, f * 128:(f + 1) * 128], hT[:, c, :]) for c in range(NCH)], [b_hT, b_wup[f // 8]], b_ps)
                r, b_r = rbuf[ri % NR], b_rbuf[ri % NR]
                ri += 1
                fw.op("act", lambda e, ps=ps, r=r: e.activation(r, ps[:], AF.Relu), writes=[b_r, b_ps])
                fw.op("dve", lambda e, f=f, r=r: e.tensor_tensor(out=aT[:, f, :], in0=r, in1=r, op=ALU.mult), reads=[b_r], writes=[b_aT])
            for c in range(NCH):
                ps, b_ps = self.next_ps()
                self.mm_group(ps[:], [(wdn[:, f, c * 128:(c + 1) * 128], aT[:, f, :]) for f in range(32)], [b_aT] + b_wdn, b_ps)
                fw.op("dve", lambda e, c=c, ps=ps: e.tensor_tensor(out=xb[:, c, :], in0=xb[:, c, :], in1=ps[:], op=ALU.add), writes=[b_xb, b_ps])
            if final:
                sq2 = hT
                fw.op("act", lambda e: e.activation(sq2, xb, AF.Square), reads=[b_xb], writes=[b_hT])
                ps, b_ps = self.next_ps()
                self.mm_group(ps[:], [(self.ones[:], sq2[:, c, :]) for c in range(NCH)], [b_hT, self.b_const], b_ps)
                fw.op("act", lambda e, ps=ps: e.activation(rstd, ps[:], AF.Sqrt, scale=1.0 / D, bias=self.eps_col[:, 0:1]), writes=[b_rstd, b_ps])
                fw.op("dve", lambda e: e.reciprocal(rstd, rstd), writes=[b_rstd])
                for c in range(NCH):
                    fw.op("dve", lambda e, c=c: e.scalar_tensor_tensor(out=yb[:, c, :], in0=xb[:, c, :], scalar=self.gains[:, c, 8:9], in1=rstd,
                                                                       op0=ALU.mult, op1=ALU.mult),
                          reads=[b_xb, b_rstd, self.b_const], writes=[b_aT])
                self.emit_output(yb, b_aT, blk, ost, b_aT)
            else:
                self.store_xblk(xb, b_xb, blk)
        A.release(m)

    def stage_c(self, l, oT, b_oT, wo_name):
        fw, A = self.fw, self.arena
        m = A.mark()
        wo = A.alloc(NCH * D).rearrange("p (c n) -> p c n", c=NCH)
        b_wo = Buf("wo")
        if not hasattr(self, "c_wo"):
            self.c_wo = fw.new_chan()
        self.load_weight(wo, self.W[f"l{l}_{wo_name}"], b_wo, self.c_wo)
        xbs = [A.alloc(NCH * BT, F32).rearrange("p (c t) -> p c t", c=NCH) for _ in range(2)]
        b_xbs = [Buf("cxb0"), Buf("cxb1")]
        if not hasattr(self, "c_cxb"):
            self.c_cxb = [fw.new_chan(), fw.new_chan()]
        for blk in range(NBLK):
            s = blk % 2
            xb, b_xb = xbs[s], b_xbs[s]
            self.load_xblk(xb, b_xb, self.c_cxb[s], blk)
            for c in range(NCH):
                ps, b_ps = self.next_ps()
                self.mm_group(ps[:], [(wo[:, k, c * 128:(c + 1) * 128], oT[:, k, blk * BT:(blk + 1) * BT]) for k in range(NCH)], [b_oT, b_wo], b_ps)
                fw.op("dve", lambda e, c=c, ps=ps, xb=xb: e.tensor_tensor(out=xb[:, c, :], in0=xb[:, c, :], in1=ps[:], op=ALU.add), writes=[b_xb, b_ps])
            self.store_xblk(xb, b_xb, blk)
        A.release(m)

    def mla_phase(self, l):
        fw, A = self.fw, self.arena
        m0 = A.mark()
        gq_row = 13 if l == 0 else 14
        gkv_row = 15 if l == 0 else 16
        b_oT = Buf("oT")
        mAB = A.mark()
        self.build_tables()
        TAB = self.TAB
        cqT = A.alloc(3 * T).rearrange("p (c t) -> p c t", c=3)
        ckvT = A.alloc(2 * T).rearrange("p (c t) -> p c t", c=2)
        b_cq, b_ckv = Buf("cqT"), Buf("ckvT")
        kh = [A.alloc(T) for _ in range(2)]
        b_khr = [Buf("khr0"), Buf("khr1")]
        b_khn = [Buf("khn0"), Buf("khn1")]
        mA = A.mark()
        wdq = A.alloc(NCH * 384).rearrange("p (c n) -> p c n", c=NCH)
        wdkv = A.alloc(NCH * 320).rearrange("p (c n) -> p c n", c=NCH)
        b_wdq, b_wdkv = Buf("wdq"), Buf("wdkv")
        if not hasattr(self, "c_mla_w"):
            self.c_mla_w = [fw.new_chan() for _ in range(6)]
        cw = self.c_mla_w
        self.load_weight(wdq, self.W[f"l{l}_w_dq"], b_wdq, cw[0])
        wdkv_d = self.W[f"l{l}_w_dkv"].rearrange("(c p) n -> p c n", p=128)
        fw.op("pool", lambda e: e.dma_start(out=wdkv[:, :, 0:288], in_=wdkv_d), writes=[b_wdkv], chan=cw[1])
        fw.op("pool", lambda e: e.dma_start(out=wdkv[:, :, 288:304], in_=wdkv_d[:, :, 272:288]), writes=[b_wdkv], chan=cw[1])
        fw.op("pool", lambda e: e.dma_start(out=wdkv[:, :, 304:320], in_=wdkv_d[:, :, 256:272]), writes=[b_wdkv], chan=cw[1])
        xbs = [A.alloc(NCH * BT, F32).rearrange("p (c t) -> p c t", c=NCH) for _ in range(2)]
        b_xbs = [Buf("axb0"), Buf("axb1")]
        if not hasattr(self, "c_axb"):
            self.c_axb = [fw.new_chan(), fw.new_chan()]
        hT = A.alloc(NCH * BT).rearrange("p (c t) -> p c t", c=NCH)
        sq = A.alloc(NCH * BT).rearrange("p (c t) -> p c t", c=NCH)
        raw = A.alloc(3 * BT, F32).rearrange("p (c t) -> p c t", c=3)
        raw2 = A.alloc(2 * BT, F32).rearrange("p (c t) -> p c t", c=2)
        rstd = A.alloc(BT, F32)
        rstd2 = A.alloc(BT, F32)
        rstd3 = A.alloc(BT, F32)
        t1 = A.alloc(BT, F32)
        t2 = A.alloc(BT, F32)
        b_hT, b_sq, b_raw, b_raw2 = Buf("hT"), Buf("sq"), Buf("raw"), Buf("raw2")
        b_rstd, b_rstd2, b_rstd3, b_t1, b_t2 = Buf("rstd"), Buf("rstd2"), Buf("rstd3"), Buf("t1"), Buf("t2")
        sq_b, sq_c = Buf("sqb"), Buf("sqc")
        sqq = A.alloc(3 * BT).rearrange("p (c t) -> p c t", c=3)
        sqk = A.alloc(2 * BT).rearrange("p (c t) -> p c t", c=2)
        print(f"[mla {l} A] arena peak {A.peak * 2 / 1024:.1f} KiB")
        for blk in range(NBLK):
            s = blk % 2
            xb, b_xb = xbs[s], b_xbs[s]
            cols = slice(blk * BT, (blk + 1) * BT)
            self.load_xblk(xb, b_xb, self.c_axb[s], blk)
            self.rms_feature_major(xb, NCH, BT, self.gcol(l), lambda c: hT[:, c, :], sq, b_sq, rstd, b_rstd, b_xb, b_hT, D)
            for mch in range(3):
                ps, b_ps = self.next_ps()
                self.mm_group(ps[:], [(wdq[:, c, mch * 128:(mch + 1) * 128], hT[:, c, :]) for c in range(NCH)], [b_hT, b_wdq], b_ps)
                fw.op("act", lambda e, mch=mch, ps=ps: e.copy(raw[:, mch, :], ps[:]), writes=[b_raw, b_ps])
            self.rms_feature_major(raw, 3, BT, self.gcol(gq_row), lambda c: cqT[:, c, cols], sqq, sq_b, rstd2, b_rstd2, b_raw, b_cq, 384)
            for mch in range(2):
                ps, b_ps = self.next_ps()
                self.mm_group(ps[:], [(wdkv[:, c, mch * 128:(mch + 1) * 128], hT[:, c, :]) for c in range(NCH)], [b_hT, b_wdkv], b_ps)
                fw.op("act", lambda e, mch=mch, ps=ps: e.copy(raw2[:, mch, :], ps[:]), writes=[b_raw2, b_ps])
            self.rms_feature_major(raw2, 2, BT, self.gcol(gkv_row), lambda c: ckvT[:, c, cols], sqk, sq_c, rstd3, b_rstd3, b_raw2, b_ckv, 256)
            ps, b_ps = self.next_ps()
            self.mm_group(ps[:], [(wdkv[:, c, 192:320], hT[:, c, :]) for c in range(NCH)], [b_hT, b_wdkv], b_ps)
            fw.op("dve", lambda e, ps=ps, cols=cols: e.tensor_tensor(out=t1[64:96, :], in0=ps[64:96, :], in1=TAB[64:96, cols], op=ALU.mult),
                  reads=[self.b_TAB], writes=[b_t1, b_ps])
            fw.op("dve", lambda e, ps=ps, cols=cols: e.tensor_tensor(out=t2[64:96, :], in0=ps[96:128, :], in1=TAB[96:128, cols], op=ALU.mult),
                  reads=[self.b_TAB], writes=[b_t2, b_ps])
            fw.op("dve", lambda e, cols=cols: e.tensor_tensor(out=kh[0][64:96, cols], in0=t1[64:96, :], in1=t2[64:96, :], op=ALU.add),
                  reads=[b_t1, b_t2], writes=[b_khr[0]])
            fw.op("pool", lambda e, cols=cols: e.tensor_copy(kh[1][64:96, cols], kh[0][64:96, cols]), reads=[b_khr[0]], writes=[b_khr[1]])
        self.sdump("TAB", TAB[64:128, :], self.b_TAB)
        self.sdump("cqT", cqT.rearrange("p c t -> p (c t)"), b_cq)
        self.sdump("ckvT", ckvT.rearrange("p c t -> p (c t)"), b_ckv)
        self.sdump("krope", kh[0][64:96, :], b_khr[0])
        self.sdump("t1", t1[64:96, :], b_t1)
        self.sdump("t2", t2[64:96, :], b_t2)
        self.sdump("raw", raw.rearrange("p c t -> p (c t)"), b_raw)
        self.sdump("rstd2", rstd2, b_rstd2)
        self.sdump("wdkv", wdkv.rearrange("p c n -> p (c n)"), b_wdkv)
        if self.cfg.get("stopA"):
            A.release_top()
            A.release(m0)
            return
        fw.barrier()
        A.release(mA)
        oT = A.alloc_top(NCH * T).rearrange("p (c t) -> p c t", c=NCH)
        wq = A.alloc(3 * 16 * 128).rearrange("p (c h e) -> p c h e", c=3, h=16)
        wuk = A.alloc(2 * 1024).rearrange("p (c n) -> p c n", c=2)
        wuv = A.alloc(2 * 1024).rearrange("p (c n) -> p c n", c=2)
        b_wq, b_wuk, b_wuv = Buf("wq"), Buf("wuk"), Buf("wuv")
        wuq_d = self.W[f"l{l}_w_uq"].rearrange("(c p) (h e) -> p c h e", p=128, e=96)
        for c3 in range(3):
            fw.op("pool", lambda e, c3=c3: e.dma_start(out=wq[:, c3, :, 0:96], in_=wuq_d[:, c3, :, :]), writes=[b_wq], chan=cw[2])
            fw.op("pool", lambda e, c3=c3: e.dma_start(out=wq[:, c3, :, 96:112], in_=wuq_d[:, c3, :, 80:96]), writes=[b_wq], chan=cw[2])
            fw.op("pool", lambda e, c3=c3: e.dma_start(out=wq[:, c3, :, 112:128], in_=wuq_d[:, c3, :, 64:80]), writes=[b_wq], chan=cw[2])
        self.load_weight(wuk, self.W[f"l{l}_w_uk"], b_wuk, cw[3])
        self.load_weight(wuv, self.W[f"l{l}_w_uv"], b_wuv, cw[4])
        qh = [A.alloc(T) for _ in range(2)]
        vh = [A.alloc(32 * 128).rearrange("p (j e) -> p j e", j=32) for _ in range(2)]
        b_qh = [Buf("qh0"), Buf("qh1")]
        b_vh = [Buf("vh0"), Buf("vh1")]
        NPT = 4
        pts = [A.alloc(BT) for _ in range(NPT)]
        b_pts = [Buf(f"pt{i}") for i in range(NPT)]
        rec = A.alloc(BT, F32)
        b_rec = Buf("rec")
        qt1 = A.alloc(BT, F32)
        qt2 = A.alloc(BT, F32)
        b_qt1, b_qt2 = Buf("qt1"), Buf("qt2")
        print(f"[mla {l} B] arena peak {A.peak * 2 / 1024:.1f} KiB")
        for i in range(2):
            fw.op("pool", lambda e, i=i: e.memset(vh[i][:, :, 64:128], 1.0), writes=[b_vh[i]])
        scale = 1.0 / math.sqrt(96.0)
        heads = self.cfg.get("heads", list(range(16)))
        pti = 0
        for hi, h in enumerate(heads):
            s = hi % 2
            for tb in range(NBLK):
                cols = slice(tb * BT, (tb + 1) * BT)
                ps, b_ps = self.next_ps()
                self.mm_group(ps[:], [(wq[:, k, h, :], cqT[:, k, cols]) for k in range(3)], [b_cq, b_wq], b_ps)
                fw.op("dve", lambda e, ps=ps, s=s, cols=cols: e.tensor_copy(qh[s][0:64, cols], ps[0:64, :]), writes=[b_qh[s], b_ps])
                fw.op("dve", lambda e, ps=ps, cols=cols: e.tensor_tensor(out=qt1[64:96, :], in0=ps[64:96, :], in1=TAB[64:96, cols], op=ALU.mult),
                      reads=[self.b_TAB], writes=[b_qt1, b_ps])
                fw.op("dve", lambda e, ps=ps, cols=cols: e.tensor_tensor(out=qt2[64:96, :], in0=ps[96:128, :], in1=TAB[96:128, cols], op=ALU.mult),
                      reads=[self.b_TAB], writes=[b_qt2, b_ps])
                fw.op("pool", lambda e, s=s, cols=cols: e.tensor_tensor(out=qh[s][64:96, cols], in0=qt1[64:96, :], in1=qt2[64:96, :], op=ALU.add),
                      reads=[b_qt1, b_qt2], writes=[b_qh[s]])
            for tb in range(NBLK):
                cols = slice(tb * BT, (tb + 1) * BT)
                ps, b_ps = self.next_ps()
                self.mm_group(ps[0:64, :], [(wuk[:, k, h * 64:(h + 1) * 64], ckvT[:, k, cols]) for k in range(2)], [b_ckv, b_wuk], b_ps)
                fw.op("dve", lambda e, ps=ps, s=s, cols=cols: e.tensor_copy(kh[s][0:64, cols], ps[0:64, :]), writes=[b_khn[s], b_ps])
            for j0 in range(0, 32, 8):
                ps, b_ps = self.next_ps()
                fns = []
                for jj in range(8):
                    j = j0 + jj
                    for k in range(2):
                        fns.append(lambda e, ps=ps, jj=jj, j=j, k=k, h=h: e.matmul(ps[:, jj * 64:(jj + 1) * 64], lhsT=ckvT[:, k, j * 128:(j + 1) * 128],
                                                                             rhs=wuv[:, k, h * 64:(h + 1) * 64], start=(k == 0), stop=(k == 1)))
                fw.pe_group(fns, reads=[b_ckv, b_wuv], writes=[b_ps])
                fw.op("act", lambda e, ps=ps, s=s, j0=j0: e.copy(vh[s][:, j0:j0 + 8, 0:64], ps[:].rearrange("p (j e) -> p j e", j=8)), writes=[b_vh[s], b_ps])
            tiles = []
            for qb in range(NBLK):
                for kc in range(4 * qb + 4):
                    tiles.append((qb, kc))
            LA = 2
            st_ps = {}
            cur_acc = {}

            def emit_S(i):
                qb, kc = tiles[i]
                nq0 = max(0, kc - 4 * qb) * 128
                ps, b_ps = self.next_ps()
                st_ps[i] = (ps, b_ps)
                self.mm_group(ps[:, nq0:BT], [(kh[s][0:96, kc * 128:(kc + 1) * 128], qh[s][0:96, qb * BT + nq0:(qb + 1) * BT])],
                              [b_khr[s], b_khn[s], b_qh[s]], b_ps)

            for i in range(min(LA, len(tiles))):
                emit_S(i)
            for i, (qb, kc) in enumerate(tiles):
                if i + LA < len(tiles):
                    emit_S(i + LA)
                nq0 = max(0, kc - 4 * qb) * 128
                last = 4 * qb + 3
                ps, b_ps = st_ps.pop(i)
                pt, b_pt = pts[pti % NPT], b_pts[pti % NPT]
                pti += 1
                fw.op("act", lambda e, ps=ps, pt=pt, nq0=nq0: e.activation(pt[:, nq0:BT], ps[:, nq0:BT], AF.Exp, scale=scale), writes=[b_pt, b_ps])
                if kc >= 4 * qb:
                    fw.op("pool", lambda e, pt=pt, nq0=nq0: e.tensor_tensor(out=pt[:, nq0:nq0 + 128], in0=pt[:, nq0:nq0 + 128], in1=self.triI[:], op=ALU.mult),
                          reads=[self.b_const], writes=[b_pt])
                if kc == 0:
                    cur_acc[qb] = self.next_acc()
                po, b_po = cur_acc[qb]
                fw.pe_group([lambda e, po=po, pt=pt, nq0=nq0, kc=kc, last=last, s=s: e.matmul(po[:, nq0:BT], lhsT=vh[s][:, kc, :], rhs=pt[:, nq0:BT],
                                                                                         start=(kc == 0), stop=(kc == last))],
                            reads=[b_vh[s], b_pt], writes=[b_po])
                if kc == last:
                    qc = slice(qb * BT, (qb + 1) * BT)
                    fw.op("dve", lambda e, po=po: e.reciprocal(rec[0:64, :], po[64:128, :]), writes=[b_rec, b_po])
                    fw.op("dve", lambda e, po=po, h=h, qc=qc: e.tensor_tensor(out=oT[(h % 2) * 64:(h % 2) * 64 + 64, h // 2, qc], in0=po[0:64, :],
                                                                              in1=rec[0:64, :], op=ALU.mult),
                          reads=[b_rec], writes=[b_oT, b_po])
        self.sdump("qh", qh[(len(heads) - 1) % 2][0:96, :], b_qh[(len(heads) - 1) % 2])
        self.sdump("khn", kh[(len(heads) - 1) % 2][0:64, :], b_khn[(len(heads) - 1) % 2])
        self.sdump("vh", vh[(len(heads) - 1) % 2].rearrange("p j e -> p (j e)"), b_vh[(len(heads) - 1) % 2])
        self.sdump("oT", oT.rearrange("p c t -> p (c t)"), b_oT)
        A.release(mAB)
        fw.barrier()
        self.stage_c(l, oT, b_oT, "w_o")
        A.release_top()
        A.release(m0)

    def conv_phase(self, l):
        fw, A = self.fw, self.arena
        m0 = A.mark()
        win = A.alloc(NCH * 3072).rearrange("p (c n) -> p c n", c=NCH)
        wout = A.alloc(NCH * D).rearrange("p (c n) -> p c n", c=NCH)
        b_win = [Buf(f"win{i}") for i in range(3)]
        b_wout = Buf("wout")
        if not hasattr(self, "c_conv_w"):
            self.c_conv_w = [fw.new_chan() for _ in range(4)]
        cw = self.c_conv_w
        win_d = self.W[f"l{l}_w_in"].rearrange("(c p) n -> p c n", p=128)
        for i in range(3):
            fw.op("pool", lambda e, i=i: e.dma_start(out=win[:, :, i * 1024:(i + 1) * 1024], in_=win_d[:, :, i * 1024:(i + 1) * 1024]),
                  writes=[b_win[i]], chan=cw[i])
        self.load_weight(wout, self.W[f"l{l}_w_out"], b_wout, cw[3])
        xbs = [A.alloc(NCH * BT, F32).rearrange("p (c t) -> p c t", c=NCH) for _ in range(2)]
        b_xbs = [Buf("vxb0"), Buf("vxb1")]
        if not hasattr(self, "c_vxb"):
            self.c_vxb = [fw.new_chan(), fw.new_chan()]
        hT = A.alloc(NCH * BT).rearrange("p (c t) -> p c t", c=NCH)
        sq = A.alloc(NCH * BT).rearrange("p (c t) -> p c t", c=NCH)
        zT = A.alloc(NCH * BT).rearrange("p (c t) -> p c t", c=NCH)
        ucur = A.alloc(NCH * (BT + 2), F32).rearrange("p (c t) -> p c t", c=NCH)
        rstd = A.alloc(BT, F32)
        NTM = 2
        tmpc = [A.alloc(BT, F32) for _ in range(NTM)]
        acc = [A.alloc(BT, F32) for _ in range(NTM)]
        b_hT, b_sq, b_zT, b_rstd = Buf("hT"), Buf("sq"), Buf("zT"), Buf("rstd")
        b_u = [Buf(f"u{c}") for c in range(NCH)]
        b_tmpc = [Buf(f"tc{i}") for i in range(NTM)]
        b_acc = [Buf(f"ac{i}") for i in range(NTM)]
        print(f"[conv {l}] arena peak {A.peak * 2 / 1024:.1f} KiB")
        fw.op("pool", lambda e: e.memset(ucur.rearrange("p c t -> p (c t)"), 0.0), writes=b_u)
        ti = 0
        for blk in range(NBLK):
            s = blk % 2
            xb, b_xb = xbs[s], b_xbs[s]
            self.load_xblk(xb, b_xb, self.c_vxb[s], blk)
            self.rms_feature_major(xb, NCH, BT, self.gcol(l), lambda c: hT[:, c, :], sq, b_sq, rstd, b_rstd, b_xb, b_hT, D)
            for c in range(NCH):
                psB, b_psB = self.next_ps()
                self.mm_group(psB[:], [(win[:, k, c * 128:(c + 1) * 128], hT[:, k, :]) for k in range(NCH)], [b_hT, b_win[0]], b_psB)
                psC, b_psC = self.next_ps()
                self.mm_group(psC[:], [(win[:, k, 1024 + c * 128:1024 + (c + 1) * 128], hT[:, k, :]) for k in range(NCH)], [b_hT, b_win[1]], b_psC)
                psU, b_psU = self.next_ps()
                self.mm_group(psU[:], [(win[:, k, 2048 + c * 128:2048 + (c + 1) * 128], hT[:, k, :]) for k in range(NCH)], [b_hT, b_win[2]], b_psU)
                tc_, b_tc = tmpc[ti % NTM], b_tmpc[ti % NTM]
                ac_, b_ac = acc[ti % NTM], b_acc[ti % NTM]
                ti += 1
                fw.op("act", lambda e, psC=psC, tc_=tc_: e.copy(tc_, psC[:]), writes=[b_tc, b_psC])
                if blk > 0:
                    fw.op("pool", lambda e, c=c: e.tensor_copy(ucur[:, c, 0:2], ucur[:, c, BT:BT + 2]), writes=[b_u[c]])
                fw.op("dve", lambda e, c=c, psU=psU, tc_=tc_: e.tensor_tensor(out=ucur[:, c, 2:BT + 2], in0=psU[:], in1=tc_, op=ALU.mult),
                      reads=[b_tc], writes=[b_u[c], b_psU])
                fw.op("dve", lambda e, c=c, ac_=ac_: e.tensor_scalar(ac_, ucur[:, c, 2:BT + 2], self.gains[:, c, 11:12], self.gains[:, c, 12:13],
                                                                     op0=ALU.mult, op1=ALU.add),
                      reads=[b_u[c], self.b_const], writes=[b_ac])
                fw.op("dve", lambda e, c=c, ac_=ac_: e.scalar_tensor_tensor(out=ac_, in0=ucur[:, c, 1:BT + 1], scalar=self.gains[:, c, 10:11], in1=ac_,
                                                                             op0=ALU.mult, op1=ALU.add),
                      reads=[b_u[c], self.b_const], writes=[b_ac])
                fw.op("dve", lambda e, c=c, ac_=ac_: e.scalar_tensor_tensor(out=ac_, in0=ucur[:, c, 0:BT], scalar=self.gains[:, c, 9:10], in1=ac_,
                                                                             op0=ALU.mult, op1=ALU.add),
                      reads=[b_u[c], self.b_const], writes=[b_ac])
                fw.op("dve", lambda e, c=c, ac_=ac_, psB=psB: e.tensor_tensor(out=zT[:, c, :], in0=psB[:], in1=ac_, op=ALU.mult),
                      reads=[b_ac], writes=[b_zT, b_psB])
            for c in range(NCH):
                ps, b_ps = self.next_ps()
                self.mm_group(ps[:], [(wout[:, k, c * 128:(c + 1) * 128], zT[:, k, :]) for k in range(NCH)], [b_zT, b_wout], b_ps)
                fw.op("dve", lambda e, c=c, ps=ps, xb=xb: e.tensor_tensor(out=xb[:, c, :], in0=xb[:, c, :], in1=ps[:], op=ALU.add), writes=[b_xb, b_ps])
            self.store_xblk(xb, b_xb, blk)
        A.release(m0)

    def sb_phase(self, l):
        fw, A = self.fw, self.arena
        m0 = A.mark()
        b_oT = Buf("oT")
        mAB = A.mark()
        hTf = A.alloc(NCH * T).rearrange("p (c t) -> p c t", c=NCH)
        b_hTf = Buf("hTf")
        mA = A.mark()
        xbs = [A.alloc(NCH * BT, F32).rearrange("p (c t) -> p c t", c=NCH) for _ in range(2)]
        b_xbs = [Buf("sxb0"), Buf("sxb1")]
        if not hasattr(self, "c_sxb"):
            self.c_sxb = [fw.new_chan(), fw.new_chan()]
        sq = A.alloc(NCH * BT).rearrange("p (c t) -> p c t", c=NCH)
        rstd = A.alloc(BT, F32)
        b_sq, b_rstd = Buf("sq"), Buf("rstd")
        for blk in range(NBLK):
            s = blk % 2
            cols = slice(blk * BT, (blk + 1) * BT)
            self.load_xblk(xbs[s], b_xbs[s], self.c_sxb[s], blk)
            self.rms_feature_major(xbs[s], NCH, BT, self.gcol(l), lambda c: hTf[:, c, cols], sq, b_sq, rstd, b_rstd, b_xbs[s], b_hTf, D)
        fw.barrier()
        A.release(mA)
        oT = A.alloc_top(NCH * T).rearrange("p (c t) -> p c t", c=NCH)
        wp = [A.alloc(NCH * 3 * 128).rearrange("p (c g e) -> p c g e", c=NCH, g=3) for _ in range(2)]
        b_wp = [Buf("wp0"), Buf("wp1")]
        if not hasattr(self, "c_wp"):
            self.c_wp = [fw.new_chan(), fw.new_chan()]
        qp = [A.alloc(T) for _ in range(2)]
        kp = [A.alloc(T) for _ in range(2)]
        vp = [A.alloc(32 * 128).rearrange("p (j e) -> p j e", j=32) for _ in range(2)]
        b_qp, b_kp, b_vp = [Buf("qp0"), Buf("qp1")], [Buf("kp0"), Buf("kp1")], [Buf("vp0"), Buf("vp1")]
        NE = 2
        Es = [A.alloc(BT, F32) for _ in range(NE)]
        b_Es = [Buf(f"E{i}") for i in range(NE)]
        NS = 3
        sps = [A.alloc(BT) for _ in range(NS)]
        b_sps = [Buf(f"sp{i}") for i in range(NS)]
        ats = [A.alloc(BT) for _ in range(NS)]
        b_ats = [Buf(f"at{i}") for i in range(NS)]
        Rs = [A.alloc(BT) for _ in range(2)]
        b_Rs = [Buf("R0"), Buf("R1")]
        print(f"[sb {l} B] arena peak {A.peak * 2 / 1024:.1f} KiB")
        wqkv_d = self.W[f"l{l}_w_qkv"].rearrange("(c p) (g n) -> p c g n", p=128, g=3)
        pairs = self.cfg.get("pairs", list(range(8)))
        ei = si = ai = 0
        ri = 0
        for pi, p in enumerate(pairs):
            s = pi % 2
            for g3 in range(3):
                fw.op("pool", lambda e, s=s, p=p, g3=g3: e.dma_start(out=wp[s][:, :, g3, :], in_=wqkv_d[:, :, g3, p * 128:(p + 1) * 128]),
                      writes=[b_wp[s]], chan=self.c_wp[s])
            for tb in range(NBLK):
                cols = slice(tb * BT, (tb + 1) * BT)
                ps, b_ps = self.next_ps()
                self.mm_group(ps[:], [(wp[s][:, c, 0, :], hTf[:, c, cols]) for c in range(NCH)], [b_hTf, b_wp[s]], b_ps)
                fw.op("dve", lambda e, ps=ps, s=s, cols=cols: e.tensor_copy(qp[s][:, cols], ps[:]), writes=[b_qp[s], b_ps])
                ps, b_ps = self.next_ps()
                self.mm_group(ps[:], [(wp[s][:, c, 1, :], hTf[:, c, cols]) for c in range(NCH)], [b_hTf, b_wp[s]], b_ps)
                fw.op("act", lambda e, ps=ps, s=s, cols=cols: e.copy(kp[s][:, cols], ps[:]), writes=[b_kp[s], b_ps])
            for j0 in range(0, 32, 4):
                ps, b_ps = self.next_ps()
                fns = []
                for jj in range(4):
                    j = j0 + jj
                    for c in range(NCH):
                        fns.append(lambda e, ps=ps, jj=jj, j=j, c=c, s=s: e.matmul(ps[:, jj * 128:(jj + 1) * 128], lhsT=hTf[:, c, j * 128:(j + 1) * 128],
                                                                                  rhs=wp[s][:, c, 2, :], start=(c == 0), stop=(c == NCH - 1)))
                fw.pe_group(fns, reads=[b_hTf, b_wp[s]], writes=[b_ps])
                fw.op("dve", lambda e, ps=ps, s=s, j0=j0: e.tensor_copy(vp[s][:, j0:j0 + 4, :], ps[:].rearrange("p (j e) -> p j e", j=4)), writes=[b_vp[s], b_ps])
            for hh in range(2):
                pr = slice(hh * 64, (hh + 1) * 64)
                for qb in range(NBLK):
                    last = 4 * qb + 3
                    po, b_po = self.next_acc()
                    R, b_R = Rs[ri % 2], b_Rs[ri % 2]
                    ri += 1
                    fw.op("pool", lambda e, R=R: e.memset(R, 0.0), writes=[b_R])
                    for kc in range(last, -1, -1):
                        nq0 = max(0, kc - 4 * qb) * 128
                        diag = kc >= 4 * qb
                        kcs = slice(kc * 128, (kc + 1) * 128)
                        qcs = slice(qb * BT + nq0, (qb + 1) * BT)
                        psZ, b_psZ = self.next_ps()
                        self.mm_group(psZ[:, nq0:BT], [(kp[s][pr, kcs], qp[s][pr, qcs])], [b_kp[s], b_qp[s]], b_psZ)
                        E, b_E = Es[ei % NE], b_Es[ei % NE]
                        ei += 1
                        sp, b_sp = sps[si % NS], b_sps[si % NS]
                        si += 1
                        at, b_at = ats[ai % NS], b_ats[ai % NS]
                        ai += 1
                        fw.op("act", lambda e, psZ=psZ, E=E, nq0=nq0: e.activation(E[:, nq0:BT], psZ[:, nq0:BT], AF.Exp, scale=0.125), writes=[b_E, b_psZ])
                        fw.op("act", lambda e, E=E, sp=sp, nq0=nq0: e.activation(sp[:, nq0:BT], E[:, nq0:BT], AF.Ln, bias=1.0), reads=[b_E], writes=[b_sp])
                        if diag:
                            fw.op("pool", lambda e, sp=sp, nq0=nq0: e.tensor_tensor(out=sp[:, nq0:nq0 + 128], in0=sp[:, nq0:nq0 + 128], in1=self.triS[:], op=ALU.mult),
                                  reads=[self.b_const], writes=[b_sp])
                        psL, b_psL = self.next_ps()
                        prs = [(kp[s][pr, kcs], qp[s][pr, qcs]), (self.Uinc[:], sp[:, nq0:BT])]
                        rd = [b_kp[s], b_qp[s], b_sp, self.b_const]
                        if kc != last:
                            prs.append((self.neg8[:], R[:, nq0:BT]))
                            rd.append(b_R)
                        self.mm_group(psL[:, nq0:BT], prs, rd, b_psL)
                        fw.op("act", lambda e, psL=psL, at=at, nq0=nq0: e.activation(at[:, nq0:BT], psL[:, nq0:BT], AF.Exp, scale=0.125), writes=[b_at, b_psL])
                        if diag:
                            fw.op("pool", lambda e, at=at, nq0=nq0: e.tensor_tensor(out=at[:, nq0:nq0 + 128], in0=at[:, nq0:nq0 + 128], in1=self.triS[:], op=ALU.mult),
                                  reads=[self.b_const], writes=[b_at])
                        if kc != 0:
                            fw.op("dve", lambda e, R=R, sp=sp, nq0=nq0: e.tensor_tensor(out=R[:, nq0:BT], in0=R[:, nq0:BT], in1=sp[:, nq0:BT], op=ALU.add),
                                  reads=[b_sp], writes=[b_R])
                        fw.pe_group([lambda e, po=po, at=at, nq0=nq0, kc=kc, last=last, s=s: e.matmul(po[:, nq0:BT], lhsT=vp[s][:, kc, :], rhs=at[:, nq0:BT],
                                                                                                     start=(kc == last), stop=(kc == 0))],
                                    reads=[b_vp[s], b_at], writes=[b_po])
                    qc = slice(qb * BT, (qb + 1) * BT)
                    fw.op("dve", lambda e, po=po, p=p, pr=pr, qc=qc: e.tensor_copy(oT[pr, p, qc], po[pr, :]), writes=[b_oT, b_po])
        A.release(mAB)
        fw.barrier()
        self.stage_c(l, oT, b_oT, "w_o")
        A.release_top()
        A.release(m0)


_NC_CACHE = {}


def build_nc(cfg=None):
    key = repr(sorted((cfg or {}).items()))
    if key not in _NC_CACHE:
        _NC_CACHE[key] = Prog(cfg).build()
    return _NC_CACHE[key]


def kernel(**inputs):
    nc = build_nc(None)
    x = np.ascontiguousarray(inputs["x"], dtype=np.float32)
    pos = np.ascontiguousarray(inputs["positions"], dtype=np.int32)
    shared = {k: np.ascontiguousarray(v, dtype=np.float32) for k, v in inputs.items() if k not in ("x", "positions")}
    in_maps = []
    for b in range(N_CORES):
        d = dict(shared)
        d["x"] = x[b]
        d["positions"] = pos[b:b + 1]
        in_maps.append(d)
    res = run_bass_kernel_spmd(nc, in_maps, core_ids=list(range(N_CORES)))
    return np.stack([np.asarray(r["out"], dtype=np.float32) for r in res.results], axis=0)
```

```python
import math
from contextlib import ExitStack
import numpy as np
import concourse.bass as bass
import concourse.mybir as mybir
from concourse.bass_utils import run_bass_kernel_spmd

F32 = mybir.dt.float32
BF16 = mybir.dt.bfloat16
I32 = mybir.dt.int32
AF = mybir.ActivationFunctionType
ALU = mybir.AluOpType

T = 4096
D = 1024
BT = 512
NBLK = T // BT
NCH = D // 128
DFF = 4096
EPS = 1e-6
N_CORES = 8

MLA_LAYERS = (0, 3)
WNAMES = {
    0: ["w_dq", "norm_q", "w_uq", "w_dkv", "norm_kv", "w_uk", "w_uv", "w_o"],
    1: ["w_in", "conv_w", "conv_b", "w_out"],
    2: ["w_qkv", "w_o"],
    3: ["w_dq", "norm_q", "w_uq", "w_dkv", "norm_kv", "w_uk", "w_uv", "w_o"],
}
WSHAPES = {
    "w_dq": [1024, 384], "norm_q": [384], "w_uq": [384, 1536], "w_dkv": [1024, 288], "norm_kv": [256],
    "w_uk": [256, 1024], "w_uv": [256, 1024], "w_o": [1024, 1024], "w_in": [1024, 3072],
    "conv_w": [3, 1024], "conv_b": [1024], "w_out": [1024, 1024], "w_qkv": [1024, 3072],
    "norm_mix": [1024], "norm_mlp": [1024], "w_up": [1024, 4096], "w_down": [4096, 1024],
}


class Buf:
    __slots__ = ("name", "w", "r")

    def __init__(self, name):
        self.name = name
        self.w = None
        self.r = {}


class Chan:
    __slots__ = ("sem", "count")

    def __init__(self):
        self.sem = None
        self.count = 0


class Stream:
    def __init__(self, name):
        self.name = name
        self.items = []
        self.nops = 0
        self.seen = {}
        self.sem = None
        self.referenced = set()
        self.rank = {}


class FW:
    def __init__(self):
        self.streams = {n: Stream(n) for n in ("pe", "act", "dve", "pool", "sp")}
        self.chans = []

    def _need(self, st, tok):
        if tok is None:
            return
        if tok[0] == 'E':
            src = tok[1]
            if src is st and st.name in ("pe", "sp"):
                return
            key = src.name
        else:
            key = id(tok[1])
        val = tok[2]
        if st.seen.get(key, -1) >= val:
            return
        st.seen[key] = val
        st.items.append(('wait', tok))
        if tok[0] == 'E':
            src.referenced.add(val)

    def _deps(self, st, reads, writes):
        for b in reads:
            self._need(st, b.w)
        for b in writes:
            self._need(st, b.w)
            for t in b.r.values():
                self._need(st, t)

    def _commit(self, tok, key, reads, writes):
        for b in reads:
            b.r[key] = tok
        for b in writes:
            b.w = tok
            b.r = {}

    def op(self, eng, fn, reads=(), writes=(), chan=None):
        st = self.streams[eng]
        self._deps(st, reads, writes)
        if chan is not None:
            chan.count += 16
            tok = ('C', chan, chan.count)
            st.items.append(('dma', fn, chan))
            key = id(chan)
        else:
            st.nops += 1
            tok = ('E', st, st.nops)
            st.items.append(('op', fn, st.nops))
            key = st.name
        self._commit(tok, key, reads, writes)
        return tok

    def pe_group(self, fns, reads=(), writes=()):
        st = self.streams["pe"]
        self._deps(st, reads, writes)
        for fn in fns[:-1]:
            st.items.append(('op', fn, None))
        st.nops += 1
        tok = ('E', st, st.nops)
        st.items.append(('op', fns[-1], st.nops))
        self._commit(tok, "pe", reads, writes)
        return tok

    def new_chan(self):
        c = Chan()
        self.chans.append(c)
        return c

    def wait_all(self, eng, bufs):
        st = self.streams[eng]
        for b in bufs:
            self._need(st, b.w)
            for t in b.r.values():
                self._need(st, t)

    def barrier(self):
        toks = []
        for st in self.streams.values():
            if st.nops > 0:
                toks.append(('E', st, st.nops))
        for c in self.chans:
            if c.count > 0:
                toks.append(('C', c, c.count))
        for st in self.streams.values():
            for t in toks:
                self._need(st, t)

    def n_sems(self):
        return len(self.streams) + len(self.chans)

    def replay(self, block, sems):
        it = iter(sems)
        for st in self.streams.values():
            st.sem = next(it)
        for c in self.chans:
            c.sem = next(it)
        for st in self.streams.values():
            st.rank = {idx: i + 1 for i, idx in enumerate(sorted(st.referenced))}

        def run(st, h):
            for item in st.items:
                if item[0] == 'wait':
                    tok = item[1]
                    if tok[0] == 'E':
                        h.wait_ge(tok[1].sem, tok[1].rank[tok[2]])
                    else:
                        h.wait_ge(tok[1].sem, tok[2])
                elif item[0] == 'dma':
                    item[1](h).then_inc(item[2].sem, 16)
                else:
                    ins = item[1](h)
                    if item[2] is not None and item[2] in st.rank:
                        ins.then_inc(st.sem, 1)

        S = self.streams
        block.tensor(lambda h: run(S["pe"], h))
        block.scalar(lambda h: run(S["act"], h))
        block.vector(lambda h: run(S["dve"], h))
        block.gpsimd(lambda h: run(S["pool"], h))
        block.sync(lambda h: run(S["sp"], h))


class Arena:
    def __init__(self, ap):
        self.ap = ap
        self.n = ap.shape[1]
        self.off = 0
        self.peak = 0
        self.top = self.n

    def alloc(self, nelem, dt=BF16):
        nb = nelem * (4 if dt in (F32, I32) else 2)
        n16 = (nb + 31) // 32 * 16
        s = self.off
        self.off += n16
        self.peak = max(self.peak, self.off)
        assert self.off <= self.top, f"arena overflow {self.off}>{self.top}"
        v = self.ap[:, s:s + nb // 2]
        if dt != BF16:
            v = v.bitcast(dt)
        return v

    def alloc_top(self, nelem):
        self.top -= (nelem + 15) // 16 * 16
        assert self.off <= self.top, f"arena overflow(top) {self.off}>{self.top}"
        return self.ap[:, self.top:self.top + nelem]

    def release_top(self):
        self.top = self.n

    def mark(self):
        return self.off

    def release(self, m):
        self.off = m


class Prog:
    def __init__(self, cfg=None):
        self.cfg = cfg or {}
        self.nc = bass.Bass("TRN2", target_bir_lowering=False)
        self.fw = FW()

    def next_ps(self):
        i = self.ps_i
        self.ps_i = (self.ps_i + 1) % len(self.ps_gen)
        return self.ps_gen[i]

    def next_acc(self):
        i = self.acc_i
        self.acc_i = (self.acc_i + 1) % len(self.ps_acc)
        return self.ps_acc[i]

    def run_pipeline(self, N, stages, skews, extra):
        total = N + max(skews)
        per = max(1, N // (len(extra) + 1)) if extra else 0
        ti = 0
        for n in range(total):
            for fn, sk in zip(stages, skews):
                i = n - sk
                if 0 <= i < N:
                    fn(i)
            if extra and ti < len(extra) and n % per == per - 1:
                extra[ti]()
                ti += 1
        while ti < len(extra):
            extra[ti]()
            ti += 1

    def mm_group(self, out_ap, pairs, reads, ps_buf):
        n = len(pairs)
        fns = []
        for i, (l, r) in enumerate(pairs):
            fns.append(lambda e, l=l, r=r, i=i: e.matmul(out_ap, lhsT=l, rhs=r, start=(i == 0), stop=(i == n - 1)))
        return self.fw.pe_group(fns, reads=reads, writes=[ps_buf])

    def load_weight(self, dst3, src2, buf, chan, ncols_piece=None):
        self.fw.op("pool", lambda e: e.dma_start(out=dst3, in_=src2.rearrange("(c p) n -> p c n", p=128)),
                   writes=[buf], chan=chan)

    def rms_feature_major(self, xap, nch, width, gcol, out_fn, sq, b_sq, rstd, b_rstd, b_x, b_out, n_feat):
        fw = self.fw
        fw.op("act", lambda e: e.activation(sq[:, 0:nch, 0:width], xap, AF.Square), reads=[b_x], writes=[b_sq])
        ps, b_ps = self.next_ps()
        self.mm_group(ps[:, 0:width], [(self.ones[:], sq[:, c, 0:width]) for c in range(nch)], [b_sq, self.b_const], b_ps)
        fw.op("act", lambda e: e.activation(rstd[:, 0:width], ps[:, 0:width], AF.Sqrt, scale=1.0 / n_feat, bias=self.eps_col[:, 0:1]),
              reads=[], writes=[b_rstd, b_ps])
        fw.op("dve", lambda e: e.reciprocal(rstd[:, 0:width], rstd[:, 0:width]), reads=[], writes=[b_rstd])
        for c in range(nch):
            dst = out_fn(c)
            gsc = gcol(c)
            fw.op("dve", lambda e, c=c, dst=dst, gsc=gsc: e.scalar_tensor_tensor(out=dst, in0=xap[:, c, :], scalar=gsc, in1=rstd[:, 0:width],
                                                               op0=ALU.mult, op1=ALU.mult),
                  reads=[b_x, b_rstd, self.b_const], writes=[b_out])

    def build(self):
        nc, fw = self.nc, self.fw
        cfg = self.cfg
        self.es = es = ExitStack()
        with es:
            self.x_in = nc.dram_tensor("x", [T, D], F32, kind="ExternalInput").ap()
            self.pos_in = nc.dram_tensor("positions", [1, T], I32, kind="ExternalInput").ap()
            self.W = {}
            for l in range(4):
                for nm in ["norm_mix"] + WNAMES[l] + ["norm_mlp", "w_up", "w_down"]:
                    full = f"l{l}_{nm}"
                    self.W[full] = nc.dram_tensor(full, WSHAPES[nm], F32, kind="ExternalInput").ap()
            self.W["final_norm"] = nc.dram_tensor("final_norm", [D], F32, kind="ExternalInput").ap()
            self.out = nc.dram_tensor("out", [T, D], F32, kind="ExternalOutput").ap()
            self.xT_d = nc.dram_tensor("xT_scratch", [NCH, 128, T], F32).ap()
            self.b_xd = [Buf(f"xd{b}") for b in range(NBLK)]
            self.c_xd = [fw.new_chan() for _ in range(NBLK)]
            self.b_outd = Buf("outd")
            self.c_outd = fw.new_chan()
            self.dbg = []
            if cfg.get("dbg"):
                for k in range(7):
                    self.dbg.append(nc.dram_tensor(f"dbg{k}", [NCH, 128, T], F32, kind="ExternalOutput").ap())
            self.dbg_i = 0

            self.ps_all = []
            for i in range(8):
                t = es.enter_context(nc.psum_tensor(f"ps{i}", [128, 512], F32))
                self.ps_all.append((t, Buf(f"ps{i}")))
            self.ps_acc = self.ps_all[0:2]
            self.ps_gen = self.ps_all[2:8]
            self.ps_i = 0
            self.acc_i = 0

            def sb(name, shape, dt):
                return es.enter_context(nc.sbuf_tensor(name, shape, dt))
            self.ones = sb("ones", [128, 128], BF16)
            self.ident = sb("ident", [128, 128], F32)
            self.triI = sb("triI", [128, 128], BF16)
            self.triS = sb("triS", [128, 128], BF16)
            self.Uinc = sb("Uinc", [128, 128], BF16)
            self.neg8 = sb("neg8", [128, 128], BF16)
            self.gains = sb("gains", [128, NCH, 32], F32)
            self.eps_col = sb("eps_col", [128, 1], F32)
            self.negpi = sb("negpi", [128, 1], F32)
            self.b_const = Buf("const")
            self.b_TAB = Buf("TAB")
            arena_elems = (int(nc.sbuf_bytes_remaining) - 1024) // 64 * 32
            print("arena KiB", arena_elems * 2 / 1024)
            self.arena = Arena(sb("arena", [128, arena_elems], BF16))
            self.b_arena_guard = Buf("arena")

            self.setup_consts()
            fw.barrier()
            self.prologue()
            fw.barrier()
            layers = cfg.get("layers", [0, 1, 2, 3])
            for l in layers:
                if cfg.get("mix", True):
                    if l in MLA_LAYERS:
                        self.mla_phase(l)
                    elif l == 1:
                        self.conv_phase(l)
                    else:
                        self.sb_phase(l)
                    fw.barrier()
                    self.dump_dbg()
                if cfg.get("ffn", True):
                    self.ffn_phase(l, final=(l == layers[-1]) and cfg.get("final", True))
                    fw.barrier()
                    if not getattr(self, "_emitted_final", False):
                        self.dump_dbg()
            if not getattr(self, "_emitted_final", False):
                self.dump_xT()
            fw.wait_all("sp", [self.b_outd])
            sems = [es.enter_context(nc.semaphore(f"s{i}")) for i in range(fw.n_sems())]
            block = es.enter_context(nc.Block())
            fw.replay(block, sems)
        return nc

    def setup_consts(self):
        nc, fw, A = self.nc, self.fw, self.arena
        m = A.mark()
        bc = self.b_const
        iot = A.alloc(128, I32)
        b_t = Buf("ctmp")
        fw.op("pool", lambda e: e.iota(iot, pattern=[[1, 128]], base=0, channel_multiplier=-1), writes=[b_t])
        fw.op("dve", lambda e: e.tensor_single_scalar(self.ident[:], iot, 0, ALU.is_equal), reads=[b_t], writes=[bc])
        fw.op("dve", lambda e: e.tensor_single_scalar(self.triI[:], iot, 0, ALU.is_ge), reads=[b_t], writes=[bc])
        fw.op("dve", lambda e: e.tensor_single_scalar(self.triS[:], iot, 0, ALU.is_gt), reads=[b_t], writes=[bc])
        fw.op("dve", lambda e: e.tensor_scalar(self.Uinc[:], iot, 0, -8.0, op0=ALU.is_le, op1=ALU.mult), reads=[b_t], writes=[bc])
        fw.op("dve", lambda e: e.memset(self.ones[:], 1.0), writes=[bc])
        fw.op("dve", lambda e: e.memset(self.neg8[:], -8.0), writes=[bc])
        fw.op("dve", lambda e: e.memset(self.eps_col[:], EPS), writes=[bc])
        fw.op("dve", lambda e: e.memset(self.negpi[:], -math.pi), writes=[bc])
        gv = A.alloc(1024, F32)
        b_gv = Buf("gv")
        c_gv = fw.new_chan()
        fw.op("dve", lambda e: e.memset(gv[0:32, :], 0.0), writes=[b_gv])
        rows = []
        for l in range(4):
            rows.append((l, f"l{l}_norm_mix", 1024))
            rows.append((4 + l, f"l{l}_norm_mlp", 1024))
        rows.append((8, "final_norm", 1024))
        rows.append((12, "l1_conv_b", 1024))
        rows.append((13, "l0_norm_q", 384))
        rows.append((14, "l3_norm_q", 384))
        rows.append((15, "l0_norm_kv", 256))
        rows.append((16, "l3_norm_kv", 256))
        for r, nm, n in rows:
            fw.op("sp", lambda e, r=r, nm=nm, n=n: e.dma_start(out=gv[r:r + 1, 0:n], in_=self.W[nm].rearrange("(o n) -> o n", o=1)),
                  writes=[b_gv], chan=c_gv)
        fw.op("sp", lambda e: e.dma_start(out=gv[9:12, :], in_=self.W["l1_conv_w"]), writes=[b_gv], chan=c_gv)
        ps, b_ps = self.next_ps()
        fns = [lambda e, c=c: e.transpose(ps[:, c * 32:(c + 1) * 32], gv[0:32, c * 128:(c + 1) * 128], self.ident[0:32, 0:32]) for c in range(NCH)]
        fw.pe_group(fns, reads=[b_gv, bc], writes=[b_ps])
        fw.op("dve", lambda e: e.tensor_copy(self.gains[:].rearrange("p c r -> p (c r)"), ps[:, 0:256]), reads=[], writes=[bc, b_ps])
        self._const_mark = m

    def build_tables(self):
        fw, A = self.fw, self.arena
        bc = self.b_const
        b_t = Buf("ttmp")
        TAB = self.TAB = A.alloc(T, F32)
        m = A.mark()
        posi = A.alloc(T, I32)
        ang = A.alloc(T, F32)
        tq = A.alloc(T, F32)
        idx = A.alloc(1, I32)
        cf = A.alloc(1, F32)
        s1 = A.alloc(1, F32)
        invf = A.alloc(1, F32)
        off = A.alloc(1, F32)
        b_pos = Buf("pos")
        if not hasattr(self, "c_pos"):
            self.c_pos = fw.new_chan()
        c_pos = self.c_pos
        R = slice(64, 128)
        TWO_PI = 2.0 * math.pi
        fw.op("sp", lambda e: e.dma_start(out=posi[R, :], in_=self.pos_in[0, :].partition_broadcast(64)), writes=[b_pos], chan=c_pos)
        fw.op("pool", lambda e: e.iota(idx[R, :], pattern=[[0, 1]], base=0, channel_multiplier=1), writes=[b_t])
        fw.op("dve", lambda e: e.tensor_copy(cf[R, :], idx[R, :]), reads=[b_t], writes=[b_t])
        fw.op("dve", lambda e: e.tensor_copy(invf[R, :], cf[R, :]), writes=[b_t])
        for thr in (16.0, 32.0, 48.0):
            fw.op("dve", lambda e, thr=thr: e.tensor_scalar(s1[R, :], cf[R, :], thr, 16.0, op0=ALU.is_ge, op1=ALU.mult), writes=[b_t])
            fw.op("dve", lambda e: e.tensor_tensor(out=invf[R, :], in0=invf[R, :], in1=s1[R, :], op=ALU.subtract), writes=[b_t])
        fw.op("act", lambda e: e.activation(invf[R, :], invf[R, :], AF.Exp, scale=-math.log(10000.0) / 16.0), writes=[b_t])
        fw.op("dve", lambda e: e.memset(off[64:96, :], 0.5 * math.pi), writes=[b_t])
        fw.op("dve", lambda e: e.memset(off[96:128, :], 0.0), writes=[b_t])
        fw.op("dve", lambda e: e.memset(off[96:112, :], math.pi), writes=[b_t])
        fw.op("dve", lambda e: e.tensor_copy(ang[R, :], posi[R, :]), reads=[b_pos], writes=[b_t])
        fw.op("dve", lambda e: e.tensor_scalar(ang[R, :], ang[R, :], invf[R, 0:1], off[R, 0:1], op0=ALU.mult, op1=ALU.add), writes=[b_t])
        fw.op("dve", lambda e: e.tensor_scalar(tq[R, :], ang[R, :], 1.0 / TWO_PI, None, op0=ALU.mult), writes=[b_t])
        fw.op("dve", lambda e: e.tensor_copy(posi[R, :], tq[R, :]), writes=[b_t])
        fw.op("dve", lambda e: e.tensor_copy(tq[R, :], posi[R, :]), writes=[b_t])
        fw.op("dve", lambda e: e.scalar_tensor_tensor(out=ang[R, :], in0=tq[R, :], scalar=-TWO_PI, in1=ang[R, :], op0=ALU.mult, op1=ALU.add), writes=[b_t])
        fw.op("dve", lambda e: e.tensor_scalar(tq[R, :], ang[R, :], math.pi, TWO_PI, op0=ALU.is_ge, op1=ALU.mult), writes=[b_t])
        fw.op("dve", lambda e: e.tensor_tensor(out=ang[R, :], in0=ang[R, :], in1=tq[R, :], op=ALU.subtract), writes=[b_t])
        fw.op("dve", lambda e: e.tensor_scalar(tq[R, :], ang[R, :], -math.pi, TWO_PI, op0=ALU.is_lt, op1=ALU.mult), writes=[b_t])
        fw.op("dve", lambda e: e.tensor_tensor(out=ang[R, :], in0=ang[R, :], in1=tq[R, :], op=ALU.add), writes=[b_t])
        fw.op("act", lambda e: e.activation(TAB[R, :], ang[R, :], AF.Sin), reads=[b_t, bc], writes=[self.b_TAB])
        fw.barrier()
        A.release(m)

    def dump_dbg(self):
        if not self.dbg:
            return
        d = self.dbg[self.dbg_i]
        self.dbg_i += 1
        self.fw.op("sp", lambda e: e.dma_start(out=d[:, :, :], in_=self.xT_d[:, :, :]), reads=self.b_xd, writes=[self.b_outd], chan=self.c_outd)
        self.fw.barrier()

    def sdump(self, name, ap, buf):
        if not self.cfg.get("sdump"):
            return
        shp = list(ap.shape)
        d = self.nc.dram_tensor("sd_" + name, shp, ap.dtype, kind="ExternalOutput").ap()
        self.fw.barrier()
        self.fw.op("sp", lambda e: e.dma_start(out=d, in_=ap), reads=[buf], writes=[self.b_outd], chan=self.c_outd)
        self.fw.barrier()

    def gcol(self, row):
        return lambda c: self.gains[:, c, row:row + 1]

    def prologue(self):
        fw, A = self.fw, self.arena
        A.release(self._const_mark)
        m = A.mark()
        xin = [A.alloc(4 * D, F32).rearrange("p (t d) -> p t d", t=4) for _ in range(2)]
        xb = [A.alloc(NCH * BT, F32).rearrange("p (c t) -> p c t", c=NCH) for _ in range(2)]
        b_xin = [Buf("xin0"), Buf("xin1")]
        c_xin = [fw.new_chan(), fw.new_chan()]
        b_xb = [Buf("pxb0"), Buf("pxb1")]
        for blk in range(NBLK):
            s = blk % 2
            fw.op("sp", lambda e, s=s, blk=blk: e.dma_start(out=xin[s], in_=self.x_in[blk * BT:(blk + 1) * BT, :].rearrange("(t p) d -> p t d", p=128)),
                  writes=[b_xin[s]], chan=c_xin[s])
            for c in range(NCH):
                ps, b_ps = self.next_ps()
                fns = [lambda e, tt=tt, c=c, s=s, ps=ps: e.transpose(ps[:, tt * 128:(tt + 1) * 128], xin[s][:, tt, c * 128:(c + 1) * 128], self.ident[:])
                       for tt in range(4)]
                fw.pe_group(fns, reads=[b_xin[s], self.b_const], writes=[b_ps])
                eng = "dve" if c % 2 == 0 else "act"
                if eng == "dve":
                    fw.op("dve", lambda e, c=c, s=s, ps=ps: e.tensor_copy(xb[s][:, c, :], ps[:]), writes=[b_xb[s], b_ps])
                else:
                    fw.op("act", lambda e, c=c, s=s, ps=ps: e.copy(xb[s][:, c, :], ps[:]), writes=[b_xb[s], b_ps])
            self.store_xblk(xb[s], b_xb[s], blk)
        A.release(m)

    def xd_view(self, blk):
        return self.xT_d[:, :, blk * BT:(blk + 1) * BT].rearrange("c p t -> p c t")

    def store_xblk(self, xb, b_xb, blk):
        self.fw.op("sp", lambda e: e.dma_start(out=self.xd_view(blk), in_=xb), reads=[b_xb], writes=[self.b_xd[blk]], chan=self.c_xd[blk])

    def load_xblk(self, xb, b_xb, c_xb, blk):
        self.fw.op("sp", lambda e: e.dma_start(out=xb, in_=self.xd_view(blk)), reads=[self.b_xd[blk]], writes=[b_xb], chan=c_xb)

    def dump_xT(self):
        fw, A = self.fw, self.arena
        m = A.mark()
        xb = A.alloc(NCH * BT, F32).rearrange("p (c t) -> p c t", c=NCH)
        b_xb, c_xb = Buf("dxb"), fw.new_chan()
        for blk in range(NBLK):
            self.load_xblk(xb, b_xb, c_xb, blk)
            self.emit_output(xb, b_xb, blk)
        A.release(m)

    def emit_output(self, yb, b_yb, blk, ost=None, b_ost=None):
        fw, A = self.fw, self.arena
        m = A.mark()
        if ost is None:
            ost = A.alloc(4 * D, F32).rearrange("p (t d) -> p t d", t=4)
            b_ost = self.b_ost if hasattr(self, "b_ost") else Buf("ost")
            self.b_ost = b_ost
        for tt in range(4):
            for half in range(2):
                ps, b_ps = self.next_ps()
                fns = [lambda e, tt=tt, c=c, ps=ps: e.transpose(ps[:, (c % 4) * 128:(c % 4 + 1) * 128], yb[:, c, tt * 128:(tt + 1) * 128], self.ident[:])
                       for c in range(half * 4, half * 4 + 4)]
                fw.pe_group(fns, reads=[b_yb, self.b_const], writes=[b_ps])
                if (tt + half) % 2 == 0:
                    fw.op("dve", lambda e, tt=tt, half=half, ps=ps: e.tensor_copy(ost[:, tt, half * 512:(half + 1) * 512], ps[:]), writes=[b_ost, b_ps])
                else:
                    fw.op("act", lambda e, tt=tt, half=half, ps=ps: e.copy(ost[:, tt, half * 512:(half + 1) * 512], ps[:]), writes=[b_ost, b_ps])
        fw.op("sp", lambda e: e.dma_start(out=self.out[blk * BT:(blk + 1) * BT, :].rearrange("(t p) d -> p t d", p=128), in_=ost),
              reads=[b_ost], writes=[self.b_outd], chan=self.c_outd)
        A.release(m)

    def ffn_phase(self, l, final=False):
        fw, A = self.fw, self.arena
        if final:
            self._emitted_final = True
        m = A.mark()
        wup = A.alloc(NCH * DFF).rearrange("p (c n) -> p c n", c=NCH)
        wdn = A.alloc(32 * D).rearrange("p (f n) -> p f n", f=32)
        NP = 4
        b_wup = [Buf(f"wup{i}") for i in range(NP)]
        b_wdn = [Buf(f"wdn{i}") for i in range(NP)]
        if not hasattr(self, "c_wup"):
            self.c_wup = [fw.new_chan() for _ in range(NP)]
            self.c_wdn = [fw.new_chan() for _ in range(NP)]
        wu_d = self.W[f"l{l}_w_up"].rearrange("(c p) n -> p c n", p=128)
        wd_d = self.W[f"l{l}_w_down"].rearrange("(f p) n -> p f n", p=128)
        for i in range(NP):
            fw.op("pool", lambda e, i=i: e.dma_start(out=wup[:, :, i * 1024:(i + 1) * 1024], in_=wu_d[:, :, i * 1024:(i + 1) * 1024]),
                  writes=[b_wup[i]], chan=self.c_wup[i])
        for i in range(NP):
            fw.op("pool", lambda e, i=i: e.dma_start(out=wdn[:, i * 8:(i + 1) * 8, :], in_=wd_d[:, i * 8:(i + 1) * 8, :]),
                  writes=[b_wdn[i]], chan=self.c_wdn[i])
        xb = A.alloc(NCH * BT, F32).rearrange("p (c t) -> p c t", c=NCH)
        b_xb = Buf("fxb")
        if not hasattr(self, "c_fxb"):
            self.c_fxb = fw.new_chan()
        hT = A.alloc(NCH * BT).rearrange("p (c t) -> p c t", c=NCH)
        b_hT = Buf("hT")
        aT_raw = A.alloc(32 * BT)
        aT = aT_raw.rearrange("p (f t) -> p f t", f=32)
        b_aT = Buf("aT")
        rstd = A.alloc(BT, F32)
        b_rstd = Buf("rstd")
        NR = 2
        rbuf = [A.alloc(BT, F32) for _ in range(NR)]
        b_rbuf = [Buf(f"r{i}") for i in range(NR)]
        sq = aT_raw[:, 0:NCH * BT].rearrange("p (c t) -> p c t", c=NCH)
        yb = aT_raw[:, 0:2 * NCH * BT].bitcast(F32).rearrange("p (c t) -> p c t", c=NCH)
        ost = aT_raw[:, 2 * NCH * BT:4 * NCH * BT].bitcast(F32).rearrange("p (t d) -> p t d", t=4)
        print(f"[ffn {l}] arena peak {A.peak * 2 / 1024:.1f} KiB")
        ri = 0
        for blk in range(NBLK):
            self.load_xblk(xb, b_xb, self.c_fxb, blk)
            self.rms_feature_major(xb, NCH, BT, self.gcol(4 + l), lambda c: hT[:, c, :], sq, b_aT, rstd, b_rstd, b_xb, b_hT, D)
            for f in range(32):
                ps, b_ps = self.next_ps()
                self.mm_group(ps[:], [(wup[:, c, f * 128:(f + 1) * 128], hT[:, c, :]) for c in range(NCH)], [b_hT, b_wup[f // 8]], b_ps)
                r, b_r = rbuf[ri % NR], b_rbuf[ri % NR]
                ri += 1
                fw.op("act", lambda e, ps=ps, r=r: e.activation(r, ps[:], AF.Relu), writes=[b_r, b_ps])
                fw.op("dve", lambda e, f=f, r=r: e.tensor_tensor(out=aT[:, f, :], in0=r, in1=r, op=ALU.mult), reads=[b_r], writes=[b_aT])
            for c in range(NCH):
                ps, b_ps = self.next_ps()
                self.mm_group(ps[:], [(wdn[:, f, c * 128:(c + 1) * 128], aT[:, f, :]) for f in range(32)], [b_aT] + b_wdn, b_ps)
                fw.op("dve", lambda e, c=c, ps=ps: e.tensor_tensor(out=xb[:, c, :], in0=xb[:, c, :], in1=ps[:], op=ALU.add), writes=[b_xb, b_ps])
            if final:
                sq2 = hT
                fw.op("act", lambda e: e.activation(sq2, xb, AF.Square), reads=[b_xb], writes=[b_hT])
                ps, b_ps = self.next_ps()
                self.mm_group(ps[:], [(self.ones[:], sq2[:, c, :]) for c in range(NCH)], [b_hT, self.b_const], b_ps)
                fw.op("act", lambda e, ps=ps: e.activation(rstd, ps[:], AF.Sqrt, scale=1.0 / D, bias=self.eps_col[:, 0:1]), writes=[b_rstd, b_ps])
                fw.op("dve", lambda e: e.reciprocal(rstd, rstd), writes=[b_rstd])
                for c in range(NCH):
                    fw.op("dve", lambda e, c=c: e.scalar_tensor_tensor(out=yb[:, c, :], in0=xb[:, c, :], scalar=self.gains[:, c, 8:9], in1=rstd,
                                                                       op0=ALU.mult, op1=ALU.mult),
                          reads=[b_xb, b_rstd, self.b_const], writes=[b_aT])
                self.emit_output(yb, b_aT, blk, ost, b_aT)
            else:
                self.store_xblk(xb, b_xb, blk)
        A.release(m)

    def stage_c(self, l, oT, b_oT, wo_name):
        fw, A = self.fw, self.arena
        m = A.mark()
        wo = A.alloc(NCH * D).rearrange("p (c n) -> p c n", c=NCH)
        b_wo = Buf("wo")
        if not hasattr(self, "c_wo"):
            self.c_wo = fw.new_chan()
        self.load_weight(wo, self.W[f"l{l}_{wo_name}"], b_wo, self.c_wo)
        xbs = [A.alloc(NCH * BT, F32).rearrange("p (c t) -> p c t", c=NCH) for _ in range(2)]
        b_xbs = [Buf("cxb0"), Buf("cxb1")]
        if not hasattr(self, "c_cxb"):
            self.c_cxb = [fw.new_chan(), fw.new_chan()]
        for blk in range(NBLK):
            s = blk % 2
            xb, b_xb = xbs[s], b_xbs[s]
            self.load_xblk(xb, b_xb, self.c_cxb[s], blk)
            for c in range(NCH):
                ps, b_ps = self.next_ps()
                self.mm_group(ps[:], [(wo[:, k, c * 128:(c + 1) * 128], oT[:, k, blk * BT:(blk + 1) * BT]) for k in range(NCH)], [b_oT, b_wo], b_ps)
                fw.op("dve", lambda e, c=c, ps=ps, xb=xb: e.tensor_tensor(out=xb[:, c, :], in0=xb[:, c, :], in1=ps[:], op=ALU.add), writes=[b_xb, b_ps])
            self.store_xblk(xb, b_xb, blk)
        A.release(m)

    def mla_phase(self, l):
        fw, A = self.fw, self.arena
        m0 = A.mark()
        gq_row = 13 if l == 0 else 14
        gkv_row = 15 if l == 0 else 16
        b_oT = Buf("oT")
        mAB = A.mark()
        self.build_tables()
        TAB = self.TAB
        cqT = A.alloc(3 * T).rearrange("p (c t) -> p c t", c=3)
        ckvT = A.alloc(2 * T).rearrange("p (c t) -> p c t", c=2)
        b_cq, b_ckv = Buf("cqT"), Buf("ckvT")
        kh = [A.alloc(T) for _ in range(2)]
        b_khr = [Buf("khr0"), Buf("khr1")]
        b_khn = [Buf("khn0"), Buf("khn1")]
        mA = A.mark()
        wdq = A.alloc(NCH * 384).rearrange("p (c n) -> p c n", c=NCH)
        wdkv = A.alloc(NCH * 320).rearrange("p (c n) -> p c n", c=NCH)
        b_wdq, b_wdkv = Buf("wdq"), Buf("wdkv")
        if not hasattr(self, "c_mla_w"):
            self.c_mla_w = [fw.new_chan() for _ in range(6)]
        cw = self.c_mla_w
        self.load_weight(wdq, self.W[f"l{l}_w_dq"], b_wdq, cw[0])
        wdkv_d = self.W[f"l{l}_w_dkv"].rearrange("(c p) n -> p c n", p=128)
        fw.op("pool", lambda e: e.dma_start(out=wdkv[:, :, 0:288], in_=wdkv_d), writes=[b_wdkv], chan=cw[1])
        fw.op("pool", lambda e: e.dma_start(out=wdkv[:, :, 288:304], in_=wdkv_d[:, :, 272:288]), writes=[b_wdkv], chan=cw[1])
        fw.op("pool", lambda e: e.dma_start(out=wdkv[:, :, 304:320], in_=wdkv_d[:, :, 256:272]), writes=[b_wdkv], chan=cw[1])
        xbs = [A.alloc(NCH * BT, F32).rearrange("p (c t) -> p c t", c=NCH) for _ in range(2)]
        b_xbs = [Buf("axb0"), Buf("axb1")]
        if not hasattr(self, "c_axb"):
            self.c_axb = [fw.new_chan(), fw.new_chan()]
        hT = A.alloc(NCH * BT).rearrange("p (c t) -> p c t", c=NCH)
        sq = A.alloc(NCH * BT).rearrange("p (c t) -> p c t", c=NCH)
        raw = A.alloc(3 * BT, F32).rearrange("p (c t) -> p c t", c=3)
        raw2 = A.alloc(2 * BT, F32).rearrange("p (c t) -> p c t", c=2)
        rstd = A.alloc(BT, F32)
        rstd2 = A.alloc(BT, F32)
        rstd3 = A.alloc(BT, F32)
        t1 = A.alloc(BT, F32)
        t2 = A.alloc(BT, F32)
        b_hT, b_sq, b_raw, b_raw2 = Buf("hT"), Buf("sq"), Buf("raw"), Buf("raw2")
        b_rstd, b_rstd2, b_rstd3, b_t1, b_t2 = Buf("rstd"), Buf("rstd2"), Buf("rstd3"), Buf("t1"), Buf("t2")
        sq_b, sq_c = Buf("sqb"), Buf("sqc")
        sqq = A.alloc(3 * BT).rearrange("p (c t) -> p c t", c=3)
        sqk = A.alloc(2 * BT).rearrange("p (c t) -> p c t", c=2)
        print(f"[mla {l} A] arena peak {A.peak * 2 / 1024:.1f} KiB")
        for blk in range(NBLK):
            s = blk % 2
            xb, b_xb = xbs[s], b_xbs[s]
            cols = slice(blk * BT, (blk + 1) * BT)
            self.load_xblk(xb, b_xb, self.c_axb[s], blk)
            self.rms_feature_major(xb, NCH, BT, self.gcol(l), lambda c: hT[:, c, :], sq, b_sq, rstd, b_rstd, b_xb, b_hT, D)
            for mch in range(3):
                ps, b_ps = self.next_ps()
                self.mm_group(ps[:], [(wdq[:, c, mch * 128:(mch + 1) * 128], hT[:, c, :]) for c in range(NCH)], [b_hT, b_wdq], b_ps)
                fw.op("act", lambda e, mch=mch, ps=ps: e.copy(raw[:, mch, :], ps[:]), writes=[b_raw, b_ps])
            self.rms_feature_major(raw, 3, BT, self.gcol(gq_row), lambda c: cqT[:, c, cols], sqq, sq_b, rstd2, b_rstd2, b_raw, b_cq, 384)
            for mch in range(2):
                ps, b_ps = self.next_ps()
                self.mm_group(ps[:], [(wdkv[:, c, mch * 128:(mch + 1) * 128], hT[:, c, :]) for c in range(NCH)], [b_hT, b_wdkv], b_ps)
                fw.op("act", lambda e, mch=mch, ps=ps: e.copy(raw2[:, mch, :], ps[:]), writes=[b_raw2, b_ps])
            self.rms_feature_major(raw2, 2, BT, self.gcol(gkv_row), lambda c: ckvT[:, c, cols], sqk, sq_c, rstd3, b_rstd3, b_raw2, b_ckv, 256)
            ps, b_ps = self.next_ps()
            self.mm_group(ps[:], [(wdkv[:, c, 192:320], hT[:, c, :]) for c in range(NCH)], [b_hT, b_wdkv], b_ps)
            fw.op("dve", lambda e, ps=ps, cols=cols: e.tensor_tensor(out=t1[64:96, :], in0=ps[64:96, :], in1=TAB[64:96, cols], op=ALU.mult),
                  reads=[self.b_TAB], writes=[b_t1, b_ps])
            fw.op("dve", lambda e, ps=ps, cols=cols: e.tensor_tensor(out=t2[64:96, :], in0=ps[96:128, :], in1=TAB[96:128, cols], op=ALU.mult),
                  reads=[self.b_TAB], writes=[b_t2, b_ps])
            fw.op("dve", lambda e, cols=cols: e.tensor_tensor(out=kh[0][64:96, cols], in0=t1[64:96, :], in1=t2[64:96, :], op=ALU.add),
                  reads=[b_t1, b_t2], writes=[b_khr[0]])
            fw.op("pool", lambda e, cols=cols: e.tensor_copy(kh[1][64:96, cols], kh[0][64:96, cols]), reads=[b_khr[0]], writes=[b_khr[1]])
        self.sdump("TAB", TAB[64:128, :], self.b_TAB)
        self.sdump("cqT", cqT.rearrange("p c t -> p (c t)"), b_cq)
        self.sdump("ckvT", ckvT.rearrange("p c t -> p (c t)"), b_ckv)
        self.sdump("krope", kh[0][64:96, :], b_khr[0])
        self.sdump("t1", t1[64:96, :], b_t1)
        self.sdump("t2", t2[64:96, :], b_t2)
        self.sdump("raw", raw.rearrange("p c t -> p (c t)"), b_raw)
        self.sdump("rstd2", rstd2, b_rstd2)
        self.sdump("wdkv", wdkv.rearrange("p c n -> p (c n)"), b_wdkv)
        if self.cfg.get("stopA"):
            A.release_top()
            A.release(m0)
            return
        fw.barrier()
        A.release(mA)
        oT = A.alloc_top(NCH * T).rearrange("p (c t) -> p c t", c=NCH)
        wq = A.alloc(3 * 16 * 128).rearrange("p (c h e) -> p c h e", c=3, h=16)
        wuk = A.alloc(2 * 1024).rearrange("p (c n) -> p c n", c=2)
        wuv = A.alloc(2 * 1024).rearrange("p (c n) -> p c n", c=2)
        b_wq, b_wuk, b_wuv = Buf("wq"), Buf("wuk"), Buf("wuv")
        wuq_d = self.W[f"l{l}_w_uq"].rearrange("(c p) (h e) -> p c h e", p=128, e=96)
        for c3 in range(3):
            fw.op("pool", lambda e, c3=c3: e.dma_start(out=wq[:, c3, :, 0:96], in_=wuq_d[:, c3, :, :]), writes=[b_wq], chan=cw[2])
            fw.op("pool", lambda e, c3=c3: e.dma_start(out=wq[:, c3, :, 96:112], in_=wuq_d[:, c3, :, 80:96]), writes=[b_wq], chan=cw[2])
            fw.op("pool", lambda e, c3=c3: e.dma_start(out=wq[:, c3, :, 112:128], in_=wuq_d[:, c3, :, 64:80]), writes=[b_wq], chan=cw[2])
        self.load_weight(wuk, self.W[f"l{l}_w_uk"], b_wuk, cw[3])
        self.load_weight(wuv, self.W[f"l{l}_w_uv"], b_wuv, cw[4])
        qh = [A.alloc(T) for _ in range(2)]
        vh = [A.alloc(32 * 128).rearrange("p (j e) -> p j e", j=32) for _ in range(2)]
        b_qh = [Buf("qh0"), Buf("qh1")]
        b_vh = [Buf("vh0"), Buf("vh1")]
        NPT = 4
        pts = [A.alloc(BT) for _ in range(NPT)]
        b_pts = [Buf(f"pt{i}") for i in range(NPT)]
        rec = A.alloc(BT, F32)
        b_rec = Buf("rec")
        qt1 = A.alloc(BT, F32)
        qt2 = A.alloc(BT, F32)
        b_qt1, b_qt2 = Buf("qt1"), Buf("qt2")
        print(f"[mla {l} B] arena peak {A.peak * 2 / 1024:.1f} KiB")
        for i in range(2):
            fw.op("pool", lambda e, i=i: e.memset(vh[i][:, :, 64:128], 1.0), writes=[b_vh[i]])
        scale = 1.0 / math.sqrt(96.0)
        heads = self.cfg.get("heads", list(range(16)))
        def proj_tasks(hi, h):
            s = hi % 2
            tasks = []
            for tb in range(NBLK):
                def tq(tb=tb, s=s, h=h):
                    cols = slice(tb * BT, (tb + 1) * BT)
                    ps, b_ps = self.next_ps()
                    self.mm_group(ps[:], [(wq[:, k, h, :], cqT[:, k, cols]) for k in range(3)], [b_cq, b_wq], b_ps)
                    fw.op("dve", lambda e, ps=ps, s=s, cols=cols: e.tensor_copy(qh[s][0:64, cols], ps[0:64, :]), writes=[b_qh[s], b_ps])
                    fw.op("dve", lambda e, ps=ps, cols=cols: e.tensor_tensor(out=qt1[64:96, :], in0=ps[64:96, :], in1=TAB[64:96, cols], op=ALU.mult),
                          reads=[self.b_TAB], writes=[b_qt1, b_ps])
                    fw.op("dve", lambda e, ps=ps, cols=cols: e.tensor_tensor(out=qt2[64:96, :], in0=ps[96:128, :], in1=TAB[96:128, cols], op=ALU.mult),
                          reads=[self.b_TAB], writes=[b_qt2, b_ps])
                    fw.op("pool", lambda e, s=s, cols=cols: e.tensor_tensor(out=qh[s][64:96, cols], in0=qt1[64:96, :], in1=qt2[64:96, :], op=ALU.add),
                          reads=[b_qt1, b_qt2], writes=[b_qh[s]])
                tasks.append(tq)

                def tk(tb=tb, s=s, h=h):
                    cols = slice(tb * BT, (tb + 1) * BT)
                    ps, b_ps = self.next_ps()
                    self.mm_group(ps[0:64, :], [(wuk[:, k, h * 64:(h + 1) * 64], ckvT[:, k, cols]) for k in range(2)], [b_ckv, b_wuk], b_ps)
                    fw.op("dve", lambda e, ps=ps, s=s, cols=cols: e.tensor_copy(kh[s][0:64, cols], ps[0:64, :]), writes=[b_khn[s], b_ps])
                tasks.append(tk)
            for j0 in range(0, 32, 8):
                def tv(j0=j0, s=s, h=h):
                    ps, b_ps = self.next_ps()
                    fns = []
                    for jj in range(8):
                        j = j0 + jj
                        for k in range(2):
                            fns.append(lambda e, ps=ps, jj=jj, j=j, k=k, h=h: e.matmul(ps[:, jj * 64:(jj + 1) * 64], lhsT=ckvT[:, k, j * 128:(j + 1) * 128],
                                                                                 rhs=wuv[:, k, h * 64:(h + 1) * 64], start=(k == 0), stop=(k == 1)))
                    fw.pe_group(fns, reads=[b_ckv, b_wuv], writes=[b_ps])
                    fw.op("dve", lambda e, ps=ps, s=s, j0=j0: e.tensor_copy(vh[s][:, j0:j0 + 8, 0:64], ps[:].rearrange("p (j e) -> p j e", j=8)), writes=[b_vh[s], b_ps])
                tasks.append(tv)
            return tasks

        tiles = []
        for qb in range(NBLK):
            for kc in range(4 * qb + 4):
                tiles.append((qb, kc))
        NT = len(tiles)
        for t in proj_tasks(0, heads[0]):
            t()
        for hi, h in enumerate(heads):
            s = hi % 2
            extra = proj_tasks(hi + 1, heads[hi + 1]) if hi + 1 < len(heads) else []
            st = {}

            def S_(i, s=s):
                qb, kc = tiles[i]
                nq0 = max(0, kc - 4 * qb) * 128
                ps, b_ps = self.next_ps()
                st[i] = [ps, b_ps, None, None]
                self.mm_group(ps[:, nq0:BT], [(kh[s][0:96, kc * 128:(kc + 1) * 128], qh[s][0:96, qb * BT + nq0:(qb + 1) * BT])],
                              [b_khr[s], b_khn[s], b_qh[s]], b_ps)

            def E_(i, s=s):
                qb, kc = tiles[i]
                nq0 = max(0, kc - 4 * qb) * 128
                ps, b_ps = st[i][0], st[i][1]
                pt, b_pt = pts[i % NPT], b_pts[i % NPT]
                st[i][2], st[i][3] = pt, b_pt
                fw.op("act", lambda e, ps=ps, pt=pt, nq0=nq0: e.activation(pt[:, nq0:BT], ps[:, nq0:BT], AF.Exp, scale=scale), writes=[b_pt, b_ps])
                if kc >= 4 * qb:
                    fw.op("pool", lambda e, pt=pt, nq0=nq0: e.tensor_tensor(out=pt[:, nq0:nq0 + 128], in0=pt[:, nq0:nq0 + 128], in1=self.triI[:], op=ALU.mult),
                          reads=[self.b_const], writes=[b_pt])

            cur = {}

            def P_(i, s=s, h=h):
                qb, kc = tiles[i]
                nq0 = max(0, kc - 4 * qb) * 128
                last = 4 * qb + 3
                pt, b_pt = st[i][2], st[i][3]
                del st[i]
                if kc == 0:
                    cur[qb] = self.next_acc()
                po, b_po = cur[qb]
                fw.pe_group([lambda e, po=po, pt=pt, nq0=nq0, kc=kc, last=last, s=s: e.matmul(po[:, nq0:BT], lhsT=vh[s][:, kc, :], rhs=pt[:, nq0:BT],
                                                                                              start=(kc == 0), stop=(kc == last))],
                            reads=[b_vh[s], b_pt], writes=[b_po])
                if kc == last:
                    qc = slice(qb * BT, (qb + 1) * BT)
                    fw.op("dve", lambda e, po=po: e.reciprocal(rec[0:64, :], po[64:128, :]), writes=[b_rec, b_po])
                    fw.op("dve", lambda e, po=po, h=h, qc=qc: e.tensor_tensor(out=oT[(h % 2) * 64:(h % 2) * 64 + 64, h // 2, qc], in0=po[0:64, :],
                                                                              in1=rec[0:64, :], op=ALU.mult),
                          reads=[b_rec], writes=[b_oT, b_po])

            self.run_pipeline(NT, [S_, E_, P_], [0, 2, 3], extra)
        self.sdump("qh", qh[(len(heads) - 1) % 2][0:96, :], b_qh[(len(heads) - 1) % 2])
        self.sdump("khn", kh[(len(heads) - 1) % 2][0:64, :], b_khn[(len(heads) - 1) % 2])
        self.sdump("vh", vh[(len(heads) - 1) % 2].rearrange("p j e -> p (j e)"), b_vh[(len(heads) - 1) % 2])
        self.sdump("oT", oT.rearrange("p c t -> p (c t)"), b_oT)
        A.release(mAB)
        fw.barrier()
        self.stage_c(l, oT, b_oT, "w_o")
        A.release_top()
        A.release(m0)

    def conv_phase(self, l):
        fw, A = self.fw, self.arena
        m0 = A.mark()
        win = A.alloc(NCH * 3072).rearrange("p (c n) -> p c n", c=NCH)
        wout = A.alloc(NCH * D).rearrange("p (c n) -> p c n", c=NCH)
        b_win = [Buf(f"win{i}") for i in range(3)]
        b_wout = Buf("wout")
        if not hasattr(self, "c_conv_w"):
            self.c_conv_w = [fw.new_chan() for _ in range(4)]
        cw = self.c_conv_w
        win_d = self.W[f"l{l}_w_in"].rearrange("(c p) n -> p c n", p=128)
        for i in range(3):
            fw.op("pool", lambda e, i=i: e.dma_start(out=win[:, :, i * 1024:(i + 1) * 1024], in_=win_d[:, :, i * 1024:(i + 1) * 1024]),
                  writes=[b_win[i]], chan=cw[i])
        self.load_weight(wout, self.W[f"l{l}_w_out"], b_wout, cw[3])
        xbs = [A.alloc(NCH * BT, F32).rearrange("p (c t) -> p c t", c=NCH) for _ in range(2)]
        b_xbs = [Buf("vxb0"), Buf("vxb1")]
        if not hasattr(self, "c_vxb"):
            self.c_vxb = [fw.new_chan(), fw.new_chan()]
        hT = A.alloc(NCH * BT).rearrange("p (c t) -> p c t", c=NCH)
        sq = A.alloc(NCH * BT).rearrange("p (c t) -> p c t", c=NCH)
        zT = A.alloc(NCH * BT).rearrange("p (c t) -> p c t", c=NCH)
        ucur = A.alloc(NCH * (BT + 2), F32).rearrange("p (c t) -> p c t", c=NCH)
        rstd = A.alloc(BT, F32)
        NTM = 2
        tmpc = [A.alloc(BT, F32) for _ in range(NTM)]
        acc = [A.alloc(BT, F32) for _ in range(NTM)]
        b_hT, b_sq, b_zT, b_rstd = Buf("hT"), Buf("sq"), Buf("zT"), Buf("rstd")
        b_u = [Buf(f"u{c}") for c in range(NCH)]
        b_tmpc = [Buf(f"tc{i}") for i in range(NTM)]
        b_acc = [Buf(f"ac{i}") for i in range(NTM)]
        print(f"[conv {l}] arena peak {A.peak * 2 / 1024:.1f} KiB")
        fw.op("pool", lambda e: e.memset(ucur.rearrange("p c t -> p (c t)"), 0.0), writes=b_u)
        ti = 0
        for blk in range(NBLK):
            s = blk % 2
            xb, b_xb = xbs[s], b_xbs[s]
            self.load_xblk(xb, b_xb, self.c_vxb[s], blk)
            self.rms_feature_major(xb, NCH, BT, self.gcol(l), lambda c: hT[:, c, :], sq, b_sq, rstd, b_rstd, b_xb, b_hT, D)
            for c in range(NCH):
                psB, b_psB = self.next_ps()
                self.mm_group(psB[:], [(win[:, k, c * 128:(c + 1) * 128], hT[:, k, :]) for k in range(NCH)], [b_hT, b_win[0]], b_psB)
                psC, b_psC = self.next_ps()
                self.mm_group(psC[:], [(win[:, k, 1024 + c * 128:1024 + (c + 1) * 128], hT[:, k, :]) for k in range(NCH)], [b_hT, b_win[1]], b_psC)
                psU, b_psU = self.next_ps()
                self.mm_group(psU[:], [(win[:, k, 2048 + c * 128:2048 + (c + 1) * 128], hT[:, k, :]) for k in range(NCH)], [b_hT, b_win[2]], b_psU)
                tc_, b_tc = tmpc[ti % NTM], b_tmpc[ti % NTM]
                ac_, b_ac = acc[ti % NTM], b_acc[ti % NTM]
                ti += 1
                fw.op("act", lambda e, psC=psC, tc_=tc_: e.copy(tc_, psC[:]), writes=[b_tc, b_psC])
                if blk > 0:
                    fw.op("pool", lambda e, c=c: e.tensor_copy(ucur[:, c, 0:2], ucur[:, c, BT:BT + 2]), writes=[b_u[c]])
                fw.op("dve", lambda e, c=c, psU=psU, tc_=tc_: e.tensor_tensor(out=ucur[:, c, 2:BT + 2], in0=psU[:], in1=tc_, op=ALU.mult),
                      reads=[b_tc], writes=[b_u[c], b_psU])
                fw.op("dve", lambda e, c=c, ac_=ac_: e.tensor_scalar(ac_, ucur[:, c, 2:BT + 2], self.gains[:, c, 11:12], self.gains[:, c, 12:13],
                                                                     op0=ALU.mult, op1=ALU.add),
                      reads=[b_u[c], self.b_const], writes=[b_ac])
                fw.op("dve", lambda e, c=c, ac_=ac_: e.scalar_tensor_tensor(out=ac_, in0=ucur[:, c, 1:BT + 1], scalar=self.gains[:, c, 10:11], in1=ac_,
                                                                             op0=ALU.mult, op1=ALU.add),
                      reads=[b_u[c], self.b_const], writes=[b_ac])
                fw.op("dve", lambda e, c=c, ac_=ac_: e.scalar_tensor_tensor(out=ac_, in0=ucur[:, c, 0:BT], scalar=self.gains[:, c, 9:10], in1=ac_,
                                                                             op0=ALU.mult, op1=ALU.add),
                      reads=[b_u[c], self.b_const], writes=[b_ac])
                fw.op("dve", lambda e, c=c, ac_=ac_, psB=psB: e.tensor_tensor(out=zT[:, c, :], in0=psB[:], in1=ac_, op=ALU.mult),
                      reads=[b_ac], writes=[b_zT, b_psB])
            for c in range(NCH):
                ps, b_ps = self.next_ps()
                self.mm_group(ps[:], [(wout[:, k, c * 128:(c + 1) * 128], zT[:, k, :]) for k in range(NCH)], [b_zT, b_wout], b_ps)
                fw.op("dve", lambda e, c=c, ps=ps, xb=xb: e.tensor_tensor(out=xb[:, c, :], in0=xb[:, c, :], in1=ps[:], op=ALU.add), writes=[b_xb, b_ps])
            self.store_xblk(xb, b_xb, blk)
        A.release(m0)

    def sb_phase(self, l):
        fw, A = self.fw, self.arena
        m0 = A.mark()
        b_oT = Buf("oT")
        mAB = A.mark()
        hTf = A.alloc(NCH * T).rearrange("p (c t) -> p c t", c=NCH)
        b_hTf = Buf("hTf")
        mA = A.mark()
        xbs = [A.alloc(NCH * BT, F32).rearrange("p (c t) -> p c t", c=NCH) for _ in range(2)]
        b_xbs = [Buf("sxb0"), Buf("sxb1")]
        if not hasattr(self, "c_sxb"):
            self.c_sxb = [fw.new_chan(), fw.new_chan()]
        sq = A.alloc(NCH * BT).rearrange("p (c t) -> p c t", c=NCH)
        rstd = A.alloc(BT, F32)
        b_sq, b_rstd = Buf("sq"), Buf("rstd")
        for blk in range(NBLK):
            s = blk % 2
            cols = slice(blk * BT, (blk + 1) * BT)
            self.load_xblk(xbs[s], b_xbs[s], self.c_sxb[s], blk)
            self.rms_feature_major(xbs[s], NCH, BT, self.gcol(l), lambda c: hTf[:, c, cols], sq, b_sq, rstd, b_rstd, b_xbs[s], b_hTf, D)
        fw.barrier()
        A.release(mA)
        oT = A.alloc_top(NCH * T).rearrange("p (c t) -> p c t", c=NCH)
        wp = [A.alloc(NCH * 3 * 128).rearrange("p (c g e) -> p c g e", c=NCH, g=3) for _ in range(2)]
        b_wp = [Buf("wp0"), Buf("wp1")]
        if not hasattr(self, "c_wp"):
            self.c_wp = [fw.new_chan(), fw.new_chan()]
        qp = [A.alloc(T) for _ in range(2)]
        kp = [A.alloc(T) for _ in range(2)]
        vp = [A.alloc(32 * 128).rearrange("p (j e) -> p j e", j=32) for _ in range(2)]
        b_qp, b_kp, b_vp = [Buf("qp0"), Buf("qp1")], [Buf("kp0"), Buf("kp1")], [Buf("vp0"), Buf("vp1")]
        NE = 2
        Es = [A.alloc(BT, F32) for _ in range(NE)]
        b_Es = [Buf(f"E{i}") for i in range(NE)]
        NS = 3
        ats = [A.alloc(BT) for _ in range(NS)]
        b_ats = [Buf(f"at{i}") for i in range(NS)]
        print(f"[sb {l} B] arena peak {A.peak * 2 / 1024:.1f} KiB")
        wqkv_d = self.W[f"l{l}_w_qkv"].rearrange("(c p) (g n) -> p c g n", p=128, g=3)
        pairs = self.cfg.get("pairs", list(range(8)))
        NRB = 4
        Rs = [A.alloc(BT) for _ in range(NRB)]
        b_Rs = [Buf(f"R{i}") for i in range(NRB)]
        NSP = 4
        sps = [A.alloc(BT) for _ in range(NSP)]
        b_sps = [Buf(f"sp{i}") for i in range(NSP)]

        def load_wp(pi):
            p = pairs[pi]
            s = pi % 2
            for g3 in range(3):
                fw.op("pool", lambda e, s=s, p=p, g3=g3: e.dma_start(out=wp[s][:, :, g3, :], in_=wqkv_d[:, :, g3, p * 128:(p + 1) * 128]),
                      writes=[b_wp[s]], chan=self.c_wp[s])

        def proj_tasks(pi):
            s = pi % 2
            tasks = []
            for tb in range(NBLK):
                for g3, dst, b_dst in ((0, qp, b_qp), (1, kp, b_kp)):
                    def tqk(tb=tb, s=s, g3=g3, dst=dst, b_dst=b_dst):
                        cols = slice(tb * BT, (tb + 1) * BT)
                        ps, b_ps = self.next_ps()
                        self.mm_group(ps[:], [(wp[s][:, c, g3, :], hTf[:, c, cols]) for c in range(NCH)], [b_hTf, b_wp[s]], b_ps)
                        fw.op("dve", lambda e, ps=ps, s=s, cols=cols, dst=dst: e.tensor_copy(dst[s][:, cols], ps[:]), writes=[b_dst[s], b_ps])
                    tasks.append(tqk)
            for j0 in range(0, 32, 4):
                def tv(j0=j0, s=s):
                    ps, b_ps = self.next_ps()
                    fns = []
                    for jj in range(4):
                        j = j0 + jj
                        for c in range(NCH):
                            fns.append(lambda e, ps=ps, jj=jj, j=j, c=c, s=s: e.matmul(ps[:, jj * 128:(jj + 1) * 128], lhsT=hTf[:, c, j * 128:(j + 1) * 128],
                                                                                      rhs=wp[s][:, c, 2, :], start=(c == 0), stop=(c == NCH - 1)))
                    fw.pe_group(fns, reads=[b_hTf, b_wp[s]], writes=[b_ps])
                    fw.op("dve", lambda e, ps=ps, s=s, j0=j0: e.tensor_copy(vp[s][:, j0:j0 + 4, :], ps[:].rearrange("p (j e) -> p j e", j=4)), writes=[b_vp[s], b_ps])
                tasks.append(tv)
            return tasks

        tiles = []
        for hh in range(2):
            for qb in range(NBLK):
                for kc in range(4 * qb + 3, -1, -1):
                    tiles.append((hh, qb, kc))
        NT = len(tiles)
        load_wp(0)
        for t in proj_tasks(0):
            t()
        gi = [0]
        for pi, p in enumerate(pairs):
            s = pi % 2
            extra = []
            if pi + 1 < len(pairs):
                load_wp(pi + 1)
                extra = proj_tasks(pi + 1)
            st = {}
            base = gi[0]

            def info(i):
                hh, qb, kc = tiles[i]
                nq0 = max(0, kc - 4 * qb) * 128
                return hh, qb, kc, nq0, slice(hh * 64, (hh + 1) * 64), slice(kc * 128, (kc + 1) * 128), slice(qb * BT + nq0, (qb + 1) * BT)

            def S1(i, s=s):
                hh, qb, kc, nq0, pr, kcs, qcs = info(i)
                psZ, b_psZ = self.next_ps()
                st[i] = {"psZ": (psZ, b_psZ)}
                self.mm_group(psZ[:, nq0:BT], [(kp[s][pr, kcs], qp[s][pr, qcs])], [b_kp[s], b_qp[s]], b_psZ)

            def S2(i, s=s, base=base):
                hh, qb, kc, nq0, pr, kcs, qcs = info(i)
                g = base + i
                psZ, b_psZ = st[i]["psZ"]
                E, b_E = Es[g % NE], b_Es[g % NE]
                sp, b_sp = sps[g % NSP], b_sps[g % NSP]
                st[i]["sp"] = (sp, b_sp)
                last = 4 * qb + 3
                diag = kc >= 4 * qb
                fw.op("act", lambda e, psZ=psZ, E=E, nq0=nq0: e.activation(E[:, nq0:BT], psZ[:, nq0:BT], AF.Exp, scale=0.125), writes=[b_E, b_psZ])
                fw.op("act", lambda e, E=E, sp=sp, nq0=nq0: e.activation(sp[:, nq0:BT], E[:, nq0:BT], AF.Ln, bias=1.0), reads=[b_E], writes=[b_sp])
                if diag:
                    fw.op("pool", lambda e, sp=sp, nq0=nq0: e.tensor_tensor(out=sp[:, nq0:nq0 + 128], in0=sp[:, nq0:nq0 + 128], in1=self.triS[:], op=ALU.mult),
                          reads=[self.b_const], writes=[b_sp])
                Rin, b_Rin = Rs[g % NRB], b_Rs[g % NRB]
                Rout, b_Rout = Rs[(g + 1) % NRB], b_Rs[(g + 1) % NRB]
                st[i]["Rin"] = (Rin, b_Rin)
                if kc != 0:
                    if kc == last:
                        fw.op("dve", lambda e, Rout=Rout, sp=sp, nq0=nq0: e.tensor_copy(Rout[:, nq0:BT], sp[:, nq0:BT]), reads=[b_sp], writes=[b_Rout])
                    else:
                        fw.op("dve", lambda e, Rout=Rout, Rin=Rin, sp=sp, nq0=nq0: e.tensor_tensor(out=Rout[:, nq0:BT], in0=Rin[:, nq0:BT], in1=sp[:, nq0:BT], op=ALU.add),
                              reads=[b_sp, b_Rin], writes=[b_Rout])
                    if kc > 4 * qb:
                        fw.op("pool", lambda e, Rout=Rout, nq0=nq0: e.memset(Rout[:, nq0 - 128:nq0], 0.0), writes=[b_Rout])

            def S3(i, s=s):
                hh, qb, kc, nq0, pr, kcs, qcs = info(i)
                sp, b_sp = st[i]["sp"]
                Rin, b_Rin = st[i]["Rin"]
                psL, b_psL = self.next_ps()
                st[i]["psL"] = (psL, b_psL)
                prs = [(kp[s][pr, kcs], qp[s][pr, qcs]), (self.Uinc[:], sp[:, nq0:BT])]
                rd = [b_kp[s], b_qp[s], b_sp, self.b_const]
                if kc != 4 * qb + 3:
                    prs.append((self.neg8[:], Rin[:, nq0:BT]))
                    rd.append(b_Rin)
                self.mm_group(psL[:, nq0:BT], prs, rd, b_psL)

            def S4(i, s=s, base=base):
                hh, qb, kc, nq0, pr, kcs, qcs = info(i)
                g = base + i
                psL, b_psL = st[i]["psL"]
                at, b_at = ats[g % NS], b_ats[g % NS]
                st[i]["at"] = (at, b_at)
                fw.op("act", lambda e, psL=psL, at=at, nq0=nq0: e.activation(at[:, nq0:BT], psL[:, nq0:BT], AF.Exp, scale=0.125), writes=[b_at, b_psL])
                if kc >= 4 * qb:
                    fw.op("pool", lambda e, at=at, nq0=nq0: e.tensor_tensor(out=at[:, nq0:nq0 + 128], in0=at[:, nq0:nq0 + 128], in1=self.triS[:], op=ALU.mult),
                          reads=[self.b_const], writes=[b_at])

            cur = {}

            def S5(i, s=s, p=p):
                hh, qb, kc, nq0, pr, kcs, qcs = info(i)
                last = 4 * qb + 3
                at, b_at = st[i]["at"]
                del st[i]
                if kc == last:
                    cur[(hh, qb)] = self.next_acc()
                po, b_po = cur[(hh, qb)]
                fw.pe_group([lambda e, po=po, at=at, nq0=nq0, kc=kc, last=last, s=s: e.matmul(po[:, nq0:BT], lhsT=vp[s][:, kc, :], rhs=at[:, nq0:BT],
                                                                                             start=(kc == last), stop=(kc == 0))],
                            reads=[b_vp[s], b_at], writes=[b_po])
                if kc == 0:
                    qc = slice(qb * BT, (qb + 1) * BT)
                    fw.op("dve", lambda e, po=po, p=p, pr=pr, qc=qc: e.tensor_copy(oT[pr, p, qc], po[pr, :]), writes=[b_oT, b_po])

            self.run_pipeline(NT, [S1, S2, S3, S4, S5], [0, 1, 2, 3, 4], extra)
            gi[0] += NT
        A.release(mAB)
        fw.barrier()
        self.stage_c(l, oT, b_oT, "w_o")
        A.release_top()
        A.release(m0)


_NC_CACHE = {}


def build_nc(cfg=None):
    key = repr(sorted((cfg or {}).items()))
    if key not in _NC_CACHE:
        _NC_CACHE[key] = Prog(cfg).build()
    return _NC_CACHE[key]


def kernel(**inputs):
    nc = build_nc(None)
    x = np.ascontiguousarray(inputs["x"], dtype=np.float32)
    pos = np.ascontiguousarray(inputs["positions"], dtype=np.int32)
    shared = {k: np.ascontiguousarray(v, dtype=np.float32) for k, v in inputs.items() if k not in ("x", "positions")}
    in_maps = []
    for b in range(N_CORES):
        d = dict(shared)
        d["x"] = x[b]
        d["positions"] = pos[b:b + 1]
        in_maps.append(d)
    res = run_bass_kernel_spmd(nc, in_maps, core_ids=list(range(N_CORES)))
    return np.stack([np.asarray(r["out"], dtype=np.float32) for r in res.results], axis=0)
```

```python
import math
from contextlib import ExitStack
import numpy as np
import concourse.bass as bass
import concourse.mybir as mybir
from concourse.bass_utils import run_bass_kernel_spmd

F32 = mybir.dt.float32
BF16 = mybir.dt.bfloat16
I32 = mybir.dt.int32
AF = mybir.ActivationFunctionType
ALU = mybir.AluOpType

T = 4096
D = 1024
BT = 512
NBLK = T // BT
NCH = D // 128
DFF = 4096
EPS = 1e-6
N_CORES = 8

MLA_LAYERS = (0, 3)
WNAMES = {
    0: ["w_dq", "norm_q", "w_uq", "w_dkv", "norm_kv", "w_uk", "w_uv", "w_o"],
    1: ["w_in", "conv_w", "conv_b", "w_out"],
    2: ["w_qkv", "w_o"],
    3: ["w_dq", "norm_q", "w_uq", "w_dkv", "norm_kv", "w_uk", "w_uv", "w_o"],
}
WSHAPES = {
    "w_dq": [1024, 384], "norm_q": [384], "w_uq": [384, 1536], "w_dkv": [1024, 288], "norm_kv": [256],
    "w_uk": [256, 1024], "w_uv": [256, 1024], "w_o": [1024, 1024], "w_in": [1024, 3072],
    "conv_w": [3, 1024], "conv_b": [1024], "w_out": [1024, 1024], "w_qkv": [1024, 3072],
    "norm_mix": [1024], "norm_mlp": [1024], "w_up": [1024, 4096], "w_down": [4096, 1024],
}


class Buf:
    __slots__ = ("name", "w", "r")

    def __init__(self, name):
        self.name = name
        self.w = None
        self.r = {}


class Chan:
    __slots__ = ("sem", "count")

    def __init__(self):
        self.sem = None
        self.count = 0


class Stream:
    def __init__(self, name):
        self.name = name
        self.items = []
        self.nops = 0
        self.seen = {}
        self.sem = None
        self.referenced = set()
        self.rank = {}


class FW:
    def __init__(self):
        self.streams = {n: Stream(n) for n in ("pe", "act", "dve", "pool", "sp")}
        self.chans = []

    def _need(self, st, tok):
        if tok is None:
            return
        if tok[0] == 'E':
            src = tok[1]
            if src is st and st.name in ("pe", "sp"):
                return
            key = src.name
        else:
            key = id(tok[1])
        val = tok[2]
        if st.seen.get(key, -1) >= val:
            return
        st.seen[key] = val
        st.items.append(('wait', tok))
        if tok[0] == 'E':
            src.referenced.add(val)

    def _deps(self, st, reads, writes):
        for b in reads:
            self._need(st, b.w)
        for b in writes:
            self._need(st, b.w)
            for t in b.r.values():
                self._need(st, t)

    def _commit(self, tok, key, reads, writes):
        for b in reads:
            b.r[key] = tok
        for b in writes:
            b.w = tok
            b.r = {}

    def op(self, eng, fn, reads=(), writes=(), chan=None):
        st = self.streams[eng]
        self._deps(st, reads, writes)
        if chan is not None:
            chan.count += 16
            tok = ('C', chan, chan.count)
            st.items.append(('dma', fn, chan))
            key = id(chan)
        else:
            st.nops += 1
            tok = ('E', st, st.nops)
            st.items.append(('op', fn, st.nops))
            key = st.name
        self._commit(tok, key, reads, writes)
        return tok

    def pe_group(self, fns, reads=(), writes=()):
        st = self.streams["pe"]
        self._deps(st, reads, writes)
        for fn in fns[:-1]:
            st.items.append(('op', fn, None))
        st.nops += 1
        tok = ('E', st, st.nops)
        st.items.append(('op', fns[-1], st.nops))
        self._commit(tok, "pe", reads, writes)
        return tok

    def new_chan(self):
        c = Chan()
        self.chans.append(c)
        return c

    def wait_all(self, eng, bufs):
        st = self.streams[eng]
        for b in bufs:
            self._need(st, b.w)
            for t in b.r.values():
                self._need(st, t)

    def barrier(self):
        toks = []
        for st in self.streams.values():
            if st.nops > 0:
                toks.append(('E', st, st.nops))
        for c in self.chans:
            if c.count > 0:
                toks.append(('C', c, c.count))
        for st in self.streams.values():
            for t in toks:
                self._need(st, t)

    def n_sems(self):
        return len(self.streams) + len(self.chans)

    def replay(self, block, sems):
        it = iter(sems)
        for st in self.streams.values():
            st.sem = next(it)
        for c in self.chans:
            c.sem = next(it)
        for st in self.streams.values():
            st.rank = {idx: i + 1 for i, idx in enumerate(sorted(st.referenced))}

        def run(st, h):
            for item in st.items:
                if item[0] == 'wait':
                    tok = item[1]
                    if tok[0] == 'E':
                        h.wait_ge(tok[1].sem, tok[1].rank[tok[2]])
                    else:
                        h.wait_ge(tok[1].sem, tok[2])
                elif item[0] == 'dma':
                    item[1](h).then_inc(item[2].sem, 16)
                else:
                    ins = item[1](h)
                    if item[2] is not None and item[2] in st.rank:
                        ins.then_inc(st.sem, 1)

        S = self.streams
        block.tensor(lambda h: run(S["pe"], h))
        block.scalar(lambda h: run(S["act"], h))
        block.vector(lambda h: run(S["dve"], h))
        block.gpsimd(lambda h: run(S["pool"], h))
        block.sync(lambda h: run(S["sp"], h))


class Arena:
    def __init__(self, ap):
        self.ap = ap
        self.n = ap.shape[1]
        self.off = 0
        self.peak = 0
        self.top = self.n

    def alloc(self, nelem, dt=BF16):
        nb = nelem * (4 if dt in (F32, I32) else 2)
        n16 = (nb + 31) // 32 * 16
        s = self.off
        self.off += n16
        self.peak = max(self.peak, self.off)
        assert self.off <= self.top, f"arena overflow {self.off}>{self.top}"
        v = self.ap[:, s:s + nb // 2]
        if dt != BF16:
            v = v.bitcast(dt)
        return v

    def alloc_top(self, nelem):
        self.top -= (nelem + 15) // 16 * 16
        assert self.off <= self.top, f"arena overflow(top) {self.off}>{self.top}"
        return self.ap[:, self.top:self.top + nelem]

    def release_top(self):
        self.top = self.n

    def mark(self):
        return self.off

    def release(self, m):
        self.off = m


class Prog:
    def __init__(self, cfg=None):
        self.cfg = cfg or {}
        self.nc = bass.Bass("TRN2", target_bir_lowering=False)
        self.fw = FW()

    def next_ps(self):
        i = self.ps_i
        self.ps_i = (self.ps_i + 1) % len(self.ps_gen)
        return self.ps_gen[i]

    def next_ps_tile(self):
        i = self.pst_i
        self.pst_i = (self.pst_i + 1) % 4
        return self.ps_all[2 + i]

    def next_ps_proj(self):
        i = self.psp_i
        self.psp_i = (self.psp_i + 1) % 2
        return self.ps_all[6 + i]

    def next_acc(self):
        i = self.acc_i
        self.acc_i = (self.acc_i + 1) % len(self.ps_acc)
        return self.ps_acc[i]

    def run_pipeline(self, N, stages, skews, extra):
        total = N + max(skews)
        per = max(1, N // (len(extra) + 1)) if extra else 0
        ti = 0
        for n in range(total):
            for fn, sk in zip(stages, skews):
                i = n - sk
                if 0 <= i < N:
                    fn(i)
            if extra and ti < len(extra) and n % per == per - 1:
                extra[ti]()
                ti += 1
        while ti < len(extra):
            extra[ti]()
            ti += 1

    def mm_group(self, out_ap, pairs, reads, ps_buf):
        n = len(pairs)
        fns = []
        for i, (l, r) in enumerate(pairs):
            fns.append(lambda e, l=l, r=r, i=i: e.matmul(out_ap, lhsT=l, rhs=r, start=(i == 0), stop=(i == n - 1)))
        return self.fw.pe_group(fns, reads=reads, writes=[ps_buf])

    def load_weight(self, dst3, src2, buf, chan, ncols_piece=None):
        self.fw.op("pool", lambda e: e.dma_start(out=dst3, in_=src2.rearrange("(c p) n -> p c n", p=128)),
                   writes=[buf], chan=chan)

    def rms_feature_major(self, xap, nch, width, gcol, out_fn, sq, b_sq, rstd, b_rstd, b_x, b_out, n_feat):
        fw = self.fw
        fw.op("act", lambda e: e.activation(sq[:, 0:nch, 0:width], xap, AF.Square), reads=[b_x], writes=[b_sq])
        ps, b_ps = self.next_ps()
        self.mm_group(ps[:, 0:width], [(self.ones[:], sq[:, c, 0:width]) for c in range(nch)], [b_sq, self.b_const], b_ps)
        fw.op("act", lambda e: e.activation(rstd[:, 0:width], ps[:, 0:width], AF.Sqrt, scale=1.0 / n_feat, bias=self.eps_col[:, 0:1]),
              reads=[], writes=[b_rstd, b_ps])
        fw.op("dve", lambda e: e.reciprocal(rstd[:, 0:width], rstd[:, 0:width]), reads=[], writes=[b_rstd])
        for c in range(nch):
            dst = out_fn(c)
            gsc = gcol(c)
            fw.op("dve", lambda e, c=c, dst=dst, gsc=gsc: e.scalar_tensor_tensor(out=dst, in0=xap[:, c, :], scalar=gsc, in1=rstd[:, 0:width],
                                                               op0=ALU.mult, op1=ALU.mult),
                  reads=[b_x, b_rstd, self.b_const], writes=[b_out])

    def rms_part1(self, xap, nch, width, sq, b_sq, b_x):
        fw = self.fw
        fw.op("act", lambda e: e.activation(sq[:, 0:nch, 0:width], xap, AF.Square), reads=[b_x], writes=[b_sq])
        ps, b_ps = self.next_ps()
        self.mm_group(ps[:, 0:width], [(self.ones[:], sq[:, c, 0:width]) for c in range(nch)], [b_sq, self.b_const], b_ps)
        return ps, b_ps

    def rms_part2(self, psb, xap, nch, width, gcol, out_fn, rstd, b_rstd, b_x, b_out, n_feat):
        fw = self.fw
        ps, b_ps = psb
        fw.op("act", lambda e: e.activation(rstd[:, 0:width], ps[:, 0:width], AF.Sqrt, scale=1.0 / n_feat, bias=self.eps_col[:, 0:1]),
              reads=[], writes=[b_rstd, b_ps])
        fw.op("dve", lambda e: e.reciprocal(rstd[:, 0:width], rstd[:, 0:width]), reads=[], writes=[b_rstd])
        for c in range(nch):
            dst = out_fn(c)
            gsc = gcol(c)
            fw.op("dve", lambda e, c=c, dst=dst, gsc=gsc: e.scalar_tensor_tensor(out=dst, in0=xap[:, c, :], scalar=gsc, in1=rstd[:, 0:width],
                                                                               op0=ALU.mult, op1=ALU.mult),
                  reads=[b_x, b_rstd, self.b_const], writes=[b_out])

    def build(self):
        nc, fw = self.nc, self.fw
        cfg = self.cfg
        self.es = es = ExitStack()
        with es:
            self.x_in = nc.dram_tensor("x", [T, D], F32, kind="ExternalInput").ap()
            self.pos_in = nc.dram_tensor("positions", [1, T], I32, kind="ExternalInput").ap()
            self.W = {}
            for l in range(4):
                for nm in ["norm_mix"] + WNAMES[l] + ["norm_mlp", "w_up", "w_down"]:
                    full = f"l{l}_{nm}"
                    self.W[full] = nc.dram_tensor(full, WSHAPES[nm], F32, kind="ExternalInput").ap()
            self.W["final_norm"] = nc.dram_tensor("final_norm", [D], F32, kind="ExternalInput").ap()
            self.out = nc.dram_tensor("out", [T, D], F32, kind="ExternalOutput").ap()
            self.xT_d = nc.dram_tensor("xT_scratch", [NCH, 128, T], F32).ap()
            self.b_xd = [Buf(f"xd{b}") for b in range(NBLK)]
            self.c_xd = [fw.new_chan() for _ in range(NBLK)]
            self.b_outd = Buf("outd")
            self.c_outd = fw.new_chan()
            self.dbg = []
            if cfg.get("dbg"):
                for k in range(7):
                    self.dbg.append(nc.dram_tensor(f"dbg{k}", [NCH, 128, T], F32, kind="ExternalOutput").ap())
            self.dbg_i = 0

            self.ps_all = []
            for i in range(8):
                t = es.enter_context(nc.psum_tensor(f"ps{i}", [128, 512], F32))
                self.ps_all.append((t, Buf(f"ps{i}")))
            self.ps_acc = self.ps_all[0:2]
            self.ps_gen = self.ps_all[2:8]
            self.ps_i = 0
            self.acc_i = 0
            self.pst_i = 0
            self.psp_i = 0

            def sb(name, shape, dt):
                return es.enter_context(nc.sbuf_tensor(name, shape, dt))
            self.ones = sb("ones", [128, 128], BF16)
            self.ident = sb("ident", [128, 128], F32)
            self.triI = sb("triI", [128, 128], BF16)
            self.triS = sb("triS", [128, 128], BF16)
            self.Uinc = sb("Uinc", [128, 128], BF16)
            self.neg8 = sb("neg8", [128, 128], BF16)
            self.gains = sb("gains", [128, NCH, 32], F32)
            self.eps_col = sb("eps_col", [128, 1], F32)
            self.negpi = sb("negpi", [128, 1], F32)
            self.b_const = Buf("const")
            self.b_TAB = Buf("TAB")
            arena_elems = (int(nc.sbuf_bytes_remaining) - 1024) // 64 * 32
            print("arena KiB", arena_elems * 2 / 1024)
            self.arena = Arena(sb("arena", [128, arena_elems], BF16))
            self.b_arena_guard = Buf("arena")

            self.setup_consts()
            fw.barrier()
            self.prologue()
            fw.barrier()
            layers = cfg.get("layers", [0, 1, 2, 3])
            for l in layers:
                if cfg.get("mix", True):
                    if l in MLA_LAYERS:
                        self.mla_phase(l)
                    elif l == 1:
                        self.conv_phase(l)
                    else:
                        self.sb_phase(l)
                    fw.barrier()
                    self.dump_dbg()
                if cfg.get("ffn", True):
                    self.ffn_phase(l, final=(l == layers[-1]) and cfg.get("final", True))
                    fw.barrier()
                    if not getattr(self, "_emitted_final", False):
                        self.dump_dbg()
            if not getattr(self, "_emitted_final", False):
                self.dump_xT()
            fw.wait_all("sp", [self.b_outd])
            sems = [es.enter_context(nc.semaphore(f"s{i}")) for i in range(fw.n_sems())]
            block = es.enter_context(nc.Block())
            fw.replay(block, sems)
        return nc

    def setup_consts(self):
        nc, fw, A = self.nc, self.fw, self.arena
        m = A.mark()
        bc = self.b_const
        iot = A.alloc(128, I32)
        b_t = Buf("ctmp")
        fw.op("pool", lambda e: e.iota(iot, pattern=[[1, 128]], base=0, channel_multiplier=-1), writes=[b_t])
        fw.op("dve", lambda e: e.tensor_single_scalar(self.ident[:], iot, 0, ALU.is_equal), reads=[b_t], writes=[bc])
        fw.op("dve", lambda e: e.tensor_single_scalar(self.triI[:], iot, 0, ALU.is_ge), reads=[b_t], writes=[bc])
        fw.op("dve", lambda e: e.tensor_single_scalar(self.triS[:], iot, 0, ALU.is_gt), reads=[b_t], writes=[bc])
        fw.op("dve", lambda e: e.tensor_scalar(self.Uinc[:], iot, 0, -8.0, op0=ALU.is_le, op1=ALU.mult), reads=[b_t], writes=[bc])
        fw.op("dve", lambda e: e.memset(self.ones[:], 1.0), writes=[bc])
        fw.op("dve", lambda e: e.memset(self.neg8[:], -8.0), writes=[bc])
        fw.op("dve", lambda e: e.memset(self.eps_col[:], EPS), writes=[bc])
        fw.op("dve", lambda e: e.memset(self.negpi[:], -math.pi), writes=[bc])
        gv = A.alloc(1024, F32)
        b_gv = Buf("gv")
        c_gv = fw.new_chan()
        fw.op("dve", lambda e: e.memset(gv[0:32, :], 0.0), writes=[b_gv])
        rows = []
        for l in range(4):
            rows.append((l, f"l{l}_norm_mix", 1024))
            rows.append((4 + l, f"l{l}_norm_mlp", 1024))
        rows.append((8, "final_norm", 1024))
        rows.append((12, "l1_conv_b", 1024))
        rows.append((13, "l0_norm_q", 384))
        rows.append((14, "l3_norm_q", 384))
        rows.append((15, "l0_norm_kv", 256))
        rows.append((16, "l3_norm_kv", 256))
        for r, nm, n in rows:
            fw.op("sp", lambda e, r=r, nm=nm, n=n: e.dma_start(out=gv[r:r + 1, 0:n], in_=self.W[nm].rearrange("(o n) -> o n", o=1)),
                  writes=[b_gv], chan=c_gv)
        fw.op("sp", lambda e: e.dma_start(out=gv[9:12, :], in_=self.W["l1_conv_w"]), writes=[b_gv], chan=c_gv)
        ps, b_ps = self.next_ps()
        fns = [lambda e, c=c: e.transpose(ps[:, c * 32:(c + 1) * 32], gv[0:32, c * 128:(c + 1) * 128], self.ident[0:32, 0:32]) for c in range(NCH)]
        fw.pe_group(fns, reads=[b_gv, bc], writes=[b_ps])
        fw.op("dve", lambda e: e.tensor_copy(self.gains[:].rearrange("p c r -> p (c r)"), ps[:, 0:256]), reads=[], writes=[bc, b_ps])
        self._const_mark = m

    def build_tables(self):
        fw, A = self.fw, self.arena
        bc = self.b_const
        b_t = Buf("ttmp")
        TAB = self.TAB = A.alloc(T, F32)
        m = A.mark()
        posi = A.alloc(T, I32)
        ang = A.alloc(T, F32)
        tq = A.alloc(T, F32)
        idx = A.alloc(1, I32)
        cf = A.alloc(1, F32)
        s1 = A.alloc(1, F32)
        invf = A.alloc(1, F32)
        off = A.alloc(1, F32)
        b_pos = Buf("pos")
        if not hasattr(self, "c_pos"):
            self.c_pos = fw.new_chan()
        c_pos = self.c_pos
        R = slice(64, 128)
        TWO_PI = 2.0 * math.pi
        fw.op("sp", lambda e: e.dma_start(out=posi[R, :], in_=self.pos_in[0, :].partition_broadcast(64)), writes=[b_pos], chan=c_pos)
        fw.op("pool", lambda e: e.iota(idx[R, :], pattern=[[0, 1]], base=0, channel_multiplier=1), writes=[b_t])
        fw.op("dve", lambda e: e.tensor_copy(cf[R, :], idx[R, :]), reads=[b_t], writes=[b_t])
        fw.op("dve", lambda e: e.tensor_copy(invf[R, :], cf[R, :]), writes=[b_t])
        for thr in (16.0, 32.0, 48.0):
            fw.op("dve", lambda e, thr=thr: e.tensor_scalar(s1[R, :], cf[R, :], thr, 16.0, op0=ALU.is_ge, op1=ALU.mult), writes=[b_t])
            fw.op("dve", lambda e: e.tensor_tensor(out=invf[R, :], in0=invf[R, :], in1=s1[R, :], op=ALU.subtract), writes=[b_t])
        fw.op("act", lambda e: e.activation(invf[R, :], invf[R, :], AF.Exp, scale=-math.log(10000.0) / 16.0), writes=[b_t])
        fw.op("dve", lambda e: e.memset(off[64:96, :], 0.5 * math.pi), writes=[b_t])
        fw.op("dve", lambda e: e.memset(off[96:128, :], 0.0), writes=[b_t])
        fw.op("dve", lambda e: e.memset(off[96:112, :], math.pi), writes=[b_t])
        fw.op("dve", lambda e: e.tensor_copy(ang[R, :], posi[R, :]), reads=[b_pos], writes=[b_t])
        fw.op("dve", lambda e: e.tensor_scalar(ang[R, :], ang[R, :], invf[R, 0:1], off[R, 0:1], op0=ALU.mult, op1=ALU.add), writes=[b_t])
        fw.op("dve", lambda e: e.tensor_scalar(tq[R, :], ang[R, :], 1.0 / TWO_PI, None, op0=ALU.mult), writes=[b_t])
        fw.op("dve", lambda e: e.tensor_copy(posi[R, :], tq[R, :]), writes=[b_t])
        fw.op("dve", lambda e: e.tensor_copy(tq[R, :], posi[R, :]), writes=[b_t])
        fw.op("dve", lambda e: e.scalar_tensor_tensor(out=ang[R, :], in0=tq[R, :], scalar=-TWO_PI, in1=ang[R, :], op0=ALU.mult, op1=ALU.add), writes=[b_t])
        fw.op("dve", lambda e: e.tensor_scalar(tq[R, :], ang[R, :], math.pi, TWO_PI, op0=ALU.is_ge, op1=ALU.mult), writes=[b_t])
        fw.op("dve", lambda e: e.tensor_tensor(out=ang[R, :], in0=ang[R, :], in1=tq[R, :], op=ALU.subtract), writes=[b_t])
        fw.op("dve", lambda e: e.tensor_scalar(tq[R, :], ang[R, :], -math.pi, TWO_PI, op0=ALU.is_lt, op1=ALU.mult), writes=[b_t])
        fw.op("dve", lambda e: e.tensor_tensor(out=ang[R, :], in0=ang[R, :], in1=tq[R, :], op=ALU.add), writes=[b_t])
        fw.op("act", lambda e: e.activation(TAB[R, :], ang[R, :], AF.Sin), reads=[b_t, bc], writes=[self.b_TAB])
        fw.barrier()
        A.release(m)

    def dump_dbg(self):
        if not self.dbg:
            return
        d = self.dbg[self.dbg_i]
        self.dbg_i += 1
        self.fw.op("sp", lambda e: e.dma_start(out=d[:, :, :], in_=self.xT_d[:, :, :]), reads=self.b_xd, writes=[self.b_outd], chan=self.c_outd)
        self.fw.barrier()

    def sdump(self, name, ap, buf):
        if not self.cfg.get("sdump"):
            return
        shp = list(ap.shape)
        d = self.nc.dram_tensor("sd_" + name, shp, ap.dtype, kind="ExternalOutput").ap()
        self.fw.barrier()
        self.fw.op("sp", lambda e: e.dma_start(out=d, in_=ap), reads=[buf], writes=[self.b_outd], chan=self.c_outd)
        self.fw.barrier()

    def gcol(self, row):
        return lambda c: self.gains[:, c, row:row + 1]

    def prologue(self):
        fw, A = self.fw, self.arena
        A.release(self._const_mark)
        m = A.mark()
        xin = [A.alloc(4 * D, F32).rearrange("p (t d) -> p t d", t=4) for _ in range(2)]
        xb = [A.alloc(NCH * BT, F32).rearrange("p (c t) -> p c t", c=NCH) for _ in range(2)]
        b_xin = [Buf("xin0"), Buf("xin1")]
        c_xin = [fw.new_chan(), fw.new_chan()]
        b_xb = [Buf("pxb0"), Buf("pxb1")]
        for blk in range(NBLK):
            s = blk % 2
            fw.op("sp", lambda e, s=s, blk=blk: e.dma_start(out=xin[s], in_=self.x_in[blk * BT:(blk + 1) * BT, :].rearrange("(t p) d -> p t d", p=128)),
                  writes=[b_xin[s]], chan=c_xin[s])
            for c in range(NCH):
                ps, b_ps = self.next_ps()
                fns = [lambda e, tt=tt, c=c, s=s, ps=ps: e.transpose(ps[:, tt * 128:(tt + 1) * 128], xin[s][:, tt, c * 128:(c + 1) * 128], self.ident[:])
                       for tt in range(4)]
                fw.pe_group(fns, reads=[b_xin[s], self.b_const], writes=[b_ps])
                eng = "dve" if c % 2 == 0 else "act"
                if eng == "dve":
                    fw.op("dve", lambda e, c=c, s=s, ps=ps: e.tensor_copy(xb[s][:, c, :], ps[:]), writes=[b_xb[s], b_ps])
                else:
                    fw.op("act", lambda e, c=c, s=s, ps=ps: e.copy(xb[s][:, c, :], ps[:]), writes=[b_xb[s], b_ps])
            self.store_xblk(xb[s], b_xb[s], blk)
        A.release(m)

    def xd_view(self, blk):
        return self.xT_d[:, :, blk * BT:(blk + 1) * BT].rearrange("c p t -> p c t")

    def store_xblk(self, xb, b_xb, blk):
        self.fw.op("sp", lambda e: e.dma_start(out=self.xd_view(blk), in_=xb), reads=[b_xb], writes=[self.b_xd[blk]], chan=self.c_xd[blk])

    def load_xblk(self, xb, b_xb, c_xb, blk):
        self.fw.op("sp", lambda e: e.dma_start(out=xb, in_=self.xd_view(blk)), reads=[self.b_xd[blk]], writes=[b_xb], chan=c_xb)

    def dump_xT(self):
        fw, A = self.fw, self.arena
        m = A.mark()
        xb = A.alloc(NCH * BT, F32).rearrange("p (c t) -> p c t", c=NCH)
        b_xb, c_xb = Buf("dxb"), fw.new_chan()
        for blk in range(NBLK):
            self.load_xblk(xb, b_xb, c_xb, blk)
            self.emit_output(xb, b_xb, blk)
        A.release(m)

    def emit_output(self, yb, b_yb, blk, ost=None, b_ost=None):
        fw, A = self.fw, self.arena
        m = A.mark()
        if ost is None:
            ost = A.alloc(4 * D, F32).rearrange("p (t d) -> p t d", t=4)
            b_ost = self.b_ost if hasattr(self, "b_ost") else Buf("ost")
            self.b_ost = b_ost
        for tt in range(4):
            for half in range(2):
                ps, b_ps = self.next_ps()
                fns = [lambda e, tt=tt, c=c, ps=ps: e.transpose(ps[:, (c % 4) * 128:(c % 4 + 1) * 128], yb[:, c, tt * 128:(tt + 1) * 128], self.ident[:])
                       for c in range(half * 4, half * 4 + 4)]
                fw.pe_group(fns, reads=[b_yb, self.b_const], writes=[b_ps])
                if (tt + half) % 2 == 0:
                    fw.op("dve", lambda e, tt=tt, half=half, ps=ps: e.tensor_copy(ost[:, tt, half * 512:(half + 1) * 512], ps[:]), writes=[b_ost, b_ps])
                else:
                    fw.op("act", lambda e, tt=tt, half=half, ps=ps: e.copy(ost[:, tt, half * 512:(half + 1) * 512], ps[:]), writes=[b_ost, b_ps])
        fw.op("sp", lambda e: e.dma_start(out=self.out[blk * BT:(blk + 1) * BT, :].rearrange("(t p) d -> p t d", p=128), in_=ost),
              reads=[b_ost], writes=[self.b_outd], chan=self.c_outd)
        A.release(m)

    def alloc_load_wup(self, l):
        fw, A = self.fw, self.arena
        off0 = A.off
        wup = A.alloc(NCH * DFF).rearrange("p (c n) -> p c n", c=NCH)
        NP = 4
        b_wup = [Buf(f"wup{i}") for i in range(NP)]
        if not hasattr(self, "c_wup"):
            self.c_wup = [fw.new_chan() for _ in range(NP)]
            self.c_wdn = [fw.new_chan() for _ in range(NP)]
        wu_d = self.W[f"l{l}_w_up"].rearrange("(c p) n -> p c n", p=128)
        for i in range(NP):
            fw.op("pool", lambda e, i=i: e.dma_start(out=wup[:, :, i * 1024:(i + 1) * 1024], in_=wu_d[:, :, i * 1024:(i + 1) * 1024]),
                  writes=[b_wup[i]], chan=self.c_wup[i])
        return (off0, wup, b_wup)

    def ffn_phase(self, l, final=False):
        fw, A = self.fw, self.arena
        if final:
            self._emitted_final = True
        m = A.mark()
        NP = 4
        pre = getattr(self, "pre_wup", None)
        self.pre_wup = None
        if pre is not None and pre[0] == A.off:
            _, wup, b_wup = pre
            A.alloc(NCH * DFF)
        else:
            assert pre is None, "prefetched w_up at unexpected offset"
            _, wup, b_wup = self.alloc_load_wup(l)
        wdn = A.alloc(32 * D).rearrange("p (f n) -> p f n", f=32)
        b_wdn = [Buf(f"wdn{i}") for i in range(NP)]
        wd_d = self.W[f"l{l}_w_down"].rearrange("(f p) n -> p f n", p=128)
        for i in range(NP):
            fw.op("pool", lambda e, i=i: e.dma_start(out=wdn[:, i * 8:(i + 1) * 8, :], in_=wd_d[:, i * 8:(i + 1) * 8, :]),
                  writes=[b_wdn[i]], chan=self.c_wdn[i])
        xbs = [A.alloc(NCH * BT, F32).rearrange("p (c t) -> p c t", c=NCH) for _ in range(2)]
        b_xbs = [Buf("fxb0"), Buf("fxb1")]
        if not hasattr(self, "c_fxb"):
            self.c_fxb = [fw.new_chan(), fw.new_chan()]
        hT = A.alloc(NCH * BT).rearrange("p (c t) -> p c t", c=NCH)
        b_hT = Buf("hT")
        aT_raw = A.alloc(32 * BT)
        aT = aT_raw.rearrange("p (f t) -> p f t", f=32)
        b_aTf = [Buf(f"aT{f}") for f in range(32)]
        rstd = A.alloc(BT, F32)
        b_rstd = Buf("rstd")
        yb = aT_raw[:, 0:2 * NCH * BT].bitcast(F32).rearrange("p (c t) -> p c t", c=NCH)
        ost = aT_raw[:, 2 * NCH * BT:4 * NCH * BT].bitcast(F32).rearrange("p (t d) -> p t d", t=4)
        sqf = aT_raw[:, 2 * NCH * BT:3 * NCH * BT].rearrange("p (c t) -> p c t", c=NCH)
        print(f"[ffn {l}] arena peak {A.peak * 2 / 1024:.1f} KiB")

        def norm(blk):
            s = blk % 2
            self.rms_feature_major(xbs[s], NCH, BT, self.gcol(4 + l), lambda c: hT[:, c, :], hT, b_hT, rstd, b_rstd, b_xbs[s], b_hT, D)

        self.load_xblk(xbs[0], b_xbs[0], self.c_fxb[0], 0)
        norm(0)
        for blk in range(NBLK):
            s = blk % 2
            xb, b_xb = xbs[s], b_xbs[s]
            if blk + 1 < NBLK:
                self.load_xblk(xbs[1 - s], b_xbs[1 - s], self.c_fxb[1 - s], blk + 1)
            for f in range(32):
                ps, b_ps = self.next_ps()
                self.mm_group(ps[:], [(wup[:, c, f * 128:(f + 1) * 128], hT[:, c, :]) for c in range(NCH)], [b_hT, b_wup[f // 8]], b_ps)
                fw.op("act", lambda e, ps=ps, f=f: e.activation(aT[:, f, :], ps[:], AF.Relu), writes=[b_aTf[f], b_ps])
                fw.op("dve", lambda e, f=f: e.tensor_tensor(out=aT[:, f, :], in0=aT[:, f, :], in1=aT[:, f, :], op=ALU.mult), writes=[b_aTf[f]])
            for c in range(NCH):
                ps, b_ps = self.next_ps()
                self.mm_group(ps[:], [(wdn[:, f, c * 128:(c + 1) * 128], aT[:, f, :]) for f in range(32)], b_aTf + b_wdn, b_ps)
                fw.op("dve", lambda e, c=c, ps=ps, xb=xb: e.tensor_tensor(out=xb[:, c, :], in0=xb[:, c, :], in1=ps[:], op=ALU.add), writes=[b_xb, b_ps])
                if c == 1 and blk + 1 < NBLK:
                    norm(blk + 1)
            if final:
                b_sqf = b_aTf[16:24]
                b_rsf = b_aTf[28:30]
                fw.op("act", lambda e, xb=xb: e.activation(sqf, xb, AF.Square), reads=[b_xb], writes=b_sqf)
                ps, b_ps = self.next_ps()
                self.mm_group(ps[:], [(self.ones[:], sqf[:, c, :]) for c in range(NCH)], b_sqf + [self.b_const], b_ps)
                rstd_f = ost[:, 3, 0:BT]
                fw.op("act", lambda e, ps=ps: e.activation(rstd_f, ps[:], AF.Sqrt, scale=1.0 / D, bias=self.eps_col[:, 0:1]), writes=b_rsf + [b_ps])
                fw.op("dve", lambda e: e.reciprocal(rstd_f, rstd_f), writes=b_rsf)
                for c in range(NCH):
                    fw.op("dve", lambda e, c=c, xb=xb: e.scalar_tensor_tensor(out=yb[:, c, :], in0=xb[:, c, :], scalar=self.gains[:, c, 8:9], in1=rstd_f,
                                                                              op0=ALU.mult, op1=ALU.mult),
                          reads=[b_xb, self.b_const] + b_rsf, writes=b_aTf[2 * c:2 * c + 2])
                self.emit_output_multi(yb, b_aTf, blk, ost)
            else:
                self.store_xblk(xb, b_xb, blk)
        A.release(m)

    def emit_output_multi(self, yb, b_list, blk, ost):
        fw = self.fw
        for tt in range(4):
            for half in range(2):
                ps, b_ps = self.next_ps()
                fns = [lambda e, tt=tt, c=c, ps=ps: e.transpose(ps[:, (c % 4) * 128:(c % 4 + 1) * 128], yb[:, c, tt * 128:(tt + 1) * 128], self.ident[:])
                       for c in range(half * 4, half * 4 + 4)]
                b_y = b_list[8 * half:8 * half + 8]
                b_o = b_list[16 + 4 * tt + 2 * half:16 + 4 * tt + 2 * half + 2]
                fw.pe_group(fns, reads=b_y + [self.b_const], writes=[b_ps])
                if (tt + half) % 2 == 0:
                    fw.op("dve", lambda e, tt=tt, half=half, ps=ps: e.tensor_copy(ost[:, tt, half * 512:(half + 1) * 512], ps[:]), writes=b_o + [b_ps])
                else:
                    fw.op("act", lambda e, tt=tt, half=half, ps=ps: e.copy(ost[:, tt, half * 512:(half + 1) * 512], ps[:]), writes=b_o + [b_ps])
        fw.op("sp", lambda e: e.dma_start(out=self.out[blk * BT:(blk + 1) * BT, :].rearrange("(t p) d -> p t d", p=128), in_=ost),
              reads=b_list[16:32], writes=[self.b_outd], chan=self.c_outd)

    def stage_c(self, l, oT, b_oT, wo_name):
        fw, A = self.fw, self.arena
        m = A.mark()
        if self.cfg.get("ffn", True):
            self.pre_wup = self.alloc_load_wup(l)
        wo = A.alloc(NCH * D).rearrange("p (c n) -> p c n", c=NCH)
        b_wo = Buf("wo")
        if not hasattr(self, "c_wo"):
            self.c_wo = fw.new_chan()
        self.load_weight(wo, self.W[f"l{l}_{wo_name}"], b_wo, self.c_wo)
        xbs = [A.alloc(NCH * BT, F32).rearrange("p (c t) -> p c t", c=NCH) for _ in range(2)]
        b_xbs = [Buf("cxb0"), Buf("cxb1")]
        if not hasattr(self, "c_cxb"):
            self.c_cxb = [fw.new_chan(), fw.new_chan()]
        self.load_xblk(xbs[0], b_xbs[0], self.c_cxb[0], 0)
        for blk in range(NBLK):
            s = blk % 2
            xb, b_xb = xbs[s], b_xbs[s]
            if blk + 1 < NBLK:
                self.load_xblk(xbs[1 - s], b_xbs[1 - s], self.c_cxb[1 - s], blk + 1)
            for c in range(NCH):
                ps, b_ps = self.next_ps()
                self.mm_group(ps[:], [(wo[:, k, c * 128:(c + 1) * 128], oT[:, k, blk * BT:(blk + 1) * BT]) for k in range(NCH)], [b_oT, b_wo], b_ps)
                fw.op("dve", lambda e, c=c, ps=ps, xb=xb: e.tensor_tensor(out=xb[:, c, :], in0=xb[:, c, :], in1=ps[:], op=ALU.add), writes=[b_xb, b_ps])
            self.store_xblk(xb, b_xb, blk)
        A.release(m)

    def mla_phase(self, l):
        fw, A = self.fw, self.arena
        m0 = A.mark()
        gq_row = 13 if l == 0 else 14
        gkv_row = 15 if l == 0 else 16
        b_oT = Buf("oT")
        mAB = A.mark()
        self.build_tables()
        TAB = self.TAB
        cqT = A.alloc(3 * T).rearrange("p (c t) -> p c t", c=3)
        ckvT = A.alloc(2 * T).rearrange("p (c t) -> p c t", c=2)
        b_cq, b_ckv = Buf("cqT"), Buf("ckvT")
        kh = [A.alloc(T) for _ in range(2)]
        b_khr = [Buf("khr0"), Buf("khr1")]
        b_khn = [Buf("khn0"), Buf("khn1")]
        wq = A.alloc(3 * 16 * 128).rearrange("p (c h e) -> p c h e", c=3, h=16)
        wuk = A.alloc(2 * 1024).rearrange("p (c n) -> p c n", c=2)
        wuv = A.alloc(2 * 1024).rearrange("p (c n) -> p c n", c=2)
        b_wq, b_wuk, b_wuv = Buf("wq"), Buf("wuk"), Buf("wuv")
        mA = A.mark()
        wdq = A.alloc(NCH * 384).rearrange("p (c n) -> p c n", c=NCH)
        wdkv = A.alloc(NCH * 320).rearrange("p (c n) -> p c n", c=NCH)
        b_wdq, b_wdkv = Buf("wdq"), Buf("wdkv")
        if not hasattr(self, "c_mla_w"):
            self.c_mla_w = [fw.new_chan() for _ in range(6)]
        cw = self.c_mla_w
        self.load_weight(wdq, self.W[f"l{l}_w_dq"], b_wdq, cw[0])
        wdkv_d = self.W[f"l{l}_w_dkv"].rearrange("(c p) n -> p c n", p=128)
        fw.op("pool", lambda e: e.dma_start(out=wdkv[:, :, 0:288], in_=wdkv_d), writes=[b_wdkv], chan=cw[1])
        fw.op("pool", lambda e: e.dma_start(out=wdkv[:, :, 288:304], in_=wdkv_d[:, :, 272:288]), writes=[b_wdkv], chan=cw[1])
        fw.op("pool", lambda e: e.dma_start(out=wdkv[:, :, 304:320], in_=wdkv_d[:, :, 256:272]), writes=[b_wdkv], chan=cw[1])
        wuq_d = self.W[f"l{l}_w_uq"].rearrange("(c p) (h e) -> p c h e", p=128, e=96)
        for c3 in range(3):
            fw.op("pool", lambda e, c3=c3: e.dma_start(out=wq[:, c3, :, 0:96], in_=wuq_d[:, c3, :, :]), writes=[b_wq], chan=cw[2])
            fw.op("pool", lambda e, c3=c3: e.dma_start(out=wq[:, c3, :, 96:112], in_=wuq_d[:, c3, :, 80:96]), writes=[b_wq], chan=cw[2])
            fw.op("pool", lambda e, c3=c3: e.dma_start(out=wq[:, c3, :, 112:128], in_=wuq_d[:, c3, :, 64:80]), writes=[b_wq], chan=cw[2])
        self.load_weight(wuk, self.W[f"l{l}_w_uk"], b_wuk, cw[3])
        self.load_weight(wuv, self.W[f"l{l}_w_uv"], b_wuv, cw[4])
        xbs = [A.alloc(NCH * BT, F32).rearrange("p (c t) -> p c t", c=NCH) for _ in range(2)]
        b_xbs = [Buf("axb0"), Buf("axb1")]
        if not hasattr(self, "c_axb"):
            self.c_axb = [fw.new_chan(), fw.new_chan()]
        hTs = [A.alloc(NCH * BT).rearrange("p (c t) -> p c t", c=NCH) for _ in range(2)]
        raws = [A.alloc(3 * BT, F32).rearrange("p (c t) -> p c t", c=3) for _ in range(2)]
        raw2s = [A.alloc(2 * BT, F32).rearrange("p (c t) -> p c t", c=2) for _ in range(2)]
        rstd = A.alloc(BT, F32)
        rstd2 = A.alloc(BT, F32)
        rstd3 = A.alloc(BT, F32)
        t1 = A.alloc(BT, F32)
        t2 = A.alloc(BT, F32)
        b_hTs, b_raws, b_raw2s = [Buf("hT0"), Buf("hT1")], [Buf("raw0"), Buf("raw1")], [Buf("raw20"), Buf("raw21")]
        b_rstd, b_rstd2, b_rstd3, b_t1, b_t2 = Buf("rstd"), Buf("rstd2"), Buf("rstd3"), Buf("t1"), Buf("t2")
        sq_b, sq_c = Buf("sqb"), Buf("sqc")
        sqq = A.alloc(3 * BT).rearrange("p (c t) -> p c t", c=3)
        sqk = A.alloc(2 * BT).rearrange("p (c t) -> p c t", c=2)
        raw, raw2 = raws[0], raw2s[0]
        b_raw, b_raw2 = b_raws[0], b_raw2s[0]
        print(f"[mla {l} A] arena peak {A.peak * 2 / 1024:.1f} KiB")

        def A_L(blk):
            s = blk % 2
            self.load_xblk(xbs[s], b_xbs[s], self.c_axb[s], blk)

        def A_N1(blk):
            s = blk % 2
            hT = hTs[s]
            self.rms_feature_major(xbs[s], NCH, BT, self.gcol(l), lambda c: hT[:, c, :], hT, b_hTs[s], rstd, b_rstd, b_xbs[s], b_hTs[s], D)

        def A_M(blk):
            s = blk % 2
            hT, b_hT = hTs[s], b_hTs[s]
            rw, b_rw, rw2, b_rw2 = raws[s], b_raws[s], raw2s[s], b_raw2s[s]
            cols = slice(blk * BT, (blk + 1) * BT)
            for mch in range(3):
                ps, b_ps = self.next_ps()
                self.mm_group(ps[:], [(wdq[:, c, mch * 128:(mch + 1) * 128], hT[:, c, :]) for c in range(NCH)], [b_hT, b_wdq], b_ps)
                fw.op("act", lambda e, mch=mch, ps=ps, rw=rw: e.copy(rw[:, mch, :], ps[:]), writes=[b_rw, b_ps])
            for mch in range(2):
                ps, b_ps = self.next_ps()
                self.mm_group(ps[:], [(wdkv[:, c, mch * 128:(mch + 1) * 128], hT[:, c, :]) for c in range(NCH)], [b_hT, b_wdkv], b_ps)
                fw.op("act", lambda e, mch=mch, ps=ps, rw2=rw2: e.copy(rw2[:, mch, :], ps[:]), writes=[b_rw2, b_ps])
            ps, b_ps = self.next_ps()
            self.mm_group(ps[:], [(wdkv[:, c, 192:320], hT[:, c, :]) for c in range(NCH)], [b_hT, b_wdkv], b_ps)
            fw.op("dve", lambda e, ps=ps, cols=cols: e.tensor_tensor(out=t1[64:96, :], in0=ps[64:96, :], in1=TAB[64:96, cols], op=ALU.mult),
                  reads=[self.b_TAB], writes=[b_t1, b_ps])
            fw.op("dve", lambda e, ps=ps, cols=cols: e.tensor_tensor(out=t2[64:96, :], in0=ps[96:128, :], in1=TAB[96:128, cols], op=ALU.mult),
                  reads=[self.b_TAB], writes=[b_t2, b_ps])
            fw.op("pool", lambda e, cols=cols: e.tensor_tensor(out=kh[0][64:96, cols], in0=t1[64:96, :], in1=t2[64:96, :], op=ALU.add),
                  reads=[b_t1, b_t2], writes=[b_khr[0]])
            fw.op("pool", lambda e, cols=cols: e.tensor_copy(kh[1][64:96, cols], kh[0][64:96, cols]), reads=[b_khr[0]], writes=[b_khr[1]])

        def A_N2(blk):
            s = blk % 2
            cols = slice(blk * BT, (blk + 1) * BT)
            self.rms_feature_major(raws[s], 3, BT, self.gcol(gq_row), lambda c: cqT[:, c, cols], sqq, sq_b, rstd2, b_rstd2, b_raws[s], b_cq, 384)
            self.rms_feature_major(raw2s[s], 2, BT, self.gcol(gkv_row), lambda c: ckvT[:, c, cols], sqk, sq_c, rstd3, b_rstd3, b_raw2s[s], b_ckv, 256)

        self.run_pipeline(NBLK, [A_L, A_N1, A_M, A_N2], [0, 1, 2, 3], [])
        self.sdump("TAB", TAB[64:128, :], self.b_TAB)
        self.sdump("cqT", cqT.rearrange("p c t -> p (c t)"), b_cq)
        self.sdump("ckvT", ckvT.rearrange("p c t -> p (c t)"), b_ckv)
        self.sdump("krope", kh[0][64:96, :], b_khr[0])
        self.sdump("t1", t1[64:96, :], b_t1)
        self.sdump("t2", t2[64:96, :], b_t2)
        self.sdump("raw", raw.rearrange("p c t -> p (c t)"), b_raw)
        self.sdump("rstd2", rstd2, b_rstd2)
        self.sdump("wdkv", wdkv.rearrange("p c n -> p (c n)"), b_wdkv)
        if self.cfg.get("stopA"):
            A.release_top()
            A.release(m0)
            return
        fw.barrier()
        A.release(mA)
        oT = A.alloc_top(NCH * T).rearrange("p (c t) -> p c t", c=NCH)
        qh = [A.alloc(T) for _ in range(2)]
        vh = [A.alloc(32 * 128).rearrange("p (j e) -> p j e", j=32) for _ in range(2)]
        b_qh = [Buf("qh0"), Buf("qh1")]
        b_vh = [Buf("vh0"), Buf("vh1")]
        NPT = 4
        pts = [A.alloc(BT) for _ in range(NPT)]
        b_pts = [Buf(f"pt{i}") for i in range(NPT)]
        rec = A.alloc(BT, F32)
        b_rec = Buf("rec")
        qt1 = A.alloc(BT, F32)
        qt2 = A.alloc(BT, F32)
        b_qt1, b_qt2 = Buf("qt1"), Buf("qt2")
        print(f"[mla {l} B] arena peak {A.peak * 2 / 1024:.1f} KiB")
        for i in range(2):
            fw.op("pool", lambda e, i=i: e.memset(vh[i][:, :, 64:128], 1.0), writes=[b_vh[i]])
        scale = 1.0 / math.sqrt(96.0)
        heads = self.cfg.get("heads", list(range(16)))
        def proj_tasks(hi, h):
            s = hi % 2
            tasks = []
            for tb in range(NBLK):
                def tq(tb=tb, s=s, h=h):
                    cols = slice(tb * BT, (tb + 1) * BT)
                    ps, b_ps = self.next_ps_proj()
                    self.mm_group(ps[:], [(wq[:, k, h, :], cqT[:, k, cols]) for k in range(3)], [b_cq, b_wq], b_ps)
                    fw.op("dve", lambda e, ps=ps, s=s, cols=cols: e.tensor_copy(qh[s][0:64, cols], ps[0:64, :]), writes=[b_qh[s], b_ps])
                    fw.op("dve", lambda e, ps=ps, cols=cols: e.tensor_tensor(out=qt1[64:96, :], in0=ps[64:96, :], in1=TAB[64:96, cols], op=ALU.mult),
                          reads=[self.b_TAB], writes=[b_qt1, b_ps])
                    fw.op("dve", lambda e, ps=ps, cols=cols: e.tensor_tensor(out=qt2[64:96, :], in0=ps[96:128, :], in1=TAB[96:128, cols], op=ALU.mult),
                          reads=[self.b_TAB], writes=[b_qt2, b_ps])
                    fw.op("pool", lambda e, s=s, cols=cols: e.tensor_tensor(out=qh[s][64:96, cols], in0=qt1[64:96, :], in1=qt2[64:96, :], op=ALU.add),
                          reads=[b_qt1, b_qt2], writes=[b_qh[s]])
                tasks.append(tq)

                def tk(tb=tb, s=s, h=h):
                    cols = slice(tb * BT, (tb + 1) * BT)
                    ps, b_ps = self.next_ps_proj()
                    self.mm_group(ps[0:64, :], [(wuk[:, k, h * 64:(h + 1) * 64], ckvT[:, k, cols]) for k in range(2)], [b_ckv, b_wuk], b_ps)
                    fw.op("dve", lambda e, ps=ps, s=s, cols=cols: e.tensor_copy(kh[s][0:64, cols], ps[0:64, :]), writes=[b_khn[s], b_ps])
                tasks.append(tk)
            for j0 in range(0, 32, 8):
                def tv(j0=j0, s=s, h=h):
                    ps, b_ps = self.next_ps_proj()
                    fns = []
                    for jj in range(8):
                        j = j0 + jj
                        for k in range(2):
                            fns.append(lambda e, ps=ps, jj=jj, j=j, k=k, h=h: e.matmul(ps[:, jj * 64:(jj + 1) * 64], lhsT=ckvT[:, k, j * 128:(j + 1) * 128],
                                                                                 rhs=wuv[:, k, h * 64:(h + 1) * 64], start=(k == 0), stop=(k == 1)))
                    fw.pe_group(fns, reads=[b_ckv, b_wuv], writes=[b_ps])
                    fw.op("dve", lambda e, ps=ps, s=s, j0=j0: e.tensor_copy(vh[s][:, j0:j0 + 8, 0:64], ps[:].rearrange("p (j e) -> p j e", j=8)), writes=[b_vh[s], b_ps])
                tasks.append(tv)
            return tasks

        tiles = []
        for qb in range(NBLK):
            for kc in range(4 * qb + 4):
                tiles.append((qb, kc))
        NT = len(tiles)
        for t in proj_tasks(0, heads[0]):
            t()
        for hi, h in enumerate(heads):
            s = hi % 2
            extra = proj_tasks(hi + 1, heads[hi + 1]) if hi + 1 < len(heads) else []
            st = {}

            def S_(i, s=s):
                qb, kc = tiles[i]
                nq0 = max(0, kc - 4 * qb) * 128
                ps, b_ps = self.next_ps_tile()
                st[i] = [ps, b_ps, None, None]
                self.mm_group(ps[:, nq0:BT], [(kh[s][0:96, kc * 128:(kc + 1) * 128], qh[s][0:96, qb * BT + nq0:(qb + 1) * BT])],
                              [b_khr[s], b_khn[s], b_qh[s]], b_ps)

            def E_(i, s=s):
                qb, kc = tiles[i]
                nq0 = max(0, kc - 4 * qb) * 128
                ps, b_ps = st[i][0], st[i][1]
                pt, b_pt = pts[i % NPT], b_pts[i % NPT]
                st[i][2], st[i][3] = pt, b_pt
                fw.op("act", lambda e, ps=ps, pt=pt, nq0=nq0: e.activation(pt[:, nq0:BT], ps[:, nq0:BT], AF.Exp, scale=scale), writes=[b_pt, b_ps])
                if kc >= 4 * qb:
                    fw.op("pool", lambda e, pt=pt, nq0=nq0: e.tensor_tensor(out=pt[:, nq0:nq0 + 128], in0=pt[:, nq0:nq0 + 128], in1=self.triI[:], op=ALU.mult),
                          reads=[self.b_const], writes=[b_pt])

            cur = {}

            def P_(i, s=s, h=h):
                qb, kc = tiles[i]
                nq0 = max(0, kc - 4 * qb) * 128
                last = 4 * qb + 3
                pt, b_pt = st[i][2], st[i][3]
                del st[i]
                if kc == 0:
                    cur[qb] = self.next_acc()
                po, b_po = cur[qb]
                fw.pe_group([lambda e, po=po, pt=pt, nq0=nq0, kc=kc, last=last, s=s: e.matmul(po[:, nq0:BT], lhsT=vh[s][:, kc, :], rhs=pt[:, nq0:BT],
                                                                                              start=(kc == 0), stop=(kc == last))],
                            reads=[b_vh[s], b_pt], writes=[b_po])
                if kc == last:
                    qc = slice(qb * BT, (qb + 1) * BT)
                    fw.op("dve", lambda e, po=po: e.reciprocal(rec[0:64, :], po[64:128, :]), writes=[b_rec, b_po])
                    fw.op("dve", lambda e, po=po, h=h, qc=qc: e.tensor_tensor(out=oT[(h % 2) * 64:(h % 2) * 64 + 64, h // 2, qc], in0=po[0:64, :],
                                                                              in1=rec[0:64, :], op=ALU.mult),
                          reads=[b_rec], writes=[b_oT, b_po])

            self.run_pipeline(NT, [S_, E_, P_], [0, 2, 3], extra)
        self.sdump("qh", qh[(len(heads) - 1) % 2][0:96, :], b_qh[(len(heads) - 1) % 2])
        self.sdump("khn", kh[(len(heads) - 1) % 2][0:64, :], b_khn[(len(heads) - 1) % 2])
        self.sdump("vh", vh[(len(heads) - 1) % 2].rearrange("p j e -> p (j e)"), b_vh[(len(heads) - 1) % 2])
        self.sdump("oT", oT.rearrange("p c t -> p (c t)"), b_oT)
        A.release(mAB)
        fw.barrier()
        self.stage_c(l, oT, b_oT, "w_o")
        A.release_top()
        A.release(m0)

    def conv_phase(self, l):
        fw, A = self.fw, self.arena
        m0 = A.mark()
        win = A.alloc(NCH * 3072).rearrange("p (c n) -> p c n", c=NCH)
        wout = A.alloc(NCH * D).rearrange("p (c n) -> p c n", c=NCH)
        b_win = [Buf(f"win{i}") for i in range(3)]
        b_wout = Buf("wout")
        if not hasattr(self, "c_conv_w"):
            self.c_conv_w = [fw.new_chan() for _ in range(4)]
        cw = self.c_conv_w
        win_d = self.W[f"l{l}_w_in"].rearrange("(c p) n -> p c n", p=128)
        for i in range(3):
            fw.op("pool", lambda e, i=i: e.dma_start(out=win[:, :, i * 1024:(i + 1) * 1024], in_=win_d[:, :, i * 1024:(i + 1) * 1024]),
                  writes=[b_win[i]], chan=cw[i])
        self.load_weight(wout, self.W[f"l{l}_w_out"], b_wout, cw[3])
        xbs = [A.alloc(NCH * BT, F32).rearrange("p (c t) -> p c t", c=NCH) for _ in range(2)]
        b_xbs = [Buf("vxb0"), Buf("vxb1")]
        if not hasattr(self, "c_vxb"):
            self.c_vxb = [fw.new_chan(), fw.new_chan()]
        hTs = [A.alloc(NCH * BT).rearrange("p (c t) -> p c t", c=NCH) for _ in range(2)]
        b_hTs = [Buf("chT0"), Buf("chT1")]
        zT = A.alloc(NCH * BT).rearrange("p (c t) -> p c t", c=NCH)
        ucur = A.alloc(NCH * (BT + 2), F32).rearrange("p (c t) -> p c t", c=NCH)
        rstd = A.alloc(BT, F32)
        NTM = 2
        tmpc = [A.alloc(BT, F32) for _ in range(NTM)]
        acc = [A.alloc(BT, F32) for _ in range(NTM)]
        b_zT, b_rstd = Buf("zT"), Buf("rstd")
        b_u = [Buf(f"u{c}") for c in range(NCH)]
        b_tmpc = [Buf(f"tc{i}") for i in range(NTM)]
        b_acc = [Buf(f"ac{i}") for i in range(NTM)]
        print(f"[conv {l}] arena peak {A.peak * 2 / 1024:.1f} KiB")
        fw.op("pool", lambda e: e.memset(ucur.rearrange("p c t -> p (c t)"), 0.0), writes=b_u)
        ti = 0

        def cnorm(blk):
            s = blk % 2
            hTn = hTs[s]
            self.rms_feature_major(xbs[s], NCH, BT, self.gcol(l), lambda c: hTn[:, c, :], hTn, b_hTs[s], rstd, b_rstd, b_xbs[s], b_hTs[s], D)

        self.load_xblk(xbs[0], b_xbs[0], self.c_vxb[0], 0)
        cnorm(0)
        for blk in range(NBLK):
            s = blk % 2
            xb, b_xb = xbs[s], b_xbs[s]
            hT, b_hT = hTs[s], b_hTs[s]
            if blk + 1 < NBLK:
                self.load_xblk(xbs[1 - s], b_xbs[1 - s], self.c_vxb[1 - s], blk + 1)
            for c in range(NCH):
                if c == 2 and blk + 1 < NBLK:
                    cnorm(blk + 1)
                psB, b_psB = self.next_ps()
                self.mm_group(psB[:], [(win[:, k, c * 128:(c + 1) * 128], hT[:, k, :]) for k in range(NCH)], [b_hT, b_win[0]], b_psB)
                psC, b_psC = self.next_ps()
                self.mm_group(psC[:], [(win[:, k, 1024 + c * 128:1024 + (c + 1) * 128], hT[:, k, :]) for k in range(NCH)], [b_hT, b_win[1]], b_psC)
                psU, b_psU = self.next_ps()
                self.mm_group(psU[:], [(win[:, k, 2048 + c * 128:2048 + (c + 1) * 128], hT[:, k, :]) for k in range(NCH)], [b_hT, b_win[2]], b_psU)
                tc_, b_tc = tmpc[ti % NTM], b_tmpc[ti % NTM]
                ac_, b_ac = acc[ti % NTM], b_acc[ti % NTM]
                ti += 1
                fw.op("act", lambda e, psC=psC, tc_=tc_: e.copy(tc_, psC[:]), writes=[b_tc, b_psC])
                if blk > 0:
                    fw.op("pool", lambda e, c=c: e.tensor_copy(ucur[:, c, 0:2], ucur[:, c, BT:BT + 2]), writes=[b_u[c]])
                fw.op("dve", lambda e, c=c, psU=psU, tc_=tc_: e.tensor_tensor(out=ucur[:, c, 2:BT + 2], in0=psU[:], in1=tc_, op=ALU.mult),
                      reads=[b_tc], writes=[b_u[c], b_psU])
                fw.op("dve", lambda e, c=c, ac_=ac_: e.tensor_scalar(ac_, ucur[:, c, 2:BT + 2], self.gains[:, c, 11:12], self.gains[:, c, 12:13],
                                                                     op0=ALU.mult, op1=ALU.add),
                      reads=[b_u[c], self.b_const], writes=[b_ac])
                fw.op("dve", lambda e, c=c, ac_=ac_: e.scalar_tensor_tensor(out=ac_, in0=ucur[:, c, 1:BT + 1], scalar=self.gains[:, c, 10:11], in1=ac_,
                                                                             op0=ALU.mult, op1=ALU.add),
                      reads=[b_u[c], self.b_const], writes=[b_ac])
                fw.op("dve", lambda e, c=c, ac_=ac_: e.scalar_tensor_tensor(out=ac_, in0=ucur[:, c, 0:BT], scalar=self.gains[:, c, 9:10], in1=ac_,
                                                                             op0=ALU.mult, op1=ALU.add),
                      reads=[b_u[c], self.b_const], writes=[b_ac])
                fw.op("dve", lambda e, c=c, ac_=ac_, psB=psB: e.tensor_tensor(out=zT[:, c, :], in0=psB[:], in1=ac_, op=ALU.mult),
                      reads=[b_ac], writes=[b_zT, b_psB])
            for c in range(NCH):
                ps, b_ps = self.next_ps()
                self.mm_group(ps[:], [(wout[:, k, c * 128:(c + 1) * 128], zT[:, k, :]) for k in range(NCH)], [b_zT, b_wout], b_ps)
                fw.op("dve", lambda e, c=c, ps=ps, xb=xb: e.tensor_tensor(out=xb[:, c, :], in0=xb[:, c, :], in1=ps[:], op=ALU.add), writes=[b_xb, b_ps])
            self.store_xblk(xb, b_xb, blk)
        A.release(m0)

    def sb_phase(self, l):
        fw, A = self.fw, self.arena
        m0 = A.mark()
        b_oT = Buf("oT")
        mAB = A.mark()
        hTf = A.alloc(NCH * T).rearrange("p (c t) -> p c t", c=NCH)
        b_hTf = Buf("hTf")
        mA = A.mark()
        xbs = [A.alloc(NCH * BT, F32).rearrange("p (c t) -> p c t", c=NCH) for _ in range(2)]
        b_xbs = [Buf("sxb0"), Buf("sxb1")]
        if not hasattr(self, "c_sxb"):
            self.c_sxb = [fw.new_chan(), fw.new_chan()]
        sqs = [A.alloc(NCH * BT).rearrange("p (c t) -> p c t", c=NCH) for _ in range(2)]
        b_sqs = [Buf("sq0"), Buf("sq1")]
        rstds = [A.alloc(BT, F32) for _ in range(2)]
        b_rstds = [Buf("rstd0"), Buf("rstd1")]
        psd = {}

        def SA_L(blk):
            s = blk % 2
            self.load_xblk(xbs[s], b_xbs[s], self.c_sxb[s], blk)

        def SA_1(blk):
            s = blk % 2
            psd[blk] = self.rms_part1(xbs[s], NCH, BT, sqs[s], b_sqs[s], b_xbs[s])

        def SA_2(blk):
            s = blk % 2
            cols = slice(blk * BT, (blk + 1) * BT)
            self.rms_part2(psd.pop(blk), xbs[s], NCH, BT, self.gcol(l), lambda c: hTf[:, c, cols], rstds[s], b_rstds[s], b_xbs[s], b_hTf, D)

        self.run_pipeline(NBLK, [SA_2, SA_1, SA_L], [2, 1, 0], [])
        fw.barrier()
        A.release(mA)
        oT = A.alloc_top(NCH * T).rearrange("p (c t) -> p c t", c=NCH)
        wp = [A.alloc(NCH * 3 * 128).rearrange("p (c g e) -> p c g e", c=NCH, g=3) for _ in range(2)]
        b_wp = [Buf("wp0"), Buf("wp1")]
        if not hasattr(self, "c_wp"):
            self.c_wp = [fw.new_chan(), fw.new_chan()]
        qp = [A.alloc(T) for _ in range(2)]
        kp = [A.alloc(T) for _ in range(2)]
        vp = [A.alloc(32 * 128).rearrange("p (j e) -> p j e", j=32) for _ in range(2)]
        b_qp, b_kp, b_vp = [Buf("qp0"), Buf("qp1")], [Buf("kp0"), Buf("kp1")], [Buf("vp0"), Buf("vp1")]
        NE = 4
        Es = [A.alloc(BT) for _ in range(NE)]
        b_Es = [Buf(f"E{i}") for i in range(NE)]
        NS = 3
        ats = [A.alloc(BT) for _ in range(NS)]
        b_ats = [Buf(f"at{i}") for i in range(NS)]
        print(f"[sb {l} B] arena peak {A.peak * 2 / 1024:.1f} KiB")
        wqkv_d = self.W[f"l{l}_w_qkv"].rearrange("(c p) (g n) -> p c g n", p=128, g=3)
        pairs = self.cfg.get("pairs", list(range(8)))
        NRB = 4
        Rs = [A.alloc(BT) for _ in range(NRB)]
        b_Rs = [Buf(f"R{i}") for i in range(NRB)]
        NSP = 4
        sps = [A.alloc(BT) for _ in range(NSP)]
        b_sps = [Buf(f"sp{i}") for i in range(NSP)]

        def load_wp(pi):
            p = pairs[pi]
            s = pi % 2
            for g3 in range(3):
                fw.op("pool", lambda e, s=s, p=p, g3=g3: e.dma_start(out=wp[s][:, :, g3, :], in_=wqkv_d[:, :, g3, p * 128:(p + 1) * 128]),
                      writes=[b_wp[s]], chan=self.c_wp[s])

        def proj_tasks(pi):
            s = pi % 2
            tasks = []
            for tb in range(NBLK):
                for g3, dst, b_dst in ((0, qp, b_qp), (1, kp, b_kp)):
                    def tqk(tb=tb, s=s, g3=g3, dst=dst, b_dst=b_dst):
                        cols = slice(tb * BT, (tb + 1) * BT)
                        ps, b_ps = self.next_ps_proj()
                        self.mm_group(ps[:], [(wp[s][:, c, g3, :], hTf[:, c, cols]) for c in range(NCH)], [b_hTf, b_wp[s]], b_ps)
                        fw.op("dve", lambda e, ps=ps, s=s, cols=cols, dst=dst: e.tensor_copy(dst[s][:, cols], ps[:]), writes=[b_dst[s], b_ps])
                    tasks.append(tqk)
            for j0 in range(0, 32, 4):
                def tv(j0=j0, s=s):
                    ps, b_ps = self.next_ps_proj()
                    fns = []
                    for jj in range(4):
                        j = j0 + jj
                        for c in range(NCH):
                            fns.append(lambda e, ps=ps, jj=jj, j=j, c=c, s=s: e.matmul(ps[:, jj * 128:(jj + 1) * 128], lhsT=hTf[:, c, j * 128:(j + 1) * 128],
                                                                                      rhs=wp[s][:, c, 2, :], start=(c == 0), stop=(c == NCH - 1)))
                    fw.pe_group(fns, reads=[b_hTf, b_wp[s]], writes=[b_ps])
                    fw.op("dve", lambda e, ps=ps, s=s, j0=j0: e.tensor_copy(vp[s][:, j0:j0 + 4, :], ps[:].rearrange("p (j e) -> p j e", j=4)), writes=[b_vp[s], b_ps])
                tasks.append(tv)
            return tasks

        tiles = []
        for hh in range(2):
            for qb in range(NBLK):
                for kc in range(4 * qb + 3, -1, -1):
                    tiles.append((hh, qb, kc))
        NT = len(tiles)
        load_wp(0)
        for t in proj_tasks(0):
            t()
        gi = [0]
        for pi, p in enumerate(pairs):
            s = pi % 2
            extra = []
            if pi + 1 < len(pairs):
                load_wp(pi + 1)
                extra = proj_tasks(pi + 1)
            st = {}
            base = gi[0]

            def info(i):
                hh, qb, kc = tiles[i]
                nq0 = max(0, kc - 4 * qb) * 128
                return hh, qb, kc, nq0, slice(hh * 64, (hh + 1) * 64), slice(kc * 128, (kc + 1) * 128), slice(qb * BT + nq0, (qb + 1) * BT)

            def S1(i, s=s):
                hh, qb, kc, nq0, pr, kcs, qcs = info(i)
                psZ, b_psZ = self.next_ps_tile()
                st[i] = {"psZ": (psZ, b_psZ)}
                self.mm_group(psZ[:, nq0:BT], [(kp[s][pr, kcs], qp[s][pr, qcs])], [b_kp[s], b_qp[s]], b_psZ)

            def S2(i, s=s, base=base):
                hh, qb, kc, nq0, pr, kcs, qcs = info(i)
                g = base + i
                psZ, b_psZ = st[i]["psZ"]
                E, b_E = Es[g % NE], b_Es[g % NE]
                sp, b_sp = sps[g % NSP], b_sps[g % NSP]
                st[i]["sp"] = (sp, b_sp)
                last = 4 * qb + 3
                diag = kc >= 4 * qb
                fw.op("act", lambda e, psZ=psZ, E=E, nq0=nq0: e.activation(E[:, nq0:BT], psZ[:, nq0:BT], AF.Exp, scale=0.125), writes=[b_E, b_psZ])
                fw.op("act", lambda e, E=E, sp=sp, nq0=nq0: e.activation(sp[:, nq0:BT], E[:, nq0:BT], AF.Ln, bias=1.0), reads=[b_E], writes=[b_sp])
                st[i]["E"] = (E, b_E)
                if diag:
                    fw.op("pool", lambda e, sp=sp, nq0=nq0: e.tensor_tensor(out=sp[:, nq0:nq0 + 128], in0=sp[:, nq0:nq0 + 128], in1=self.triS[:], op=ALU.mult),
                          reads=[self.b_const], writes=[b_sp])
                    fw.op("pool", lambda e, E=E, nq0=nq0: e.tensor_tensor(out=E[:, nq0:nq0 + 128], in0=E[:, nq0:nq0 + 128], in1=self.triS[:], op=ALU.mult),
                          reads=[self.b_const], writes=[b_E])
                Rin, b_Rin = Rs[g % NRB], b_Rs[g % NRB]
                Rout, b_Rout = Rs[(g + 1) % NRB], b_Rs[(g + 1) % NRB]
                st[i]["Rin"] = (Rin, b_Rin)
                if kc != 0:
                    if kc == last:
                        fw.op("dve", lambda e, Rout=Rout, sp=sp, nq0=nq0: e.tensor_copy(Rout[:, nq0:BT], sp[:, nq0:BT]), reads=[b_sp], writes=[b_Rout])
                    else:
                        fw.op("dve", lambda e, Rout=Rout, Rin=Rin, sp=sp, nq0=nq0: e.tensor_tensor(out=Rout[:, nq0:BT], in0=Rin[:, nq0:BT], in1=sp[:, nq0:BT], op=ALU.add),
                              reads=[b_sp, b_Rin], writes=[b_Rout])
                    if kc > 4 * qb:
                        fw.op("pool", lambda e, Rout=Rout, nq0=nq0: e.memset(Rout[:, nq0 - 128:nq0], 0.0), writes=[b_Rout])

            def S3(i, s=s):
                hh, qb, kc, nq0, pr, kcs, qcs = info(i)
                sp, b_sp = st[i]["sp"]
                Rin, b_Rin = st[i]["Rin"]
                psL, b_psL = self.next_ps_tile()
                st[i]["psL"] = (psL, b_psL)
                prs = [(self.Uinc[:], sp[:, nq0:BT])]
                rd = [b_sp, self.b_const]
                if kc != 4 * qb + 3:
                    prs.append((self.neg8[:], Rin[:, nq0:BT]))
                    rd.append(b_Rin)
                self.mm_group(psL[:, nq0:BT], prs, rd, b_psL)

            def S4(i, s=s, base=base):
                hh, qb, kc, nq0, pr, kcs, qcs = info(i)
                g = base + i
                psL, b_psL = st[i]["psL"]
                at, b_at = ats[g % NS], b_ats[g % NS]
                st[i]["at"] = (at, b_at)
                E, b_E = st[i]["E"]
                fw.op("act", lambda e, psL=psL, at=at, nq0=nq0: e.activation(at[:, nq0:BT], psL[:, nq0:BT], AF.Exp, scale=0.125), writes=[b_at, b_psL])
                fw.op("dve", lambda e, at=at, E=E, nq0=nq0: e.tensor_tensor(out=at[:, nq0:BT], in0=at[:, nq0:BT], in1=E[:, nq0:BT], op=ALU.mult),
                      reads=[b_E], writes=[b_at])

            cur = {}

            def S5(i, s=s, p=p):
                hh, qb, kc, nq0, pr, kcs, qcs = info(i)
                last = 4 * qb + 3
                at, b_at = st[i]["at"]
                del st[i]
                if kc == last:
                    cur[(hh, qb)] = self.next_acc()
                po, b_po = cur[(hh, qb)]
                fw.pe_group([lambda e, po=po, at=at, nq0=nq0, kc=kc, last=last, s=s: e.matmul(po[:, nq0:BT], lhsT=vp[s][:, kc, :], rhs=at[:, nq0:BT],
                                                                                             start=(kc == last), stop=(kc == 0))],
                            reads=[b_vp[s], b_at], writes=[b_po])
                if kc == 0:
                    qc = slice(qb * BT, (qb + 1) * BT)
                    fw.op("dve", lambda e, po=po, p=p, pr=pr, qc=qc: e.tensor_copy(oT[pr, p, qc], po[pr, :]), writes=[b_oT, b_po])

            self.run_pipeline(NT, [S1, S2, S3, S4, S5], [0, 1, 2, 3, 4], extra)
            gi[0] += NT
        A.release(mAB)
        fw.barrier()
        self.stage_c(l, oT, b_oT, "w_o")
        A.release_top()
        A.release(m0)


_NC_CACHE = {}


def build_nc(cfg=None):
    key = repr(sorted((cfg or {}).items()))
    if key not in _NC_CACHE:
        _NC_CACHE[key] = Prog(cfg).build()
    return _NC_CACHE[key]


def kernel(**inputs):
    nc = build_nc(None)
    x = np.ascontiguousarray(inputs["x"], dtype=np.float32)
    pos = np.ascontiguousarray(inputs["positions"], dtype=np.int32)
    shared = {k: np.ascontiguousarray(v, dtype=np.float32) for k, v in inputs.items() if k not in ("x", "positions")}
    in_maps = []
    for b in range(N_CORES):
        d = dict(shared)
        d["x"] = x[b]
        d["positions"] = pos[b:b + 1]
        in_maps.append(d)
    res = run_bass_kernel_spmd(nc, in_maps, core_ids=list(range(N_CORES)))
    return np.stack([np.asarray(r["out"], dtype=np.float32) for r in res.results], axis=0)
```

```python
import math
from contextlib import ExitStack
import numpy as np
import concourse.bass as bass
import concourse.mybir as mybir
from concourse.bass_utils import run_bass_kernel_spmd

F32 = mybir.dt.float32
BF16 = mybir.dt.bfloat16
I32 = mybir.dt.int32
AF = mybir.ActivationFunctionType
ALU = mybir.AluOpType

T = 4096
D = 1024
BT = 512
NBLK = T // BT
NCH = D // 128
DFF = 4096
EPS = 1e-6
N_CORES = 8

MLA_LAYERS = (0, 3)
WNAMES = {
    0: ["w_dq", "norm_q", "w_uq", "w_dkv", "norm_kv", "w_uk", "w_uv", "w_o"],
    1: ["w_in", "conv_w", "conv_b", "w_out"],
    2: ["w_qkv", "w_o"],
    3: ["w_dq", "norm_q", "w_uq", "w_dkv", "norm_kv", "w_uk", "w_uv", "w_o"],
}
WSHAPES = {
    "w_dq": [1024, 384], "norm_q": [384], "w_uq": [384, 1536], "w_dkv": [1024, 288], "norm_kv": [256],
    "w_uk": [256, 1024], "w_uv": [256, 1024], "w_o": [1024, 1024], "w_in": [1024, 3072],
    "conv_w": [3, 1024], "conv_b": [1024], "w_out": [1024, 1024], "w_qkv": [1024, 3072],
    "norm_mix": [1024], "norm_mlp": [1024], "w_up": [1024, 4096], "w_down": [4096, 1024],
}


class Buf:
    __slots__ = ("name", "w", "r")

    def __init__(self, name):
        self.name = name
        self.w = None
        self.r = {}


class Chan:
    __slots__ = ("sem", "count")

    def __init__(self):
        self.sem = None
        self.count = 0


class Stream:
    def __init__(self, name):
        self.name = name
        self.items = []
        self.nops = 0
        self.seen = {}
        self.sem = None
        self.referenced = set()
        self.rank = {}


class FW:
    def __init__(self):
        self.streams = {n: Stream(n) for n in ("pe", "act", "dve", "pool", "sp")}
        self.chans = []

    def _need(self, st, tok):
        if tok is None:
            return
        if tok[0] == 'E':
            src = tok[1]
            if src is st and st.name in ("pe", "sp"):
                return
            key = src.name
        else:
            key = id(tok[1])
        val = tok[2]
        if st.seen.get(key, -1) >= val:
            return
        st.seen[key] = val
        st.items.append(('wait', tok))
        if tok[0] == 'E':
            src.referenced.add(val)

    def _deps(self, st, reads, writes):
        for b in reads:
            self._need(st, b.w)
        for b in writes:
            self._need(st, b.w)
            for t in b.r.values():
                self._need(st, t)

    def _commit(self, tok, key, reads, writes):
        for b in reads:
            b.r[key] = tok
        for b in writes:
            b.w = tok
            b.r = {}

    def op(self, eng, fn, reads=(), writes=(), chan=None):
        st = self.streams[eng]
        self._deps(st, reads, writes)
        if chan is not None:
            chan.count += 16
            tok = ('C', chan, chan.count)
            st.items.append(('dma', fn, chan))
            key = id(chan)
        else:
            st.nops += 1
            tok = ('E', st, st.nops)
            st.items.append(('op', fn, st.nops))
            key = st.name
        self._commit(tok, key, reads, writes)
        return tok

    def pe_group(self, fns, reads=(), writes=()):
        st = self.streams["pe"]
        self._deps(st, reads, writes)
        for fn in fns[:-1]:
            st.items.append(('op', fn, None))
        st.nops += 1
        tok = ('E', st, st.nops)
        st.items.append(('op', fns[-1], st.nops))
        self._commit(tok, "pe", reads, writes)
        return tok

    def new_chan(self):
        c = Chan()
        self.chans.append(c)
        return c

    def wait_all(self, eng, bufs):
        st = self.streams[eng]
        for b in bufs:
            self._need(st, b.w)
            for t in b.r.values():
                self._need(st, t)

    def barrier(self):
        toks = []
        for st in self.streams.values():
            if st.nops > 0:
                toks.append(('E', st, st.nops))
        for c in self.chans:
            if c.count > 0:
                toks.append(('C', c, c.count))
        for st in self.streams.values():
            for t in toks:
                self._need(st, t)

    def n_sems(self):
        return len(self.streams) + len(self.chans)

    def replay(self, block, sems):
        it = iter(sems)
        for st in self.streams.values():
            st.sem = next(it)
        for c in self.chans:
            c.sem = next(it)
        for st in self.streams.values():
            st.rank = {idx: i + 1 for i, idx in enumerate(sorted(st.referenced))}

        def run(st, h):
            for item in st.items:
                if item[0] == 'wait':
                    tok = item[1]
                    if tok[0] == 'E':
                        h.wait_ge(tok[1].sem, tok[1].rank[tok[2]])
                    else:
                        h.wait_ge(tok[1].sem, tok[2])
                elif item[0] == 'dma':
                    item[1](h).then_inc(item[2].sem, 16)
                else:
                    ins = item[1](h)
                    if item[2] is not None and item[2] in st.rank:
                        ins.then_inc(st.sem, 1)

        S = self.streams
        block.tensor(lambda h: run(S["pe"], h))
        block.scalar(lambda h: run(S["act"], h))
        block.vector(lambda h: run(S["dve"], h))
        block.gpsimd(lambda h: run(S["pool"], h))
        block.sync(lambda h: run(S["sp"], h))


class Arena:
    def __init__(self, ap):
        self.ap = ap
        self.n = ap.shape[1]
        self.off = 0
        self.peak = 0
        self.top = self.n

    def alloc(self, nelem, dt=BF16):
        nb = nelem * (4 if dt in (F32, I32) else 2)
        n16 = (nb + 31) // 32 * 16
        s = self.off
        self.off += n16
        self.peak = max(self.peak, self.off)
        assert self.off <= self.top, f"arena overflow {self.off}>{self.top}"
        v = self.ap[:, s:s + nb // 2]
        if dt != BF16:
            v = v.bitcast(dt)
        return v

    def alloc_top(self, nelem):
        self.top -= (nelem + 15) // 16 * 16
        assert self.off <= self.top, f"arena overflow(top) {self.off}>{self.top}"
        return self.ap[:, self.top:self.top + nelem]

    def release_top(self):
        self.top = self.n

    def mark(self):
        return self.off

    def release(self, m):
        self.off = m


class Prog:
    def __init__(self, cfg=None):
        self.cfg = cfg or {}
        self.nc = bass.Bass("TRN2", target_bir_lowering=False)
        self.fw = FW()

    def next_ps(self):
        i = self.ps_i
        self.ps_i = (self.ps_i + 1) % len(self.ps_gen)
        return self.ps_gen[i]

    def next_ps_tile(self):
        i = self.pst_i
        self.pst_i = (self.pst_i + 1) % 4
        return self.ps_all[2 + i]

    def next_ps_proj(self):
        i = self.psp_i
        self.psp_i = (self.psp_i + 1) % 2
        return self.ps_all[6 + i]

    def next_acc(self):
        i = self.acc_i
        self.acc_i = (self.acc_i + 1) % len(self.ps_acc)
        return self.ps_acc[i]

    def run_pipeline(self, N, stages, skews, extra):
        total = N + max(skews)
        per = max(1, N // (len(extra) + 1)) if extra else 0
        ti = 0
        for n in range(total):
            for fn, sk in zip(stages, skews):
                i = n - sk
                if 0 <= i < N:
                    fn(i)
            if extra and ti < len(extra) and n % per == per - 1:
                extra[ti]()
                ti += 1
        while ti < len(extra):
            extra[ti]()
            ti += 1

    def mm_group(self, out_ap, pairs, reads, ps_buf):
        n = len(pairs)
        fns = []
        for i, (l, r) in enumerate(pairs):
            fns.append(lambda e, l=l, r=r, i=i: e.matmul(out_ap, lhsT=l, rhs=r, start=(i == 0), stop=(i == n - 1)))
        return self.fw.pe_group(fns, reads=reads, writes=[ps_buf])

    def load_weight(self, dst3, src2, buf, chan, ncols_piece=None):
        self.fw.op("pool", lambda e: e.dma_start(out=dst3, in_=src2.rearrange("(c p) n -> p c n", p=128)),
                   writes=[buf], chan=chan)

    def rms_feature_major(self, xap, nch, width, gcol, out_fn, sq, b_sq, rstd, b_rstd, b_x, b_out, n_feat):
        fw = self.fw
        fw.op("act", lambda e: e.activation(sq[:, 0:nch, 0:width], xap, AF.Square), reads=[b_x], writes=[b_sq])
        ps, b_ps = self.next_ps()
        self.mm_group(ps[:, 0:width], [(self.ones[:], sq[:, c, 0:width]) for c in range(nch)], [b_sq, self.b_const], b_ps)
        fw.op("act", lambda e: e.activation(rstd[:, 0:width], ps[:, 0:width], AF.Sqrt, scale=1.0 / n_feat, bias=self.eps_col[:, 0:1]),
              reads=[], writes=[b_rstd, b_ps])
        fw.op("dve", lambda e: e.reciprocal(rstd[:, 0:width], rstd[:, 0:width]), reads=[], writes=[b_rstd])
        for c in range(nch):
            dst = out_fn(c)
            gsc = gcol(c)
            fw.op("dve", lambda e, c=c, dst=dst, gsc=gsc: e.scalar_tensor_tensor(out=dst, in0=xap[:, c, :], scalar=gsc, in1=rstd[:, 0:width],
                                                               op0=ALU.mult, op1=ALU.mult),
                  reads=[b_x, b_rstd, self.b_const], writes=[b_out])

    def rms_part1(self, xap, nch, width, sq, b_sq, b_x):
        fw = self.fw
        fw.op("act", lambda e: e.activation(sq[:, 0:nch, 0:width], xap, AF.Square), reads=[b_x], writes=[b_sq])
        ps, b_ps = self.next_ps()
        self.mm_group(ps[:, 0:width], [(self.ones[:], sq[:, c, 0:width]) for c in range(nch)], [b_sq, self.b_const], b_ps)
        return ps, b_ps

    def rms_part2(self, psb, xap, nch, width, gcol, out_fn, rstd, b_rstd, b_x, b_out, n_feat):
        fw = self.fw
        ps, b_ps = psb
        fw.op("act", lambda e: e.activation(rstd[:, 0:width], ps[:, 0:width], AF.Sqrt, scale=1.0 / n_feat, bias=self.eps_col[:, 0:1]),
              reads=[], writes=[b_rstd, b_ps])
        fw.op("dve", lambda e: e.reciprocal(rstd[:, 0:width], rstd[:, 0:width]), reads=[], writes=[b_rstd])
        for c in range(nch):
            dst = out_fn(c)
            gsc = gcol(c)
            fw.op("dve", lambda e, c=c, dst=dst, gsc=gsc: e.scalar_tensor_tensor(out=dst, in0=xap[:, c, :], scalar=gsc, in1=rstd[:, 0:width],
                                                                               op0=ALU.mult, op1=ALU.mult),
                  reads=[b_x, b_rstd, self.b_const], writes=[b_out])

    def build(self):
        nc, fw = self.nc, self.fw
        cfg = self.cfg
        self.es = es = ExitStack()
        with es:
            self.x_in = nc.dram_tensor("x", [T, D], F32, kind="ExternalInput").ap()
            self.pos_in = nc.dram_tensor("positions", [1, T], I32, kind="ExternalInput").ap()
            self.W = {}
            for l in range(4):
                for nm in ["norm_mix"] + WNAMES[l] + ["norm_mlp", "w_up", "w_down"]:
                    full = f"l{l}_{nm}"
                    self.W[full] = nc.dram_tensor(full, WSHAPES[nm], F32, kind="ExternalInput").ap()
            self.W["final_norm"] = nc.dram_tensor("final_norm", [D], F32, kind="ExternalInput").ap()
            self.out = nc.dram_tensor("out", [T, D], F32, kind="ExternalOutput").ap()
            self.xT_d = nc.dram_tensor("xT_scratch", [NCH, 128, T], F32).ap()
            self.b_xd = [Buf(f"xd{b}") for b in range(NBLK)]
            self.c_xd = [fw.new_chan() for _ in range(NBLK)]
            self.b_outd = Buf("outd")
            self.c_outd = fw.new_chan()
            self.dbg = []
            if cfg.get("dbg"):
                for k in range(7):
                    self.dbg.append(nc.dram_tensor(f"dbg{k}", [NCH, 128, T], F32, kind="ExternalOutput").ap())
            self.dbg_i = 0

            self.ps_all = []
            for i in range(8):
                t = es.enter_context(nc.psum_tensor(f"ps{i}", [128, 512], F32))
                self.ps_all.append((t, Buf(f"ps{i}")))
            self.ps_acc = self.ps_all[0:2]
            self.ps_gen = self.ps_all[2:8]
            self.ps_i = 0
            self.acc_i = 0
            self.pst_i = 0
            self.psp_i = 0

            def sb(name, shape, dt):
                return es.enter_context(nc.sbuf_tensor(name, shape, dt))
            self.ones = sb("ones", [128, 128], BF16)
            self.ident = sb("ident", [128, 128], F32)
            self.triI = sb("triI", [128, 128], BF16)
            self.triS = sb("triS", [128, 128], BF16)
            self.Uinc = sb("Uinc", [128, 128], BF16)
            self.neg8 = sb("neg8", [128, 128], BF16)
            self.identb = sb("identb", [128, 128], BF16)
            self.mnegI = sb("mnegI", [128, 128], BF16)
            self.mnegS = sb("mnegS", [128, 128], BF16)
            self.gains = sb("gains", [128, NCH, 32], F32)
            self.eps_col = sb("eps_col", [128, 1], F32)
            self.negpi = sb("negpi", [128, 1], F32)
            self.b_const = Buf("const")
            self.b_TAB = Buf("TAB")
            arena_elems = (int(nc.sbuf_bytes_remaining) - 1024) // 64 * 32
            print("arena KiB", arena_elems * 2 / 1024)
            self.arena = Arena(sb("arena", [128, arena_elems], BF16))
            self.b_arena_guard = Buf("arena")

            self.setup_consts()
            fw.barrier()
            self.prologue()
            fw.barrier()
            layers = cfg.get("layers", [0, 1, 2, 3])
            for l in layers:
                if cfg.get("mix", True):
                    if l in MLA_LAYERS:
                        self.mla_phase(l)
                    elif l == 1:
                        self.conv_phase(l)
                    else:
                        self.sb_phase(l)
                    fw.barrier()
                    self.dump_dbg()
                if cfg.get("ffn", True):
                    self.ffn_phase(l, final=(l == layers[-1]) and cfg.get("final", True))
                    fw.barrier()
                    if not getattr(self, "_emitted_final", False):
                        self.dump_dbg()
            if not getattr(self, "_emitted_final", False):
                self.dump_xT()
            fw.wait_all("sp", [self.b_outd])
            print("n_sems", fw.n_sems())
            sems = [es.enter_context(nc.semaphore(f"s{i}")) for i in range(fw.n_sems())]
            block = es.enter_context(nc.Block())
            fw.replay(block, sems)
        return nc

    def setup_consts(self):
        nc, fw, A = self.nc, self.fw, self.arena
        m = A.mark()
        bc = self.b_const
        iot = A.alloc(128, I32)
        b_t = Buf("ctmp")
        fw.op("pool", lambda e: e.iota(iot, pattern=[[1, 128]], base=0, channel_multiplier=-1), writes=[b_t])
        fw.op("dve", lambda e: e.tensor_single_scalar(self.ident[:], iot, 0, ALU.is_equal), reads=[b_t], writes=[bc])
        fw.op("dve", lambda e: e.tensor_single_scalar(self.triI[:], iot, 0, ALU.is_ge), reads=[b_t], writes=[bc])
        fw.op("dve", lambda e: e.tensor_single_scalar(self.triS[:], iot, 0, ALU.is_gt), reads=[b_t], writes=[bc])
        fw.op("dve", lambda e: e.tensor_scalar(self.Uinc[:], iot, 0, -8.0, op0=ALU.is_le, op1=ALU.mult), reads=[b_t], writes=[bc])
        fw.op("dve", lambda e: e.tensor_single_scalar(self.identb[:], iot, 0, ALU.is_equal), reads=[b_t], writes=[bc])
        fw.op("dve", lambda e: e.tensor_scalar(self.mnegI[:], iot, 0, -30000.0, op0=ALU.is_lt, op1=ALU.mult), reads=[b_t], writes=[bc])
        fw.op("dve", lambda e: e.tensor_scalar(self.mnegS[:], iot, 0, -30000.0, op0=ALU.is_le, op1=ALU.mult), reads=[b_t], writes=[bc])
        fw.op("dve", lambda e: e.memset(self.ones[:], 1.0), writes=[bc])
        fw.op("dve", lambda e: e.memset(self.neg8[:], -8.0), writes=[bc])
        fw.op("dve", lambda e: e.memset(self.eps_col[:], EPS), writes=[bc])
        fw.op("dve", lambda e: e.memset(self.negpi[:], -math.pi), writes=[bc])
        gv = A.alloc(1024, F32)
        b_gv = Buf("gv")
        c_gv = fw.new_chan()
        fw.op("dve", lambda e: e.memset(gv[0:32, :], 0.0), writes=[b_gv])
        rows = []
        for l in range(4):
            rows.append((l, f"l{l}_norm_mix", 1024))
            rows.append((4 + l, f"l{l}_norm_mlp", 1024))
        rows.append((8, "final_norm", 1024))
        rows.append((12, "l1_conv_b", 1024))
        rows.append((13, "l0_norm_q", 384))
        rows.append((14, "l3_norm_q", 384))
        rows.append((15, "l0_norm_kv", 256))
        rows.append((16, "l3_norm_kv", 256))
        for r, nm, n in rows:
            fw.op("sp", lambda e, r=r, nm=nm, n=n: e.dma_start(out=gv[r:r + 1, 0:n], in_=self.W[nm].rearrange("(o n) -> o n", o=1)),
                  writes=[b_gv], chan=c_gv)
        fw.op("sp", lambda e: e.dma_start(out=gv[9:12, :], in_=self.W["l1_conv_w"]), writes=[b_gv], chan=c_gv)
        ps, b_ps = self.next_ps()
        fns = [lambda e, c=c: e.transpose(ps[:, c * 32:(c + 1) * 32], gv[0:32, c * 128:(c + 1) * 128], self.ident[0:32, 0:32]) for c in range(NCH)]
        fw.pe_group(fns, reads=[b_gv, bc], writes=[b_ps])
        fw.op("dve", lambda e: e.tensor_copy(self.gains[:].rearrange("p c r -> p (c r)"), ps[:, 0:256]), reads=[], writes=[bc, b_ps])
        self._const_mark = m

    def build_tables(self):
        fw, A = self.fw, self.arena
        bc = self.b_const
        b_t = Buf("ttmp")
        TAB = self.TAB = A.alloc(T, F32)
        m = A.mark()
        posi = A.alloc(T, I32)
        ang = A.alloc(T, F32)
        tq = A.alloc(T, F32)
        idx = A.alloc(1, I32)
        cf = A.alloc(1, F32)
        s1 = A.alloc(1, F32)
        invf = A.alloc(1, F32)
        off = A.alloc(1, F32)
        b_pos = Buf("pos")
        if not hasattr(self, "c_pos"):
            self.c_pos = fw.new_chan()
        c_pos = self.c_pos
        R = slice(64, 128)
        TWO_PI = 2.0 * math.pi
        fw.op("sp", lambda e: e.dma_start(out=posi[R, :], in_=self.pos_in[0, :].partition_broadcast(64)), writes=[b_pos], chan=c_pos)
        fw.op("pool", lambda e: e.iota(idx[R, :], pattern=[[0, 1]], base=0, channel_multiplier=1), writes=[b_t])
        fw.op("dve", lambda e: e.tensor_copy(cf[R, :], idx[R, :]), reads=[b_t], writes=[b_t])
        fw.op("dve", lambda e: e.tensor_copy(invf[R, :], cf[R, :]), writes=[b_t])
        for thr in (16.0, 32.0, 48.0):
            fw.op("dve", lambda e, thr=thr: e.tensor_scalar(s1[R, :], cf[R, :], thr, 16.0, op0=ALU.is_ge, op1=ALU.mult), writes=[b_t])
            fw.op("dve", lambda e: e.tensor_tensor(out=invf[R, :], in0=invf[R, :], in1=s1[R, :], op=ALU.subtract), writes=[b_t])
        fw.op("act", lambda e: e.activation(invf[R, :], invf[R, :], AF.Exp, scale=-math.log(10000.0) / 16.0), writes=[b_t])
        fw.op("dve", lambda e: e.memset(off[64:96, :], 0.5 * math.pi), writes=[b_t])
        fw.op("dve", lambda e: e.memset(off[96:128, :], 0.0), writes=[b_t])
        fw.op("dve", lambda e: e.memset(off[96:112, :], math.pi), writes=[b_t])
        fw.op("dve", lambda e: e.tensor_copy(ang[R, :], posi[R, :]), reads=[b_pos], writes=[b_t])
        fw.op("dve", lambda e: e.tensor_scalar(ang[R, :], ang[R, :], invf[R, 0:1], off[R, 0:1], op0=ALU.mult, op1=ALU.add), writes=[b_t])
        fw.op("dve", lambda e: e.tensor_scalar(tq[R, :], ang[R, :], 1.0 / TWO_PI, None, op0=ALU.mult), writes=[b_t])
        fw.op("dve", lambda e: e.tensor_copy(posi[R, :], tq[R, :]), writes=[b_t])
        fw.op("dve", lambda e: e.tensor_copy(tq[R, :], posi[R, :]), writes=[b_t])
        fw.op("dve", lambda e: e.scalar_tensor_tensor(out=ang[R, :], in0=tq[R, :], scalar=-TWO_PI, in1=ang[R, :], op0=ALU.mult, op1=ALU.add), writes=[b_t])
        fw.op("dve", lambda e: e.tensor_scalar(tq[R, :], ang[R, :], math.pi, TWO_PI, op0=ALU.is_ge, op1=ALU.mult), writes=[b_t])
        fw.op("dve", lambda e: e.tensor_tensor(out=ang[R, :], in0=ang[R, :], in1=tq[R, :], op=ALU.subtract), writes=[b_t])
        fw.op("dve", lambda e: e.tensor_scalar(tq[R, :], ang[R, :], -math.pi, TWO_PI, op0=ALU.is_lt, op1=ALU.mult), writes=[b_t])
        fw.op("dve", lambda e: e.tensor_tensor(out=ang[R, :], in0=ang[R, :], in1=tq[R, :], op=ALU.add), writes=[b_t])
        fw.op("act", lambda e: e.activation(TAB[R, :], ang[R, :], AF.Sin), reads=[b_t, bc], writes=[self.b_TAB])
        fw.barrier()
        A.release(m)

    def dump_dbg(self):
        if not self.dbg:
            return
        d = self.dbg[self.dbg_i]
        self.dbg_i += 1
        self.fw.op("sp", lambda e: e.dma_start(out=d[:, :, :], in_=self.xT_d[:, :, :]), reads=self.b_xd, writes=[self.b_outd], chan=self.c_outd)
        self.fw.barrier()

    def sdump(self, name, ap, buf):
        if not self.cfg.get("sdump"):
            return
        shp = list(ap.shape)
        d = self.nc.dram_tensor("sd_" + name, shp, ap.dtype, kind="ExternalOutput").ap()
        self.fw.barrier()
        self.fw.op("sp", lambda e: e.dma_start(out=d, in_=ap), reads=[buf], writes=[self.b_outd], chan=self.c_outd)
        self.fw.barrier()

    def gcol(self, row):
        return lambda c: self.gains[:, c, row:row + 1]

    def prologue(self):
        fw, A = self.fw, self.arena
        A.release(self._const_mark)
        m = A.mark()
        xin = [A.alloc(4 * D, F32).rearrange("p (t d) -> p t d", t=4) for _ in range(2)]
        xb = [A.alloc(NCH * BT, F32).rearrange("p (c t) -> p c t", c=NCH) for _ in range(2)]
        b_xin = [Buf("xin0"), Buf("xin1")]
        c_xin = [fw.new_chan(), fw.new_chan()]
        b_xb = [Buf("pxb0"), Buf("pxb1")]
        for blk in range(NBLK):
            s = blk % 2
            fw.op("sp", lambda e, s=s, blk=blk: e.dma_start(out=xin[s], in_=self.x_in[blk * BT:(blk + 1) * BT, :].rearrange("(t p) d -> p t d", p=128)),
                  writes=[b_xin[s]], chan=c_xin[s])
            for c in range(NCH):
                ps, b_ps = self.next_ps()
                fns = [lambda e, tt=tt, c=c, s=s, ps=ps: e.transpose(ps[:, tt * 128:(tt + 1) * 128], xin[s][:, tt, c * 128:(c + 1) * 128], self.ident[:])
                       for tt in range(4)]
                fw.pe_group(fns, reads=[b_xin[s], self.b_const], writes=[b_ps])
                eng = "dve" if c % 2 == 0 else "act"
                if eng == "dve":
                    fw.op("dve", lambda e, c=c, s=s, ps=ps: e.tensor_copy(xb[s][:, c, :], ps[:]), writes=[b_xb[s], b_ps])
                else:
                    fw.op("act", lambda e, c=c, s=s, ps=ps: e.copy(xb[s][:, c, :], ps[:]), writes=[b_xb[s], b_ps])
            self.store_xblk(xb[s], b_xb[s], blk)
        A.release(m)

    def xd_view(self, blk):
        return self.xT_d[:, :, blk * BT:(blk + 1) * BT].rearrange("c p t -> p c t")

    def store_xblk(self, xb, b_xb, blk):
        self.fw.op("sp", lambda e: e.dma_start(out=self.xd_view(blk), in_=xb), reads=[b_xb], writes=[self.b_xd[blk]], chan=self.c_xd[blk])

    def load_xblk(self, xb, b_xb, c_xb, blk):
        self.fw.op("sp", lambda e: e.dma_start(out=xb, in_=self.xd_view(blk)), reads=[self.b_xd[blk]], writes=[b_xb], chan=c_xb)

    def dump_xT(self):
        fw, A = self.fw, self.arena
        m = A.mark()
        xb = A.alloc(NCH * BT, F32).rearrange("p (c t) -> p c t", c=NCH)
        b_xb, c_xb = Buf("dxb"), fw.new_chan()
        for blk in range(NBLK):
            self.load_xblk(xb, b_xb, c_xb, blk)
            self.emit_output(xb, b_xb, blk)
        A.release(m)

    def emit_output(self, yb, b_yb, blk, ost=None, b_ost=None):
        fw, A = self.fw, self.arena
        m = A.mark()
        if ost is None:
            ost = A.alloc(4 * D, F32).rearrange("p (t d) -> p t d", t=4)
            b_ost = self.b_ost if hasattr(self, "b_ost") else Buf("ost")
            self.b_ost = b_ost
        for tt in range(4):
            for half in range(2):
                ps, b_ps = self.next_ps()
                fns = [lambda e, tt=tt, c=c, ps=ps: e.transpose(ps[:, (c % 4) * 128:(c % 4 + 1) * 128], yb[:, c, tt * 128:(tt + 1) * 128], self.ident[:])
                       for c in range(half * 4, half * 4 + 4)]
                fw.pe_group(fns, reads=[b_yb, self.b_const], writes=[b_ps])
                if (tt + half) % 2 == 0:
                    fw.op("dve", lambda e, tt=tt, half=half, ps=ps: e.tensor_copy(ost[:, tt, half * 512:(half + 1) * 512], ps[:]), writes=[b_ost, b_ps])
                else:
                    fw.op("act", lambda e, tt=tt, half=half, ps=ps: e.copy(ost[:, tt, half * 512:(half + 1) * 512], ps[:]), writes=[b_ost, b_ps])
        fw.op("sp", lambda e: e.dma_start(out=self.out[blk * BT:(blk + 1) * BT, :].rearrange("(t p) d -> p t d", p=128), in_=ost),
              reads=[b_ost], writes=[self.b_outd], chan=self.c_outd)
        A.release(m)

    def alloc_load_wup(self, l, at=None):
        fw, A = self.fw, self.arena
        if at is None:
            off0 = A.off
            wup = A.alloc(NCH * DFF).rearrange("p (c n) -> p c n", c=NCH)
        else:
            off0 = at[0]
            wup = at[1].rearrange("p (c n) -> p c n", c=NCH)
        NP = 4
        b_wup = [Buf(f"wup{i}") for i in range(NP)]
        if not hasattr(self, "c_wup"):
            self.c_wup = [fw.new_chan() for _ in range(NP)]
            self.c_wdn = [fw.new_chan() for _ in range(NP)]
        wu_d = self.W[f"l{l}_w_up"].rearrange("(c p) n -> p c n", p=128)
        for i in range(NP):
            fw.op("pool", lambda e, i=i: e.dma_start(out=wup[:, :, i * 1024:(i + 1) * 1024], in_=wu_d[:, :, i * 1024:(i + 1) * 1024]),
                  writes=[b_wup[i]], chan=self.c_wup[i])
        return (off0, wup, b_wup)

    def ffn_phase(self, l, final=False):
        fw, A = self.fw, self.arena
        if final:
            self._emitted_final = True
        m = A.mark()
        NP = 4
        pre = getattr(self, "pre_wup", None)
        self.pre_wup = None
        if pre is not None and pre[0] == A.off:
            _, wup, b_wup = pre
            A.alloc(NCH * DFF)
        else:
            assert pre is None, "prefetched w_up at unexpected offset"
            _, wup, b_wup = self.alloc_load_wup(l)
        wdn = A.alloc(32 * D).rearrange("p (f n) -> p f n", f=32)
        b_wdn = [Buf(f"wdn{i}") for i in range(NP)]
        wd_d = self.W[f"l{l}_w_down"].rearrange("(f p) n -> p f n", p=128)
        for i in range(NP):
            fw.op("pool", lambda e, i=i: e.dma_start(out=wdn[:, i * 8:(i + 1) * 8, :], in_=wd_d[:, i * 8:(i + 1) * 8, :]),
                  writes=[b_wdn[i]], chan=self.c_wdn[i])
        xbs = [A.alloc(NCH * BT, F32).rearrange("p (c t) -> p c t", c=NCH) for _ in range(2)]
        b_xbs = [Buf("fxb0"), Buf("fxb1")]
        if not hasattr(self, "c_fxb"):
            self.c_fxb = [fw.new_chan(), fw.new_chan()]
        hT = A.alloc(NCH * BT).rearrange("p (c t) -> p c t", c=NCH)
        b_hT = Buf("hT")
        aT_raw = A.alloc(32 * BT)
        aT = aT_raw.rearrange("p (f t) -> p f t", f=32)
        b_aTf = [Buf(f"aT{f}") for f in range(32)]
        rstd = A.alloc(BT, F32)
        b_rstd = Buf("rstd")
        yb = aT_raw[:, 0:2 * NCH * BT].bitcast(F32).rearrange("p (c t) -> p c t", c=NCH)
        ost = aT_raw[:, 2 * NCH * BT:4 * NCH * BT].bitcast(F32).rearrange("p (t d) -> p t d", t=4)
        sqf = aT_raw[:, 2 * NCH * BT:3 * NCH * BT].rearrange("p (c t) -> p c t", c=NCH)
        print(f"[ffn {l}] arena peak {A.peak * 2 / 1024:.1f} KiB")

        def norm(blk):
            s = blk % 2
            self.rms_feature_major(xbs[s], NCH, BT, self.gcol(4 + l), lambda c: hT[:, c, :], hT, b_hT, rstd, b_rstd, b_xbs[s], b_hT, D)

        self.load_xblk(xbs[0], b_xbs[0], self.c_fxb[0], 0)
        norm(0)
        for blk in range(NBLK):
            s = blk % 2
            xb, b_xb = xbs[s], b_xbs[s]
            if blk + 1 < NBLK:
                self.load_xblk(xbs[1 - s], b_xbs[1 - s], self.c_fxb[1 - s], blk + 1)
            for f in range(32):
                ps, b_ps = self.next_ps()
                self.mm_group(ps[:], [(wup[:, c, f * 128:(f + 1) * 128], hT[:, c, :]) for c in range(NCH)], [b_hT, b_wup[f // 8]], b_ps)
                fw.op("act", lambda e, ps=ps, f=f: e.activation(aT[:, f, :], ps[:], AF.Relu), writes=[b_aTf[f], b_ps])
                fw.op("dve", lambda e, f=f: e.tensor_tensor(out=aT[:, f, :], in0=aT[:, f, :], in1=aT[:, f, :], op=ALU.mult), writes=[b_aTf[f]])
            for c in range(NCH):
                ps, b_ps = self.next_ps()
                self.mm_group(ps[:], [(wdn[:, f, c * 128:(c + 1) * 128], aT[:, f, :]) for f in range(32)], b_aTf + b_wdn, b_ps)
                fw.op("dve", lambda e, c=c, ps=ps, xb=xb: e.tensor_tensor(out=xb[:, c, :], in0=xb[:, c, :], in1=ps[:], op=ALU.add), writes=[b_xb, b_ps])
                if c == 1 and blk + 1 < NBLK:
                    norm(blk + 1)
            if final:
                b_sqf = b_aTf[16:24]
                b_rsf = b_aTf[28:30]
                fw.op("act", lambda e, xb=xb: e.activation(sqf, xb, AF.Square), reads=[b_xb], writes=b_sqf)
                ps, b_ps = self.next_ps()
                self.mm_group(ps[:], [(self.ones[:], sqf[:, c, :]) for c in range(NCH)], b_sqf + [self.b_const], b_ps)
                rstd_f = ost[:, 3, 0:BT]
                fw.op("act", lambda e, ps=ps: e.activation(rstd_f, ps[:], AF.Sqrt, scale=1.0 / D, bias=self.eps_col[:, 0:1]), writes=b_rsf + [b_ps])
                fw.op("dve", lambda e: e.reciprocal(rstd_f, rstd_f), writes=b_rsf)
                for c in range(NCH):
                    fw.op("dve", lambda e, c=c, xb=xb: e.scalar_tensor_tensor(out=yb[:, c, :], in0=xb[:, c, :], scalar=self.gains[:, c, 8:9], in1=rstd_f,
                                                                              op0=ALU.mult, op1=ALU.mult),
                          reads=[b_xb, self.b_const] + b_rsf, writes=b_aTf[2 * c:2 * c + 2])
                self.emit_output_multi(yb, b_aTf, blk, ost)
            else:
                self.store_xblk(xb, b_xb, blk)
        A.release(m)

    def emit_output_multi(self, yb, b_list, blk, ost):
        fw = self.fw
        for tt in range(4):
            for half in range(2):
                ps, b_ps = self.next_ps()
                fns = [lambda e, tt=tt, c=c, ps=ps: e.transpose(ps[:, (c % 4) * 128:(c % 4 + 1) * 128], yb[:, c, tt * 128:(tt + 1) * 128], self.ident[:])
                       for c in range(half * 4, half * 4 + 4)]
                b_y = b_list[8 * half:8 * half + 8]
                b_o = b_list[16 + 4 * tt + 2 * half:16 + 4 * tt + 2 * half + 2]
                fw.pe_group(fns, reads=b_y + [self.b_const], writes=[b_ps])
                if (tt + half) % 2 == 0:
                    fw.op("dve", lambda e, tt=tt, half=half, ps=ps: e.tensor_copy(ost[:, tt, half * 512:(half + 1) * 512], ps[:]), writes=b_o + [b_ps])
                else:
                    fw.op("act", lambda e, tt=tt, half=half, ps=ps: e.copy(ost[:, tt, half * 512:(half + 1) * 512], ps[:]), writes=b_o + [b_ps])
        fw.op("sp", lambda e: e.dma_start(out=self.out[blk * BT:(blk + 1) * BT, :].rearrange("(t p) d -> p t d", p=128), in_=ost),
              reads=b_list[16:32], writes=[self.b_outd], chan=self.c_outd)

    def stage_c(self, l, oT, b_oT, wo_name):
        fw, A = self.fw, self.arena
        m = A.mark()
        if self.cfg.get("ffn", True):
            off0 = A.off
            wup_reserved = A.alloc(NCH * DFF)
        wo = A.alloc(NCH * D).rearrange("p (c n) -> p c n", c=NCH)
        b_wos = [Buf(f"wo{c}") for c in range(NCH)]
        if not hasattr(self, "c_wos"):
            self.c_wos = [fw.new_chan() for _ in range(NCH)]
        wo_d = self.W[f"l{l}_{wo_name}"].rearrange("(c p) n -> p c n", p=128)
        for c in range(NCH):
            fw.op("pool", lambda e, c=c: e.dma_start(out=wo[:, :, c * 128:(c + 1) * 128], in_=wo_d[:, :, c * 128:(c + 1) * 128]),
                  writes=[b_wos[c]], chan=self.c_wos[c])
        if self.cfg.get("ffn", True):
            self.pre_wup = self.alloc_load_wup(l, at=(off0, wup_reserved))
        xbs = [A.alloc(NCH * BT, F32).rearrange("p (c t) -> p c t", c=NCH) for _ in range(2)]
        b_xbs = [Buf("cxb0"), Buf("cxb1")]
        if not hasattr(self, "c_cxb"):
            self.c_cxb = [fw.new_chan(), fw.new_chan()]
        self.load_xblk(xbs[0], b_xbs[0], self.c_cxb[0], 0)
        for blk in range(NBLK):
            s = blk % 2
            xb, b_xb = xbs[s], b_xbs[s]
            if blk + 1 < NBLK:
                self.load_xblk(xbs[1 - s], b_xbs[1 - s], self.c_cxb[1 - s], blk + 1)
            for c in range(NCH):
                ps, b_ps = self.next_ps()
                self.mm_group(ps[:], [(wo[:, k, c * 128:(c + 1) * 128], oT[:, k, blk * BT:(blk + 1) * BT]) for k in range(NCH)], [b_oT, b_wos[c]], b_ps)
                fw.op("dve", lambda e, c=c, ps=ps, xb=xb: e.tensor_tensor(out=xb[:, c, :], in0=xb[:, c, :], in1=ps[:], op=ALU.add), writes=[b_xb, b_ps])
            self.store_xblk(xb, b_xb, blk)
        A.release(m)

    def mla_phase(self, l):
        fw, A = self.fw, self.arena
        m0 = A.mark()
        gq_row = 13 if l == 0 else 14
        gkv_row = 15 if l == 0 else 16
        b_oT = Buf("oT")
        mAB = A.mark()
        if l == MLA_LAYERS[0] or not hasattr(self, "tab_d"):
            self.build_tables()
            TAB = self.TAB
            self.tab_d = self.nc.dram_tensor("tab_scratch", [64, T], F32).ap()
            self.b_tabd, self.c_tabd = Buf("tabd"), fw.new_chan()
            fw.op("sp", lambda e, TAB=TAB: e.dma_start(out=self.tab_d[:, :], in_=TAB[64:128, :]), reads=[self.b_TAB], writes=[self.b_tabd], chan=self.c_tabd)
        else:
            TAB = self.TAB = A.alloc(T, F32)
            self.c_tabl = fw.new_chan()
            fw.op("sp", lambda e, TAB=TAB: e.dma_start(out=TAB[64:128, :], in_=self.tab_d[:, :]), reads=[self.b_tabd], writes=[self.b_TAB], chan=self.c_tabl)
        cqT = A.alloc(3 * T).rearrange("p (c t) -> p c t", c=3)
        ckvT = A.alloc(2 * T).rearrange("p (c t) -> p c t", c=2)
        b_cq, b_ckv = Buf("cqT"), Buf("ckvT")
        kh = [A.alloc(T) for _ in range(2)]
        b_khr = [Buf("khr0"), Buf("khr1")]
        b_khn = [Buf("khn0"), Buf("khn1")]
        wq = A.alloc(3 * 16 * 128).rearrange("p (c h e) -> p c h e", c=3, h=16)
        wuk = A.alloc(2 * 1024).rearrange("p (c n) -> p c n", c=2)
        wuv = A.alloc(2 * 1024).rearrange("p (c n) -> p c n", c=2)
        b_wq, b_wuk, b_wuv = Buf("wq"), Buf("wuk"), Buf("wuv")
        mA = A.mark()
        wdq = A.alloc(NCH * 384).rearrange("p (c n) -> p c n", c=NCH)
        wdkv = A.alloc(NCH * 320).rearrange("p (c n) -> p c n", c=NCH)
        b_wdq, b_wdkv = Buf("wdq"), Buf("wdkv")
        if not hasattr(self, "c_mla_w"):
            self.c_mla_w = [fw.new_chan() for _ in range(6)]
        cw = self.c_mla_w
        self.load_weight(wdq, self.W[f"l{l}_w_dq"], b_wdq, cw[0])
        wdkv_d = self.W[f"l{l}_w_dkv"].rearrange("(c p) n -> p c n", p=128)
        fw.op("pool", lambda e: e.dma_start(out=wdkv[:, :, 0:288], in_=wdkv_d), writes=[b_wdkv], chan=cw[1])
        fw.op("pool", lambda e: e.dma_start(out=wdkv[:, :, 288:304], in_=wdkv_d[:, :, 272:288]), writes=[b_wdkv], chan=cw[1])
        fw.op("pool", lambda e: e.dma_start(out=wdkv[:, :, 304:320], in_=wdkv_d[:, :, 256:272]), writes=[b_wdkv], chan=cw[1])
        wuq_d = self.W[f"l{l}_w_uq"].rearrange("(c p) (h e) -> p c h e", p=128, e=96)
        for c3 in range(3):
            fw.op("pool", lambda e, c3=c3: e.dma_start(out=wq[:, c3, :, 0:96], in_=wuq_d[:, c3, :, :]), writes=[b_wq], chan=cw[2])
            fw.op("pool", lambda e, c3=c3: e.dma_start(out=wq[:, c3, :, 96:112], in_=wuq_d[:, c3, :, 80:96]), writes=[b_wq], chan=cw[2])
            fw.op("pool", lambda e, c3=c3: e.dma_start(out=wq[:, c3, :, 112:128], in_=wuq_d[:, c3, :, 64:80]), writes=[b_wq], chan=cw[2])
        self.load_weight(wuk, self.W[f"l{l}_w_uk"], b_wuk, cw[3])
        self.load_weight(wuv, self.W[f"l{l}_w_uv"], b_wuv, cw[4])
        xbs = [A.alloc(NCH * BT, F32).rearrange("p (c t) -> p c t", c=NCH) for _ in range(2)]
        b_xbs = [Buf("axb0"), Buf("axb1")]
        if not hasattr(self, "c_axb"):
            self.c_axb = [fw.new_chan(), fw.new_chan()]
        hTs = [A.alloc(NCH * BT).rearrange("p (c t) -> p c t", c=NCH) for _ in range(2)]
        raws = [A.alloc(3 * BT, F32).rearrange("p (c t) -> p c t", c=3) for _ in range(2)]
        raw2s = [A.alloc(2 * BT, F32).rearrange("p (c t) -> p c t", c=2) for _ in range(2)]
        rstd = A.alloc(BT, F32)
        rstd2 = A.alloc(BT, F32)
        rstd3 = A.alloc(BT, F32)
        t1 = A.alloc(BT, F32)
        t2 = A.alloc(BT, F32)
        b_hTs, b_raws, b_raw2s = [Buf("hT0"), Buf("hT1")], [Buf("raw0"), Buf("raw1")], [Buf("raw20"), Buf("raw21")]
        b_rstd, b_rstd2, b_rstd3, b_t1, b_t2 = Buf("rstd"), Buf("rstd2"), Buf("rstd3"), Buf("t1"), Buf("t2")
        sq_b, sq_c = Buf("sqb"), Buf("sqc")
        sqq = A.alloc(3 * BT).rearrange("p (c t) -> p c t", c=3)
        sqk = A.alloc(2 * BT).rearrange("p (c t) -> p c t", c=2)
        raw, raw2 = raws[0], raw2s[0]
        b_raw, b_raw2 = b_raws[0], b_raw2s[0]
        print(f"[mla {l} A] arena peak {A.peak * 2 / 1024:.1f} KiB")

        def A_L(blk):
            s = blk % 2
            self.load_xblk(xbs[s], b_xbs[s], self.c_axb[s], blk)

        def A_N1(blk):
            s = blk % 2
            hT = hTs[s]
            self.rms_feature_major(xbs[s], NCH, BT, self.gcol(l), lambda c: hT[:, c, :], hT, b_hTs[s], rstd, b_rstd, b_xbs[s], b_hTs[s], D)

        def A_M(blk):
            s = blk % 2
            hT, b_hT = hTs[s], b_hTs[s]
            rw, b_rw, rw2, b_rw2 = raws[s], b_raws[s], raw2s[s], b_raw2s[s]
            cols = slice(blk * BT, (blk + 1) * BT)
            for mch in range(3):
                ps, b_ps = self.next_ps()
                self.mm_group(ps[:], [(wdq[:, c, mch * 128:(mch + 1) * 128], hT[:, c, :]) for c in range(NCH)], [b_hT, b_wdq], b_ps)
                fw.op("act", lambda e, mch=mch, ps=ps, rw=rw: e.copy(rw[:, mch, :], ps[:]), writes=[b_rw, b_ps])
            for mch in range(2):
                ps, b_ps = self.next_ps()
                self.mm_group(ps[:], [(wdkv[:, c, mch * 128:(mch + 1) * 128], hT[:, c, :]) for c in range(NCH)], [b_hT, b_wdkv], b_ps)
                fw.op("act", lambda e, mch=mch, ps=ps, rw2=rw2: e.copy(rw2[:, mch, :], ps[:]), writes=[b_rw2, b_ps])
            ps, b_ps = self.next_ps()
            self.mm_group(ps[:], [(wdkv[:, c, 192:320], hT[:, c, :]) for c in range(NCH)], [b_hT, b_wdkv], b_ps)
            fw.op("dve", lambda e, ps=ps, cols=cols: e.tensor_tensor(out=t1[64:96, :], in0=ps[64:96, :], in1=TAB[64:96, cols], op=ALU.mult),
                  reads=[self.b_TAB], writes=[b_t1, b_ps])
            fw.op("dve", lambda e, ps=ps, cols=cols: e.tensor_tensor(out=t2[64:96, :], in0=ps[96:128, :], in1=TAB[96:128, cols], op=ALU.mult),
                  reads=[self.b_TAB], writes=[b_t2, b_ps])
            fw.op("pool", lambda e, cols=cols: e.tensor_tensor(out=kh[0][64:96, cols], in0=t1[64:96, :], in1=t2[64:96, :], op=ALU.add),
                  reads=[b_t1, b_t2], writes=[b_khr[0]])
            fw.op("pool", lambda e, cols=cols: e.tensor_copy(kh[1][64:96, cols], kh[0][64:96, cols]), reads=[b_khr[0]], writes=[b_khr[1]])

        def A_N2(blk):
            s = blk % 2
            cols = slice(blk * BT, (blk + 1) * BT)
            self.rms_feature_major(raws[s], 3, BT, self.gcol(gq_row), lambda c: cqT[:, c, cols], sqq, sq_b, rstd2, b_rstd2, b_raws[s], b_cq, 384)
            self.rms_feature_major(raw2s[s], 2, BT, self.gcol(gkv_row), lambda c: ckvT[:, c, cols], sqk, sq_c, rstd3, b_rstd3, b_raw2s[s], b_ckv, 256)

        self.run_pipeline(NBLK, [A_L, A_N1, A_M, A_N2], [0, 1, 2, 3], [])
        self.sdump("TAB", TAB[64:128, :], self.b_TAB)
        self.sdump("cqT", cqT.rearrange("p c t -> p (c t)"), b_cq)
        self.sdump("ckvT", ckvT.rearrange("p c t -> p (c t)"), b_ckv)
        self.sdump("krope", kh[0][64:96, :], b_khr[0])
        self.sdump("t1", t1[64:96, :], b_t1)
        self.sdump("t2", t2[64:96, :], b_t2)
        self.sdump("raw", raw.rearrange("p c t -> p (c t)"), b_raw)
        self.sdump("rstd2", rstd2, b_rstd2)
        self.sdump("wdkv", wdkv.rearrange("p c n -> p (c n)"), b_wdkv)
        if self.cfg.get("stopA"):
            A.release_top()
            A.release(m0)
            return
        fw.barrier()
        A.release(mA)
        oT = A.alloc_top(NCH * T).rearrange("p (c t) -> p c t", c=NCH)
        qh = [A.alloc(T) for _ in range(2)]
        vh = [A.alloc(32 * 128).rearrange("p (j e) -> p j e", j=32) for _ in range(2)]
        b_qh = [Buf("qh0"), Buf("qh1")]
        b_vh = [Buf("vh0"), Buf("vh1")]
        NPT = 4
        pts = [A.alloc(BT) for _ in range(NPT)]
        b_pts = [Buf(f"pt{i}") for i in range(NPT)]
        rec = A.alloc(BT, F32)
        b_rec = Buf("rec")
        qt1 = A.alloc(BT, F32)
        qt2 = A.alloc(BT, F32)
        b_qt1, b_qt2 = Buf("qt1"), Buf("qt2")
        print(f"[mla {l} B] arena peak {A.peak * 2 / 1024:.1f} KiB")
        for i in range(2):
            fw.op("pool", lambda e, i=i: e.memset(vh[i][:, :, 64:128], 1.0), writes=[b_vh[i]])
        scale = 1.0 / math.sqrt(96.0)
        heads = self.cfg.get("heads", list(range(16)))
        def proj_tasks(hi, h):
            s = hi % 2
            tasks = []
            for tb in range(NBLK):
                def tq(tb=tb, s=s, h=h):
                    cols = slice(tb * BT, (tb + 1) * BT)
                    ps, b_ps = self.next_ps_proj()
                    self.mm_group(ps[:], [(wq[:, k, h, :], cqT[:, k, cols]) for k in range(3)], [b_cq, b_wq], b_ps)
                    fw.op("dve", lambda e, ps=ps, s=s, cols=cols: e.tensor_copy(qh[s][0:64, cols], ps[0:64, :]), writes=[b_qh[s], b_ps])
                    fw.op("dve", lambda e, ps=ps, cols=cols: e.tensor_tensor(out=qt1[64:96, :], in0=ps[64:96, :], in1=TAB[64:96, cols], op=ALU.mult),
                          reads=[self.b_TAB], writes=[b_qt1, b_ps])
                    fw.op("dve", lambda e, ps=ps, cols=cols: e.tensor_tensor(out=qt2[64:96, :], in0=ps[96:128, :], in1=TAB[96:128, cols], op=ALU.mult),
                          reads=[self.b_TAB], writes=[b_qt2, b_ps])
                    fw.op("pool", lambda e, s=s, cols=cols: e.tensor_tensor(out=qh[s][64:96, cols], in0=qt1[64:96, :], in1=qt2[64:96, :], op=ALU.add),
                          reads=[b_qt1, b_qt2], writes=[b_qh[s]])
                tasks.append(tq)

                def tk(tb=tb, s=s, h=h):
                    cols = slice(tb * BT, (tb + 1) * BT)
                    ps, b_ps = self.next_ps_proj()
                    self.mm_group(ps[0:64, :], [(wuk[:, k, h * 64:(h + 1) * 64], ckvT[:, k, cols]) for k in range(2)], [b_ckv, b_wuk], b_ps)
                    fw.op("dve", lambda e, ps=ps, s=s, cols=cols: e.tensor_copy(kh[s][0:64, cols], ps[0:64, :]), writes=[b_khn[s], b_ps])
                tasks.append(tk)
            for j0 in range(0, 32, 8):
                def tv(j0=j0, s=s, h=h):
                    ps, b_ps = self.next_ps_proj()
                    fns = []
                    for jj in range(8):
                        j = j0 + jj
                        for k in range(2):
                            fns.append(lambda e, ps=ps, jj=jj, j=j, k=k, h=h: e.matmul(ps[:, jj * 64:(jj + 1) * 64], lhsT=ckvT[:, k, j * 128:(j + 1) * 128],
                                                                                 rhs=wuv[:, k, h * 64:(h + 1) * 64], start=(k == 0), stop=(k == 1)))
                    fw.pe_group(fns, reads=[b_ckv, b_wuv], writes=[b_ps])
                    fw.op("dve", lambda e, ps=ps, s=s, j0=j0: e.tensor_copy(vh[s][:, j0:j0 + 8, 0:64], ps[:].rearrange("p (j e) -> p j e", j=8)), writes=[b_vh[s], b_ps])
                tasks.append(tv)
            return tasks

        tiles = []
        for qb in range(NBLK):
            for kc in range(4 * qb + 4):
                tiles.append((qb, kc))
        NT = len(tiles)
        for t in proj_tasks(0, heads[0]):
            t()
        for hi, h in enumerate(heads):
            s = hi % 2
            extra = proj_tasks(hi + 1, heads[hi + 1]) if hi + 1 < len(heads) else []
            st = {}

            def S_(i, s=s):
                qb, kc = tiles[i]
                nq0 = max(0, kc - 4 * qb) * 128
                ps, b_ps = self.next_ps_tile()
                st[i] = [ps, b_ps, None, None]
                diag = kc >= 4 * qb
                fns = [lambda e, ps=ps, s=s, kc=kc, qb=qb, nq0=nq0, diag=diag: e.matmul(ps[:, nq0:BT], lhsT=kh[s][0:96, kc * 128:(kc + 1) * 128],
                                                                                         rhs=qh[s][0:96, qb * BT + nq0:(qb + 1) * BT], start=True, stop=not diag)]
                if diag:
                    fns.append(lambda e, ps=ps, nq0=nq0: e.matmul(ps[:, nq0:nq0 + 128], lhsT=self.identb[:], rhs=self.mnegI[:], start=False, stop=True))
                fw.pe_group(fns, reads=[b_khr[s], b_khn[s], b_qh[s], self.b_const], writes=[b_ps])

            def E_(i, s=s):
                qb, kc = tiles[i]
                nq0 = max(0, kc - 4 * qb) * 128
                ps, b_ps = st[i][0], st[i][1]
                pt, b_pt = pts[i % NPT], b_pts[i % NPT]
                st[i][2], st[i][3] = pt, b_pt
                fw.op("act", lambda e, ps=ps, pt=pt, nq0=nq0: e.activation(pt[:, nq0:BT], ps[:, nq0:BT], AF.Exp, scale=scale), writes=[b_pt, b_ps])

            cur = {}

            def P_(i, s=s, h=h):
                qb, kc = tiles[i]
                nq0 = max(0, kc - 4 * qb) * 128
                last = 4 * qb + 3
                pt, b_pt = st[i][2], st[i][3]
                del st[i]
                if kc == 0:
                    cur[qb] = self.next_acc()
                po, b_po = cur[qb]
                fw.pe_group([lambda e, po=po, pt=pt, nq0=nq0, kc=kc, last=last, s=s: e.matmul(po[:, nq0:BT], lhsT=vh[s][:, kc, :], rhs=pt[:, nq0:BT],
                                                                                              start=(kc == 0), stop=(kc == last))],
                            reads=[b_vh[s], b_pt], writes=[b_po])
                if kc == last:
                    qc = slice(qb * BT, (qb + 1) * BT)
                    fw.op("dve", lambda e, po=po: e.reciprocal(rec[0:64, :], po[64:128, :]), writes=[b_rec, b_po])
                    fw.op("dve", lambda e, po=po, h=h, qc=qc: e.tensor_tensor(out=oT[(h % 2) * 64:(h % 2) * 64 + 64, h // 2, qc], in0=po[0:64, :],
                                                                              in1=rec[0:64, :], op=ALU.mult),
                          reads=[b_rec], writes=[b_oT, b_po])

            self.run_pipeline(NT, [S_, E_, P_], [0, 3, 5], extra)
        self.sdump("qh", qh[(len(heads) - 1) % 2][0:96, :], b_qh[(len(heads) - 1) % 2])
        self.sdump("khn", kh[(len(heads) - 1) % 2][0:64, :], b_khn[(len(heads) - 1) % 2])
        self.sdump("vh", vh[(len(heads) - 1) % 2].rearrange("p j e -> p (j e)"), b_vh[(len(heads) - 1) % 2])
        self.sdump("oT", oT.rearrange("p c t -> p (c t)"), b_oT)
        A.release(mAB)
        fw.barrier()
        self.stage_c(l, oT, b_oT, "w_o")
        A.release_top()
        A.release(m0)

    def conv_phase(self, l):
        fw, A = self.fw, self.arena
        m0 = A.mark()
        win = A.alloc(NCH * 3072).rearrange("p (c n) -> p c n", c=NCH)
        wout = A.alloc(NCH * D).rearrange("p (c n) -> p c n", c=NCH)
        b_win = [Buf(f"win{i}") for i in range(3)]
        b_wout = Buf("wout")
        if not hasattr(self, "c_conv_w"):
            self.c_conv_w = [fw.new_chan() for _ in range(4)]
        cw = self.c_conv_w
        win_d = self.W[f"l{l}_w_in"].rearrange("(c p) n -> p c n", p=128)
        for i in range(3):
            fw.op("pool", lambda e, i=i: e.dma_start(out=win[:, :, i * 1024:(i + 1) * 1024], in_=win_d[:, :, i * 1024:(i + 1) * 1024]),
                  writes=[b_win[i]], chan=cw[i])
        self.load_weight(wout, self.W[f"l{l}_w_out"], b_wout, cw[3])
        xbs = [A.alloc(NCH * BT, F32).rearrange("p (c t) -> p c t", c=NCH) for _ in range(2)]
        b_xbs = [Buf("vxb0"), Buf("vxb1")]
        if not hasattr(self, "c_vxb"):
            self.c_vxb = [fw.new_chan(), fw.new_chan()]
        hTs = [A.alloc(NCH * BT).rearrange("p (c t) -> p c t", c=NCH) for _ in range(2)]
        b_hTs = [Buf("chT0"), Buf("chT1")]
        zT = A.alloc(NCH * BT).rearrange("p (c t) -> p c t", c=NCH)
        ucur = A.alloc(NCH * (BT + 2), F32).rearrange("p (c t) -> p c t", c=NCH)
        rstd = A.alloc(BT, F32)
        NTM = 2
        tmpc = [A.alloc(BT, F32) for _ in range(NTM)]
        acc = [A.alloc(BT, F32) for _ in range(NTM)]
        b_zT, b_rstd = Buf("zT"), Buf("rstd")
        b_u = [Buf(f"u{c}") for c in range(NCH)]
        b_tmpc = [Buf(f"tc{i}") for i in range(NTM)]
        b_acc = [Buf(f"ac{i}") for i in range(NTM)]
        print(f"[conv {l}] arena peak {A.peak * 2 / 1024:.1f} KiB")
        fw.op("pool", lambda e: e.memset(ucur.rearrange("p c t -> p (c t)"), 0.0), writes=b_u)
        ti = 0

        def cnorm(blk):
            s = blk % 2
            hTn = hTs[s]
            self.rms_feature_major(xbs[s], NCH, BT, self.gcol(l), lambda c: hTn[:, c, :], hTn, b_hTs[s], rstd, b_rstd, b_xbs[s], b_hTs[s], D)

        self.load_xblk(xbs[0], b_xbs[0], self.c_vxb[0], 0)
        cnorm(0)
        for blk in range(NBLK):
            s = blk % 2
            xb, b_xb = xbs[s], b_xbs[s]
            hT, b_hT = hTs[s], b_hTs[s]
            if blk + 1 < NBLK:
                self.load_xblk(xbs[1 - s], b_xbs[1 - s], self.c_vxb[1 - s], blk + 1)
            for c in range(NCH):
                if c == 2 and blk + 1 < NBLK:
                    cnorm(blk + 1)
                psB, b_psB = self.next_ps()
                self.mm_group(psB[:], [(win[:, k, c * 128:(c + 1) * 128], hT[:, k, :]) for k in range(NCH)], [b_hT, b_win[0]], b_psB)
                psC, b_psC = self.next_ps()
                self.mm_group(psC[:], [(win[:, k, 1024 + c * 128:1024 + (c + 1) * 128], hT[:, k, :]) for k in range(NCH)], [b_hT, b_win[1]], b_psC)
                psU, b_psU = self.next_ps()
                self.mm_group(psU[:], [(win[:, k, 2048 + c * 128:2048 + (c + 1) * 128], hT[:, k, :]) for k in range(NCH)], [b_hT, b_win[2]], b_psU)
                tc_, b_tc = tmpc[ti % NTM], b_tmpc[ti % NTM]
                ac_, b_ac = acc[ti % NTM], b_acc[ti % NTM]
                ti += 1
                fw.op("act", lambda e, psC=psC, tc_=tc_: e.copy(tc_, psC[:]), writes=[b_tc, b_psC])
                if blk > 0:
                    fw.op("pool", lambda e, c=c: e.tensor_copy(ucur[:, c, 0:2], ucur[:, c, BT:BT + 2]), writes=[b_u[c]])
                fw.op("dve", lambda e, c=c, psU=psU, tc_=tc_: e.tensor_tensor(out=ucur[:, c, 2:BT + 2], in0=psU[:], in1=tc_, op=ALU.mult),
                      reads=[b_tc], writes=[b_u[c], b_psU])
                fw.op("dve", lambda e, c=c, ac_=ac_: e.tensor_scalar(ac_, ucur[:, c, 2:BT + 2], self.gains[:, c, 11:12], self.gains[:, c, 12:13],
                                                                     op0=ALU.mult, op1=ALU.add),
                      reads=[b_u[c], self.b_const], writes=[b_ac])
                fw.op("dve", lambda e, c=c, ac_=ac_: e.scalar_tensor_tensor(out=ac_, in0=ucur[:, c, 1:BT + 1], scalar=self.gains[:, c, 10:11], in1=ac_,
                                                                             op0=ALU.mult, op1=ALU.add),
                      reads=[b_u[c], self.b_const], writes=[b_ac])
                fw.op("dve", lambda e, c=c, ac_=ac_: e.scalar_tensor_tensor(out=ac_, in0=ucur[:, c, 0:BT], scalar=self.gains[:, c, 9:10], in1=ac_,
                                                                             op0=ALU.mult, op1=ALU.add),
                      reads=[b_u[c], self.b_const], writes=[b_ac])
                fw.op("dve", lambda e, c=c, ac_=ac_, psB=psB: e.tensor_tensor(out=zT[:, c, :], in0=psB[:], in1=ac_, op=ALU.mult),
                      reads=[b_ac], writes=[b_zT, b_psB])
            for c in range(NCH):
                ps, b_ps = self.next_ps()
                self.mm_group(ps[:], [(wout[:, k, c * 128:(c + 1) * 128], zT[:, k, :]) for k in range(NCH)], [b_zT, b_wout], b_ps)
                fw.op("dve", lambda e, c=c, ps=ps, xb=xb: e.tensor_tensor(out=xb[:, c, :], in0=xb[:, c, :], in1=ps[:], op=ALU.add), writes=[b_xb, b_ps])
            self.store_xblk(xb, b_xb, blk)
        A.release(m0)

    def sb_phase(self, l):
        fw, A = self.fw, self.arena
        m0 = A.mark()
        b_oT = Buf("oT")
        mAB = A.mark()
        hTf = A.alloc(NCH * T).rearrange("p (c t) -> p c t", c=NCH)
        b_hTf = Buf("hTf")
        mA = A.mark()
        xbs = [A.alloc(NCH * BT, F32).rearrange("p (c t) -> p c t", c=NCH) for _ in range(2)]
        b_xbs = [Buf("sxb0"), Buf("sxb1")]
        if not hasattr(self, "c_sxb"):
            self.c_sxb = [fw.new_chan(), fw.new_chan()]
        sqs = [A.alloc(NCH * BT).rearrange("p (c t) -> p c t", c=NCH) for _ in range(2)]
        b_sqs = [Buf("sq0"), Buf("sq1")]
        rstds = [A.alloc(BT, F32) for _ in range(2)]
        b_rstds = [Buf("rstd0"), Buf("rstd1")]
        psd = {}

        def SA_L(blk):
            s = blk % 2
            self.load_xblk(xbs[s], b_xbs[s], self.c_sxb[s], blk)

        def SA_1(blk):
            s = blk % 2
            psd[blk] = self.rms_part1(xbs[s], NCH, BT, sqs[s], b_sqs[s], b_xbs[s])

        def SA_2(blk):
            s = blk % 2
            cols = slice(blk * BT, (blk + 1) * BT)
            self.rms_part2(psd.pop(blk), xbs[s], NCH, BT, self.gcol(l), lambda c: hTf[:, c, cols], rstds[s], b_rstds[s], b_xbs[s], b_hTf, D)

        self.run_pipeline(NBLK, [SA_2, SA_1, SA_L], [2, 1, 0], [])
        fw.barrier()
        A.release(mA)
        oT = A.alloc_top(NCH * T).rearrange("p (c t) -> p c t", c=NCH)
        wp = [A.alloc(NCH * 3 * 128).rearrange("p (c g e) -> p c g e", c=NCH, g=3) for _ in range(2)]
        b_wp = [Buf("wp0"), Buf("wp1")]
        if not hasattr(self, "c_wp"):
            self.c_wp = [fw.new_chan(), fw.new_chan()]
        qp = [A.alloc(T) for _ in range(2)]
        kp = [A.alloc(T) for _ in range(2)]
        vp = [A.alloc(32 * 128).rearrange("p (j e) -> p j e", j=32) for _ in range(2)]
        b_qp, b_kp, b_vp = [Buf("qp0"), Buf("qp1")], [Buf("kp0"), Buf("kp1")], [Buf("vp0"), Buf("vp1")]
        NE = 4
        Es = [A.alloc(BT) for _ in range(NE)]
        b_Es = [Buf(f"E{i}") for i in range(NE)]
        NS = 3
        ats = [A.alloc(BT) for _ in range(NS)]
        b_ats = [Buf(f"at{i}") for i in range(NS)]
        print(f"[sb {l} B] arena peak {A.peak * 2 / 1024:.1f} KiB")
        wqkv_d = self.W[f"l{l}_w_qkv"].rearrange("(c p) (g n) -> p c g n", p=128, g=3)
        pairs = self.cfg.get("pairs", list(range(8)))
        NRB = 4
        Rs = [A.alloc(BT) for _ in range(NRB)]
        b_Rs = [Buf(f"R{i}") for i in range(NRB)]
        NSP = 4
        sps = [A.alloc(BT) for _ in range(NSP)]
        b_sps = [Buf(f"sp{i}") for i in range(NSP)]

        def load_wp(pi):
            p = pairs[pi]
            s = pi % 2
            for g3 in range(3):
                fw.op("pool", lambda e, s=s, p=p, g3=g3: e.dma_start(out=wp[s][:, :, g3, :], in_=wqkv_d[:, :, g3, p * 128:(p + 1) * 128]),
                      writes=[b_wp[s]], chan=self.c_wp[s])

        def proj_tasks(pi):
            s = pi % 2
            tasks = []
            for tb in range(NBLK):
                for g3, dst, b_dst in ((0, qp, b_qp), (1, kp, b_kp)):
                    def tqk(tb=tb, s=s, g3=g3, dst=dst, b_dst=b_dst):
                        cols = slice(tb * BT, (tb + 1) * BT)
                        ps, b_ps = self.next_ps_proj()
                        self.mm_group(ps[:], [(wp[s][:, c, g3, :], hTf[:, c, cols]) for c in range(NCH)], [b_hTf, b_wp[s]], b_ps)
                        fw.op("dve", lambda e, ps=ps, s=s, cols=cols, dst=dst: e.tensor_copy(dst[s][:, cols], ps[:]), writes=[b_dst[s], b_ps])
                    tasks.append(tqk)
            for j0 in range(0, 32, 4):
                def tv(j0=j0, s=s):
                    ps, b_ps = self.next_ps_proj()
                    fns = []
                    for jj in range(4):
                        j = j0 + jj
                        for c in range(NCH):
                            fns.append(lambda e, ps=ps, jj=jj, j=j, c=c, s=s: e.matmul(ps[:, jj * 128:(jj + 1) * 128], lhsT=hTf[:, c, j * 128:(j + 1) * 128],
                                                                                      rhs=wp[s][:, c, 2, :], start=(c == 0), stop=(c == NCH - 1)))
                    fw.pe_group(fns, reads=[b_hTf, b_wp[s]], writes=[b_ps])
                    fw.op("dve", lambda e, ps=ps, s=s, j0=j0: e.tensor_copy(vp[s][:, j0:j0 + 4, :], ps[:].rearrange("p (j e) -> p j e", j=4)), writes=[b_vp[s], b_ps])
                tasks.append(tv)
            return tasks

        tiles = []
        for hh in range(2):
            for qb in range(NBLK):
                for kc in range(4 * qb + 3, -1, -1):
                    tiles.append((hh, qb, kc))
        NT = len(tiles)
        load_wp(0)
        for t in proj_tasks(0):
            t()
        gi = [0]
        for pi, p in enumerate(pairs):
            s = pi % 2
            extra = []
            if pi + 1 < len(pairs):
                load_wp(pi + 1)
                extra = proj_tasks(pi + 1)
            st = {}
            base = gi[0]

            def info(i):
                hh, qb, kc = tiles[i]
                nq0 = max(0, kc - 4 * qb) * 128
                return hh, qb, kc, nq0, slice(hh * 64, (hh + 1) * 64), slice(kc * 128, (kc + 1) * 128), slice(qb * BT + nq0, (qb + 1) * BT)

            def S1(i, s=s):
                hh, qb, kc, nq0, pr, kcs, qcs = info(i)
                psZ, b_psZ = self.next_ps_tile()
                st[i] = {"psZ": (psZ, b_psZ)}
                diag = kc >= 4 * qb
                fns = [lambda e, psZ=psZ, s=s, pr=pr, kcs=kcs, qcs=qcs, nq0=nq0, diag=diag: e.matmul(psZ[:, nq0:BT], lhsT=kp[s][pr, kcs], rhs=qp[s][pr, qcs],
                                                                                                   start=True, stop=not diag)]
                if diag:
                    fns.append(lambda e, psZ=psZ, nq0=nq0: e.matmul(psZ[:, nq0:nq0 + 128], lhsT=self.identb[:], rhs=self.mnegS[:], start=False, stop=True))
                fw.pe_group(fns, reads=[b_kp[s], b_qp[s], self.b_const], writes=[b_psZ])

            def S2(i, s=s, base=base):
                hh, qb, kc, nq0, pr, kcs, qcs = info(i)
                g = base + i
                psZ, b_psZ = st[i]["psZ"]
                E, b_E = Es[g % NE], b_Es[g % NE]
                sp, b_sp = sps[g % NSP], b_sps[g % NSP]
                st[i]["sp"] = (sp, b_sp)
                last = 4 * qb + 3
                diag = kc >= 4 * qb
                fw.op("act", lambda e, psZ=psZ, E=E, nq0=nq0: e.activation(E[:, nq0:BT], psZ[:, nq0:BT], AF.Exp, scale=0.125), writes=[b_E, b_psZ])
                fw.op("act", lambda e, E=E, sp=sp, nq0=nq0: e.activation(sp[:, nq0:BT], E[:, nq0:BT], AF.Ln, bias=1.0), reads=[b_E], writes=[b_sp])
                st[i]["E"] = (E, b_E)
                Rin, b_Rin = Rs[g % NRB], b_Rs[g % NRB]
                Rout, b_Rout = Rs[(g + 1) % NRB], b_Rs[(g + 1) % NRB]
                st[i]["Rin"] = (Rin, b_Rin)
                if kc != 0:
                    if kc == last:
                        fw.op("dve", lambda e, Rout=Rout, sp=sp, nq0=nq0: e.tensor_copy(Rout[:, nq0:BT], sp[:, nq0:BT]), reads=[b_sp], writes=[b_Rout])
                    else:
                        fw.op("dve", lambda e, Rout=Rout, Rin=Rin, sp=sp, nq0=nq0: e.tensor_tensor(out=Rout[:, nq0:BT], in0=Rin[:, nq0:BT], in1=sp[:, nq0:BT], op=ALU.add),
                              reads=[b_sp, b_Rin], writes=[b_Rout])
                    if kc > 4 * qb:
                        fw.op("pool", lambda e, Rout=Rout, nq0=nq0: e.memset(Rout[:, nq0 - 128:nq0], 0.0), writes=[b_Rout])

            def S3(i, s=s):
                hh, qb, kc, nq0, pr, kcs, qcs = info(i)
                sp, b_sp = st[i]["sp"]
                Rin, b_Rin = st[i]["Rin"]
                psL, b_psL = self.next_ps_tile()
                st[i]["psL"] = (psL, b_psL)
                prs = [(self.Uinc[:], sp[:, nq0:BT])]
                rd = [b_sp, self.b_const]
                if kc != 4 * qb + 3:
                    prs.append((self.neg8[:], Rin[:, nq0:BT]))
                    rd.append(b_Rin)
                self.mm_group(psL[:, nq0:BT], prs, rd, b_psL)

            def S4(i, s=s, base=base):
                hh, qb, kc, nq0, pr, kcs, qcs = info(i)
                g = base + i
                psL, b_psL = st[i]["psL"]
                at, b_at = ats[g % NS], b_ats[g % NS]
                st[i]["at"] = (at, b_at)
                E, b_E = st[i]["E"]
                fw.op("act", lambda e, psL=psL, at=at, nq0=nq0: e.activation(at[:, nq0:BT], psL[:, nq0:BT], AF.Exp, scale=0.125), writes=[b_at, b_psL])
                fw.op("dve", lambda e, at=at, E=E, nq0=nq0: e.tensor_tensor(out=at[:, nq0:BT], in0=at[:, nq0:BT], in1=E[:, nq0:BT], op=ALU.mult),
                      reads=[b_E], writes=[b_at])

            cur = {}

            def S5(i, s=s, p=p):
                hh, qb, kc, nq0, pr, kcs, qcs = info(i)
                last = 4 * qb + 3
                at, b_at = st[i]["at"]
                del st[i]
                if kc == last:
                    cur[(hh, qb)] = self.next_acc()
                po, b_po = cur[(hh, qb)]
                fw.pe_group([lambda e, po=po, at=at, nq0=nq0, kc=kc, last=last, s=s: e.matmul(po[:, nq0:BT], lhsT=vp[s][:, kc, :], rhs=at[:, nq0:BT],
                                                                                             start=(kc == last), stop=(kc == 0))],
                            reads=[b_vp[s], b_at], writes=[b_po])
                if kc == 0:
                    qc = slice(qb * BT, (qb + 1) * BT)
                    fw.op("dve", lambda e, po=po, p=p, pr=pr, qc=qc: e.tensor_copy(oT[pr, p, qc], po[pr, :]), writes=[b_oT, b_po])

            self.run_pipeline(NT, [S1, S2, S3, S4, S5], [0, 1, 2, 3, 4], extra)
            gi[0] += NT
        A.release(mAB)
        fw.barrier()
        self.stage_c(l, oT, b_oT, "w_o")
        A.release_top()
        A.release(m0)


_NC_CACHE = {}


def build_nc(cfg=None):
    key = repr(sorted((cfg or {}).items()))
    if key not in _NC_CACHE:
        _NC_CACHE[key] = Prog(cfg).build()
    return _NC_CACHE[key]


def kernel(**inputs):
    nc = build_nc(None)
    x = np.ascontiguousarray(inputs["x"], dtype=np.float32)
    pos = np.ascontiguousarray(inputs["positions"], dtype=np.int32)
    shared = {k: np.ascontiguousarray(v, dtype=np.float32) for k, v in inputs.items() if k not in ("x", "positions")}
    in_maps = []
    for b in range(N_CORES):
        d = dict(shared)
        d["x"] = x[b]
        d["positions"] = pos[b:b + 1]
        in_maps.append(d)
    res = run_bass_kernel_spmd(nc, in_maps, core_ids=list(range(N_CORES)))
    return np.stack([np.asarray(r["out"], dtype=np.float32) for r in res.results], axis=0)
```
